# Optimizing a Trainium2 kernel written in Bass

```python
import math
import jax
import jax.numpy as jnp
from jax import lax
import numpy as np

D_MODEL = 2048
BATCH = 4
SEQ = 4096
DEPTH = 2

GRID_W = 64
CTX_LEN = 256

DN_ALPHA = (2 * DEPTH) ** 0.25
DN_BETA = (8 * DEPTH) ** -0.25
N_SUB = 3
N_MOD = 3 * N_SUB
FFN_HALF = 0.5
D_FF = 5632
LN_EPS = 1e-5
RMS_EPS = 1e-6

BRANCH_W = D_MODEL // 2
N_BRANCH = 3

ATT_DH = 64
ATT_DV = 2 * ATT_DH
ATT_HEADS = BRANCH_W // ATT_DV
ATT_QK_W = ATT_HEADS * 2 * ATT_DH
ATT_BLOCK = 128
ROPE_BASE = 10000.0
ROPE_FREQS = ATT_DH // 4
LAMBDA_INIT_BASE = 0.8
LAMBDA_INIT_SPAN = 0.6
LAMBDA_INIT_RATE = 0.3

SSD_P = 64
SSD_HEADS = BRANCH_W // SSD_P
SSD_GROUPS = 4
SSD_HPG = SSD_HEADS // SSD_GROUPS
SSD_N = 128
SSD_CONV = 5
SSD_CHUNK = 128
SSD_INNER = SSD_HEADS * SSD_P
SSD_BC_W = SSD_GROUPS * SSD_N
SSD_XBC_W = SSD_INNER + 2 * SSD_BC_W
SSD_NORM_GROUP = SSD_INNER // SSD_GROUPS

S5_CH = BRANCH_W
S5_GROUP_CH = 16
S5_GROUPS = S5_CH // S5_GROUP_CH
S5_N = 64

IN_SPLITS = (ATT_QK_W, ATT_QK_W, ATT_HEADS * ATT_DV, SSD_INNER, SSD_XBC_W,
             2 * SSD_HEADS, S5_CH, N_BRANCH * D_MODEL)
IN_COLS = sum(IN_SPLITS)

kernel_name = "hybrid_diffattn_ssd_s5_dit_trunk"


def split_cols(t, sizes):
    cuts = [int(v) for v in np.cumsum(sizes)[:-1]]
    return jnp.split(t, cuts, axis=-1)


def flip_time(t, direction):
    return jnp.flip(t, axis=1) if direction else t


def layer_norm(t, g, b):
    tf = t.astype(jnp.float32)
    mu = jnp.mean(tf, axis=-1, keepdims=True)
    var = jnp.mean(jnp.square(tf - mu), axis=-1, keepdims=True)
    return ((tf - mu) * lax.rsqrt(var + LN_EPS) * g + b).astype(t.dtype)


def rms_norm(t, w):
    tf = t.astype(jnp.float32)
    return (tf * lax.rsqrt(jnp.mean(jnp.square(tf), axis=-1, keepdims=True) + RMS_EPS) * w).astype(t.dtype)


def swiglu(h, w1, w3, w2):
    return (jax.nn.silu(h @ w1) * (h @ w3)) @ w2


def adaln(m, j):
    return m[..., 3 * j, :], m[..., 3 * j + 1, :], m[..., 3 * j + 2, :]


def modulate(t, shift, scale):
    return t * (1.0 + scale) + shift


def post_norm(t, update, g, b):
    return layer_norm(DN_ALPHA * t + update, g, b)


def half_ffn(t, m, j, w1, w3, w2, g, b):
    shift, scale, gate = adaln(m, j)
    return post_norm(t, FFN_HALF * gate * swiglu(modulate(t, shift, scale), w1, w3, w2), g, b)


def axial_rope_tables(seq_len):
    rows = seq_len // GRID_W
    row = jnp.repeat(jnp.arange(rows), GRID_W)
    col = jnp.tile(jnp.arange(GRID_W), rows)
    inv = ROPE_BASE ** (-jnp.arange(ROPE_FREQS, dtype=jnp.float32) / ROPE_FREQS)
    ang = jnp.stack([row[:, None] * inv, col[:, None] * inv], axis=1)
    return jnp.cos(ang), jnp.sin(ang)


def apply_axial_rope(t, cos, sin):
    tr = t.astype(jnp.float32).reshape(*t.shape[:-1], 2, 2, ROPE_FREQS)
    t1, t2 = tr[..., 0, :], tr[..., 1, :]
    cs, sn = cos[:, None, None], sin[:, None, None]
    out = jnp.stack([t1 * cs - t2 * sn, t2 * cs + t1 * sn], axis=-2)
    return out.reshape(t.shape).astype(t.dtype)


def diff_attention(q, k, v, qc, kc, vc, lam_vec, subln_w, lam_init):
    bsz, seq = q.shape[:2]
    cos, sin = axial_rope_tables(seq)
    q = apply_axial_rope(q, cos, sin)
    k = apply_axial_rope(k, cos, sin)
    lv = lam_vec.astype(jnp.float32)
    lam = jnp.exp(jnp.sum(lv[0] * lv[1])) - jnp.exp(jnp.sum(lv[2] * lv[3])) + lam_init
    k_all = jnp.concatenate([kc, k], axis=1)
    v_all = jnp.concatenate([vc, v], axis=1)

    def attend(qb, kk, vv):
        s = jnp.einsum('bqhjd,bkhjd->bhjqk', qb, kk, preferred_element_type=jnp.float32) * (ATT_DH ** -0.5)
        p = jax.nn.softmax(s, axis=-1)
        a = p[:, :, 0] - lam * p[:, :, 1]
        return jnp.einsum('bhqk,bkhe->bqhe', a.astype(vv.dtype), vv)

    n_blk = seq // ATT_BLOCK
    q_blocks = q.reshape(bsz, n_blk, ATT_BLOCK, ATT_HEADS, 2, ATT_DH).swapaxes(0, 1)
    o = lax.map(lambda qb: attend(qb, k_all, v_all), q_blocks)
    o = o.swapaxes(0, 1).reshape(bsz, seq, ATT_HEADS, ATT_DV)
    oc = attend(qc, kc, vc)

    def heads_out(t):
        return (rms_norm(t, subln_w) * (1.0 - lam_init)).reshape(*t.shape[:2], ATT_HEADS * ATT_DV)

    return heads_out(o), heads_out(oc)


def dwconv_centred(u, w, b):
    out = lax.conv_general_dilated(
        u, w[:, None, :].astype(u.dtype), window_strides=(1,),
        padding=((SSD_CONV // 2, SSD_CONV // 2),),
        dimension_numbers=('NWC', 'WIO', 'NWC'), feature_group_count=u.shape[-1])
    return out + b.astype(u.dtype)


def ssd_chunked(xs, dt, a, bm, cm, h0):
    bsz, T, G, E, P = xs.shape
    N = bm.shape[-1]
    L = SSD_CHUNK
    nc = T // L
    f32 = jnp.float32
    xdt = (xs.astype(f32) * dt[..., None]).reshape(bsz, nc, L, G, E, P)
    bc = bm.astype(f32).reshape(bsz, nc, L, G, N)
    cc = cm.astype(f32).reshape(bsz, nc, L, G, N)
    da = (dt * a).reshape(bsz, nc, L, G, E).transpose(0, 1, 3, 4, 2)
    da_cs = jnp.cumsum(da, axis=-1)
    seg = da_cs[..., :, None] - da_cs[..., None, :]
    lower = jnp.tril(jnp.ones((L, L), dtype=bool))
    decay = jnp.where(lower, jnp.exp(jnp.where(lower, seg, 0.0)), 0.0)
    cb = jnp.einsum('bclgn,bcsgn->bcgls', cc, bc)
    y_diag = jnp.einsum('bcgels,bcsgep->bclgep', cb[:, :, :, None] * decay, xdt)
    to_end = jnp.exp(da_cs[..., -1:] - da_cs).transpose(0, 1, 4, 2, 3)
    states = jnp.einsum('bclgn,bclgep->bcgepn', bc, xdt * to_end[..., None])
    chunk_decay = jnp.exp(da_cs[..., -1])

    def carry_step(h, inp):
        dec, st = inp
        return h * dec[..., None, None] + st, h

    h_last, h_prev = lax.scan(carry_step, h0.astype(f32),
                              (chunk_decay.swapaxes(0, 1), states.swapaxes(0, 1)))
    h_prev = h_prev.swapaxes(0, 1)
    from_start = jnp.exp(da_cs).transpose(0, 1, 4, 2, 3)[..., None]
    y_off = jnp.einsum('bclgn,bcgepn->bclgep', cc, h_prev) * from_start
    return (y_diag + y_off).reshape(bsz, T, G, E, P), h_last


def mamba2_mixer(z, xbc, dt, zc, xbcc, dtc, conv_w, conv_b, a_log, dt_bias, d_skip, norm_w):
    f32 = jnp.float32
    a = -jnp.exp(a_log.astype(f32)).reshape(2, SSD_GROUPS, SSD_HPG)
    dsk = d_skip.astype(f32).reshape(SSD_GROUPS, SSD_HPG, 1)

    def prep(xbc_t, dt_t):
        bsz, T = xbc_t.shape[:2]
        xbc_t = jax.nn.silu(dwconv_centred(xbc_t, conv_w, conv_b))
        xs, bm, cm = split_cols(xbc_t, (SSD_INNER, SSD_BC_W, SSD_BC_W))
        xs = xs.reshape(bsz, T, SSD_GROUPS, SSD_HPG, SSD_P)
        bm = bm.reshape(bsz, T, SSD_GROUPS, SSD_N)
        cm = cm.reshape(bsz, T, SSD_GROUPS, SSD_N)
        dts = jax.nn.softplus(dt_t.astype(f32).reshape(bsz, T, 2, SSD_GROUPS, SSD_HPG)
                              + dt_bias.astype(f32).reshape(2, SSD_GROUPS, SSD_HPG))
        return xs, bm, cm, dts

    xs, bm, cm, dts = prep(xbc, dt)
    xsc, bmc, cmc, dtsc = prep(xbcc, dtc)
    y = xs.astype(f32) * dsk
    yc = xsc.astype(f32) * dsk
    h0 = jnp.zeros((xs.shape[0], SSD_GROUPS, SSD_HPG, SSD_P, SSD_N), f32)
    for direction in range(2):
        fl = lambda t: flip_time(t, direction)
        y_c, h_c = ssd_chunked(fl(xsc), fl(dtsc[:, :, direction]), a[direction], fl(bmc), fl(cmc), h0)
        y_l, _ = ssd_chunked(fl(xs), fl(dts[:, :, direction]), a[direction], fl(bm), fl(cm), h_c)
        yc = yc + fl(y_c)
        y = y + fl(y_l)

    def gate_norm(y_t, z_t):
        bsz, T = z_t.shape[:2]
        gated = y_t.reshape(bsz, T, SSD_INNER) * jax.nn.silu(z_t.astype(f32))
        normed = rms_norm(gated.reshape(bsz, T, SSD_GROUPS, SSD_NORM_GROUP),
                          norm_w.reshape(SSD_GROUPS, SSD_NORM_GROUP))
        return normed.reshape(bsz, T, SSD_INNER).astype(z_t.dtype)

    return gate_norm(y, z), gate_norm(yc, zc)


def complex_affine_combine(e1, e2):
    a1r, a1i, b1r, b1i = e1
    a2r, a2i, b2r, b2i = e2
    return (a1r * a2r - a1i * a2i, a1r * a2i + a1i * a2r,
            a2r * b1r - a2i * b1i + b2r, a2r * b1i + a2i * b1r + b2i)


def s5_scan(u, lam_re, lam_im, log_step, b_re, b_im, c_re, c_im, h0_re, h0_im):
    T = u.shape[1]
    step = jnp.exp(log_step)[:, None]
    mag = jnp.exp(lam_re * step)
    ang = lam_im * step
    ab_re, ab_im = mag * jnp.cos(ang), mag * jnp.sin(ang)
    den = lam_re * lam_re + lam_im * lam_im
    k_re = ((ab_re - 1.0) * lam_re + ab_im * lam_im) / den
    k_im = (ab_im * lam_re - (ab_re - 1.0) * lam_im) / den
    bb_re = k_re[..., None] * b_re - k_im[..., None] * b_im
    bb_im = k_re[..., None] * b_im + k_im[..., None] * b_re
    bu_re = jnp.einsum('btgh,gnh->btgn', u, bb_re)
    bu_im = jnp.einsum('btgh,gnh->btgn', u, bb_im)
    a_re = jnp.broadcast_to(ab_re, (1, T) + ab_re.shape)
    a_im = jnp.broadcast_to(ab_im, (1, T) + ab_im.shape)
    p_re, p_im, h_re, h_im = lax.associative_scan(
        complex_affine_combine, (a_re, a_im, bu_re, bu_im), axis=1)
    h_re, h_im = (h_re + p_re * h0_re[:, None] - p_im * h0_im[:, None],
                  h_im + p_re * h0_im[:, None] + p_im * h0_re[:, None])
    y = jnp.einsum('btgn,ghn->btgh', h_re, c_re) - jnp.einsum('btgn,ghn->btgh', h_im, c_im)
    return y, h_re[:, -1], h_im[:, -1]


def s5_mixer(u, uc, lam_re, lam_im, log_step, b_re, b_im, c_re, c_im, d_skip, glu_w, glu_b):
    f32 = jnp.float32
    grp = lambda t: t.astype(f32).reshape(*t.shape[:2], S5_GROUPS, S5_GROUP_CH)
    ug, ucg = grp(u), grp(uc)
    dsk = d_skip.astype(f32).reshape(S5_GROUPS, S5_GROUP_CH)
    y, yc = ug * dsk, ucg * dsk
    zeros = jnp.zeros((u.shape[0], S5_GROUPS, S5_N), f32)
    for direction in range(2):
        fl = lambda t: flip_time(t, direction)
        prm = [t[direction].astype(f32) for t in (lam_re, lam_im, log_step, b_re, b_im, c_re, c_im)]
        y_c, hc_re, hc_im = s5_scan(fl(ucg), *prm, zeros, zeros)
        y_l, _, _ = s5_scan(fl(ug), *prm, hc_re, hc_im)
        yc = yc + fl(y_c)
        y = y + fl(y_l)

    def glu(t):
        t = jax.nn.gelu(t.reshape(*t.shape[:2], S5_CH))
        return (t * jax.nn.sigmoid(t @ glu_w + glu_b)).astype(u.dtype)

    return glu(y), glu(yc)


def token_mixer(h, hc, w_in, att_lam, att_subln, lam_init, conv_w, conv_b, a_log, dt_bias,
                ssd_d, ssd_norm, lam_re, lam_im, log_step, b_re, b_im, c_re, c_im,
                s5_d, glu_w, glu_b, w_branch, w_out):
    q, k, v, z, xbc, dt, u, g = split_cols(h @ w_in, IN_SPLITS)
    qc, kc, vc, zc, xbcc, dtc, uc, gc = split_cols(hc @ w_in, IN_SPLITS)
    qk_heads = lambda t: t.reshape(*t.shape[:2], ATT_HEADS, 2, ATT_DH)
    v_heads = lambda t: t.reshape(*t.shape[:2], ATT_HEADS, ATT_DV)
    o_att, oc_att = diff_attention(qk_heads(q), qk_heads(k), v_heads(v),
                                   qk_heads(qc), qk_heads(kc), v_heads(vc),
                                   att_lam, att_subln, lam_init)
    o_ssd, oc_ssd = mamba2_mixer(z, xbc, dt, zc, xbcc, dtc, conv_w, conv_b, a_log, dt_bias, ssd_d, ssd_norm)
    o_s5, oc_s5 = s5_mixer(u, uc, lam_re, lam_im, log_step, b_re, b_im, c_re, c_im, s5_d, glu_w, glu_b)

    def merge(branches, gate_logits):
        br = jnp.stack([t.astype(gate_logits.dtype) for t in branches], axis=-2)
        gates = jax.nn.sigmoid(gate_logits.reshape(*gate_logits.shape[:-1], N_BRANCH, D_MODEL))
        mixed = jnp.sum(gates * jnp.einsum('btjc,jcd->btjd', br, w_branch), axis=-2)
        return mixed @ w_out

    return merge((o_att, o_ssd, o_s5), g), merge((oc_att, oc_ssd, oc_s5), gc)


def setup_inputs(seed: int = 0) -> dict:
    key = jax.random.key(seed)
    keys = jax.random.split(key, 40)
    kit = iter(range(40))
    f32 = jnp.float32
    L, D = DEPTH, D_MODEL

    def nrm(shape, scale):
        return jax.random.normal(keys[next(kit)], shape, f32) * scale

    def unif(shape, lo, hi):
        return jax.random.uniform(keys[next(kit)], shape, f32, lo, hi)

    x = nrm((BATCH, SEQ, D), 1.0)
    c = nrm((BATCH, D), 1.0)
    ctx = nrm((BATCH, CTX_LEN, D), 1.0)
    c_ctx = nrm((D,), 1.0)
    w_mod = nrm((L, D, N_MOD * D), D ** -0.5)
    b_mod = nrm((L, N_MOD * D), 0.01)
    ln_g = 1.0 + nrm((L, N_SUB, D), 0.02)
    ln_b = nrm((L, N_SUB, D), 0.02)
    ffn_w1 = nrm((L, 2, D, D_FF), D ** -0.5)
    ffn_w3 = nrm((L, 2, D, D_FF), D ** -0.5)
    ffn_w2 = nrm((L, 2, D_FF, D), DN_BETA * D_FF ** -0.5)
    w_in = nrm((L, D, IN_COLS), D ** -0.5)
    att_lam = nrm((L, 4, ATT_DH), 0.1)
    att_subln = 1.0 + nrm((L, ATT_DV), 0.02)
    ssd_conv_w = nrm((L, SSD_CONV, SSD_XBC_W), SSD_CONV ** -0.5)
    ssd_conv_b = nrm((L, SSD_XBC_W), 0.01)
    ssd_a_log = jnp.log(unif((L, 2, SSD_HEADS), 1.0, 16.0))
    dt0 = jnp.exp(unif((L, 2, SSD_HEADS), math.log(1e-3), math.log(1e-1)))
    ssd_dt_bias = dt0 + jnp.log(-jnp.expm1(-dt0))
    ssd_d = 1.0 + nrm((L, SSD_HEADS), 0.02)
    ssd_norm = 1.0 + nrm((L, SSD_INNER), 0.02)
    s5_lam_re = -0.5 * jnp.exp(nrm((L, 2, S5_GROUPS, S5_N), 0.02))
    s5_lam_im = math.pi * jnp.arange(S5_N, dtype=f32) + nrm((L, 2, S5_GROUPS, S5_N), 0.01)
    s5_log_step = unif((L, 2, S5_GROUPS), math.log(1e-3), math.log(1e-1))
    s5_b_re = nrm((L, 2, S5_GROUPS, S5_N, S5_GROUP_CH), (2 * S5_GROUP_CH) ** -0.5)
    s5_b_im = nrm((L, 2, S5_GROUPS, S5_N, S5_GROUP_CH), (2 * S5_GROUP_CH) ** -0.5)
    s5_c_re = nrm((L, 2, S5_GROUPS, S5_GROUP_CH, S5_N), S5_N ** -0.5)
    s5_c_im = nrm((L, 2, S5_GROUPS, S5_GROUP_CH, S5_N), S5_N ** -0.5)
    s5_d = nrm((L, S5_CH), 1.0)
    s5_glu_w = nrm((L, S5_CH, S5_CH), S5_CH ** -0.5)
    s5_glu_b = nrm((L, S5_CH), 0.01)
    w_branch = nrm((L, N_BRANCH, BRANCH_W, D), DN_BETA * BRANCH_W ** -0.5)
    w_out = nrm((L, D, D), DN_BETA * D ** -0.5)
    return {"x": x, "c": c, "ctx": ctx, "c_ctx": c_ctx, "w_mod": w_mod, "b_mod": b_mod,
            "ln_g": ln_g, "ln_b": ln_b, "ffn_w1": ffn_w1, "ffn_w3": ffn_w3, "ffn_w2": ffn_w2,
            "w_in": w_in, "att_lam": att_lam, "att_subln": att_subln,
            "ssd_conv_w": ssd_conv_w, "ssd_conv_b": ssd_conv_b, "ssd_a_log": ssd_a_log,
            "ssd_dt_bias": ssd_dt_bias, "ssd_d": ssd_d, "ssd_norm": ssd_norm,
            "s5_lam_re": s5_lam_re, "s5_lam_im": s5_lam_im, "s5_log_step": s5_log_step,
            "s5_b_re": s5_b_re, "s5_b_im": s5_b_im, "s5_c_re": s5_c_re, "s5_c_im": s5_c_im,
            "s5_d": s5_d, "s5_glu_w": s5_glu_w, "s5_glu_b": s5_glu_b,
            "w_branch": w_branch, "w_out": w_out}


def reference(x, c, ctx, c_ctx, w_mod, b_mod, ln_g, ln_b, ffn_w1, ffn_w3, ffn_w2, w_in,
              att_lam, att_subln, ssd_conv_w, ssd_conv_b, ssd_a_log, ssd_dt_bias, ssd_d, ssd_norm,
              s5_lam_re, s5_lam_im, s5_log_step, s5_b_re, s5_b_im, s5_c_re, s5_c_im,
              s5_d, s5_glu_w, s5_glu_b, w_branch, w_out):
    h, hc = x, ctx
    for i in range(DEPTH):
        mod = (jax.nn.silu(c) @ w_mod[i] + b_mod[i]).reshape(c.shape[0], 1, N_MOD, D_MODEL)
        mod_c = (jax.nn.silu(c_ctx) @ w_mod[i] + b_mod[i]).reshape(N_MOD, D_MODEL)
        lam_init = LAMBDA_INIT_BASE - LAMBDA_INIT_SPAN * math.exp(-LAMBDA_INIT_RATE * i)
        h = half_ffn(h, mod, 0, ffn_w1[i, 0], ffn_w3[i, 0], ffn_w2[i, 0], ln_g[i, 0], ln_b[i, 0])
        hc = half_ffn(hc, mod_c, 0, ffn_w1[i, 0], ffn_w3[i, 0], ffn_w2[i, 0], ln_g[i, 0], ln_b[i, 0])
        sh, sc, gt = adaln(mod, 1)
        shc, scc, gtc = adaln(mod_c, 1)
        y, yc = token_mixer(modulate(h, sh, sc), modulate(hc, shc, scc), w_in[i],
                            att_lam[i], att_subln[i], lam_init,
                            ssd_conv_w[i], ssd_conv_b[i], ssd_a_log[i], ssd_dt_bias[i], ssd_d[i], ssd_norm[i],
                            s5_lam_re[i], s5_lam_im[i], s5_log_step[i], s5_b_re[i], s5_b_im[i],
                            s5_c_re[i], s5_c_im[i], s5_d[i], s5_glu_w[i], s5_glu_b[i],
                            w_branch[i], w_out[i])
        h = post_norm(h, gt * y, ln_g[i, 1], ln_b[i, 1])
        h = half_ffn(h, mod, 2, ffn_w1[i, 1], ffn_w3[i, 1], ffn_w2[i, 1], ln_g[i, 2], ln_b[i, 2])
        if i + 1 < DEPTH:
            hc = post_norm(hc, gtc * yc, ln_g[i, 1], ln_b[i, 1])
            hc = half_ffn(hc, mod_c, 2, ffn_w1[i, 1], ffn_w3[i, 1], ffn_w2[i, 1], ln_g[i, 2], ln_b[i, 2])
    return h
```

```python
import math
from contextlib import ExitStack, contextmanager
import numpy as np
import concourse.bass as bass
import concourse.mybir as mybir
from concourse.bass_utils import run_bass_kernel_spmd
from concourse.ap import AP

F32 = mybir.dt.float32
BF16 = mybir.dt.bfloat16
I32 = mybir.dt.int32
AF = mybir.ActivationFunctionType
ALU = mybir.AluOpType
AX = mybir.AxisListType

EPOCH = 16000
NDS = 48
D = 2048
KC = 16
DFF = 5632
FC = 44
NCTX = 256
INC = 13344
DEPTH = 2
ALPHA = (2 * DEPTH) ** 0.25
PI = math.pi


class Buf:
    __slots__ = ("name", "w", "r")

    def __init__(self, name=""):
        self.name = name
        self.w = None
        self.r = {}


class TL:
    def __init__(self, t, b):
        self.t = t
        self.b = b

    def __getitem__(self, k):
        return self.t[k]


class Prog:
    ENGS = ("pe", "act", "dve", "pool", "sp")

    def __init__(self, nc, es):
        self.nc = nc
        self.es = es
        self.q = {e: [] for e in self.ENGS}
        self.cnt = {e: 0 for e in self.ENGS}
        self.csem = {e: [] for e in self.ENGS}
        self.waited = {e: {} for e in self.ENGS}
        self.dsems = [es.enter_context(nc.semaphore(f"dq{i}")) for i in range(NDS)]
        self.dcnt = [0] * NDS
        self.dlast = [None] * NDS
        self.dnext = 0
        self.bufs = {}
        self.out_events = []
        self.rr = 0
        self.phase_name = "init"
        self.scopes = False

    def buf(self, key):
        b = self.bufs.get(key)
        if b is None:
            b = self.bufs[key] = Buf(str(key))
        return b

    def _csem(self, e, ep):
        while len(self.csem[e]) <= ep:
            self.csem[e].append(
                self.es.enter_context(self.nc.semaphore(f"c{e}{len(self.csem[e])}")))
        return self.csem[e][ep]

    def emit(self, eng, fn, reads=(), writes=(), dma=False, is_out=False):
        deps = {}

        def add(ev):
            if ev is None:
                return
            k = ev[0]
            if k not in deps or deps[k][2] < ev[2]:
                deps[k] = ev

        for b in reads:
            add(b.w)
        for b in writes:
            add(b.w)
            for ev in b.r.values():
                add(ev)
        if dma:
            i = self.dnext
            self.dnext = (self.dnext + 1) % NDS
            add(self.dlast[i])
            self.dcnt[i] += 1
            ev = (("d", i), self.dsems[i], 16 * self.dcnt[i])
            self.dlast[i] = ev
            inc = 16
        else:
            self.cnt[eng] += 1
            ep = (self.cnt[eng] - 1) // EPOCH
            ev = ((eng, ep), self._csem(eng, ep), self.cnt[eng] - ep * EPOCH)
            inc = 1
        waits = []
        wd = self.waited[eng]
        for k, (_, s, v) in deps.items():
            if eng == "pe" and k[0] == "pe":
                continue
            if wd.get(k, 0) >= v:
                continue
            wd[k] = v
            waits.append((s, v))
        self.q[eng].append((waits, fn, ev[1], inc, self.phase_name))
        for b in writes:
            b.w = ev
            b.r = {}
        for b in reads:
            k = ev[0]
            if k not in b.r or b.r[k][2] < ev[2]:
                b.r[k] = ev
        if is_out:
            self.out_events.append(ev)
        return ev

    def barrier(self):
        evs = []
        for e in self.ENGS:
            c = self.cnt[e]
            if c > 0:
                ep = (c - 1) // EPOCH
                evs.append(((e, ep), self._csem(e, ep), c - ep * EPOCH))
        for i in range(NDS):
            if self.dlast[i] is not None:
                evs.append(self.dlast[i])
        for e in self.ENGS:
            waits = []
            wd = self.waited[e]
            for (k, s, v) in evs:
                if k[0] == e:
                    continue
                if wd.get(k, 0) >= v:
                    continue
                wd[k] = v
                waits.append((s, v))
            if waits:
                self.q[e].append((waits, None, None, 0, self.phase_name))

    def finish(self):
        self.barrier()
        with self.nc.Block() as block:
            def mk(e):
                def f(eo):
                    cur = None
                    cm = None
                    for waits, fn, sem, inc, ph in self.q[e]:
                        if self.scopes and ph != cur:
                            if cm is not None:
                                cm.__exit__(None, None, None)
                            cm = self.nc.named_scope(ph)
                            cm.__enter__()
                            cur = ph
                        for (s, v) in waits:
                            eo.wait_ge(s, v)
                        if fn is not None:
                            fn(eo).then_inc(sem, inc)
                    if cm is not None:
                        cm.__exit__(None, None, None)
                return f
            block.tensor(mk("pe"))
            block.scalar(mk("act"))
            block.vector(mk("dve"))
            block.gpsimd(mk("pool"))
            block.sync(mk("sp"))

    def dma(self, out, in_, reads=(), writes=(), eng="sp", is_out=False):
        return self.emit(eng, lambda e: e.dma_start(out=out, in_=in_), reads, writes,
                         dma=True, is_out=is_out)

    def mm(self, out, lhsT, rhs, start, stop, reads=(), writes=()):
        return self.emit("pe", lambda e: e.matmul(out, lhsT, rhs, start=start, stop=stop),
                         reads, writes)

    def tr(self, out, in_, ident, reads=(), writes=()):
        return self.emit("pe", lambda e: e.transpose(out, in_, ident), reads, writes)

    def act(self, out, in_, func, bias=0.0, scale=1.0, reads=(), writes=()):
        return self.emit("act", lambda e: e.activation(out, in_, func, bias=bias, scale=scale),
                         reads, writes)

    def tt(self, eng, out, in0, in1, op, reads=(), writes=()):
        return self.emit(eng, lambda e: e.tensor_tensor(out, in0, in1, op), reads, writes)

    def ts(self, eng, out, in0, s1, s2, op0, op1=None, reads=(), writes=()):
        if op1 is None:
            return self.emit(eng, lambda e: e.tensor_scalar(out, in0, s1, None, op0), reads, writes)
        return self.emit(eng, lambda e: e.tensor_scalar(out, in0, s1, s2, op0, op1), reads, writes)

    def stt(self, eng, out, in0, scalar, in1, op0, op1, reads=(), writes=()):
        return self.emit(eng, lambda e: e.scalar_tensor_tensor(out, in0, scalar, in1, op0, op1),
                         reads, writes)

    def copy(self, eng, out, in_, reads=(), writes=()):
        if eng == "act":
            return self.emit(eng, lambda e: e.copy(out, in_), reads, writes)
        return self.emit(eng, lambda e: e.tensor_copy(out, in_), reads, writes)

    def memset(self, eng, ap, val, writes=()):
        return self.emit(eng, lambda e: e.memset(ap, val), (), writes)

    def scan(self, eng, out, d0, d1, init, reads=(), writes=()):
        return self.emit(eng, lambda e: e.tensor_tensor_scan(out, d0, d1, init, ALU.mult, ALU.add),
                         reads, writes)

    def ve(self):
        self.rr += 1
        return ("dve", "pool")[self.rr % 2]


def rev_last(a):
    ap = [list(x) for x in a.ap]
    st, n = ap[-1]
    ap[-1] = [-st, n]
    return AP(a.tensor, a.offset + (n - 1) * st, ap)


def build(NX=4096, dbg=None):
    T = NCTX + NX
    NCH = T // 128
    blocks = [(0, 256, 1)] + [(NCTX + 512 * i, 512, 0) for i in range(NX // 512)]
    nc = bass.Bass("TRN2", target_bir_lowering=False)
    din = {}

    tname = {}

    def inp(name, shape, dt=F32):
        din[name] = nc.dram_tensor(name, list(shape), dt, kind="ExternalInput")
        tname[id(din[name])] = name
        return din[name]

    xT = inp("xT", [D, T])
    cvec = inp("cvec", [128, KC, 2])
    w_mod = inp("w_mod", [2, D, 9 * D])
    bmod = inp("bmod", [2, 128, 144])
    lng = inp("lng", [2, 128, 3, KC])
    lnb = inp("lnb", [2, 128, 3, KC])
    ffn_w1 = inp("ffn_w1", [2, 2, D, DFF])
    ffn_w3 = inp("ffn_w3", [2, 2, D, DFF])
    ffn_w2 = inp("ffn_w2", [2, 2, DFF, D])
    w_in = inp("w_in", [2, D, INC])
    attlam = inp("attlam", [2, 128, 256])
    subln = inp("subln", [2, 128, 1])
    convw = inp("convw", [2, 128, 16, 5])
    convb = inp("convb", [2, 128, 16])
    alog = inp("alog", [2, 128, 32])
    dtb = inp("dtb", [2, 128, 32])
    ssdD = inp("ssdD", [2, 128, 8])
    ssdnw = inp("ssdnw", [2, 128, 8])
    s5lre = inp("s5lre", [2, 2, 128, 32])
    s5lim = inp("s5lim", [2, 2, 128, 32])
    s5ls = inp("s5ls", [2, 2, 128, 32])
    s5bre = inp("s5bre", [2, 2, 32, 128, 128])
    s5bim = inp("s5bim", [2, 2, 32, 128, 128])
    s5cre = inp("s5cre", [2, 2, 32, 128, 128])
    s5cim = inp("s5cim", [2, 2, 32, 128, 128])
    s5d = inp("s5d", [2, 128, 8])
    glu_w = inp("glu_w", [2, 1024, 1024])
    glub = inp("glub", [2, 128, 8])
    w_branch = inp("w_branch", [2, 3, 1024, D])
    w_out = inp("w_out", [2, D, D])
    consts = inp("consts", [128, 10, 128])
    ropec = inp("ropec", [128, NX])
    ropes = inp("ropes", [128, NX])
    yT = nc.dram_tensor("yT", [D, NX], F32, kind="ExternalOutput")

    def scr(name, shape, dt):
        t_ = nc.dram_tensor(name, list(shape), dt, kind="Internal")
        tname[id(t_)] = name
        return t_

    w1b = scr("w1b", [2, 2, D, DFF], BF16)
    w3b = scr("w3b", [2, 2, D, DFF], BF16)
    w2b = scr("w2b", [2, 2, DFF, D], BF16)
    winb = scr("winb", [2, D, INC], BF16)
    wbrb = scr("wbrb", [2, 3, 1024, D], BF16)
    woutb = scr("woutb", [2, D, D], BF16)
    glub16 = scr("glub16", [2, 1024, 1024], BF16)
    sbreb = scr("sbreb", [2, 2, 32, 128, 128], BF16)
    sbimb = scr("sbimb", [2, 2, 32, 128, 128], BF16)
    screb = scr("screb", [2, 2, 32, 128, 128], BF16)
    scimb = scr("scimb", [2, 2, 32, 128, 128], BF16)
    hT = scr("hT", [D, T], F32)
    qT = scr("qT", [1024, T], BF16)
    kT = scr("kT", [1024, T], BF16)
    vtok = scr("vtok", [T, 1024], BF16)
    zT = scr("zT", [1024, T], F32)
    xbcT = scr("xbcT", [2048, T], F32)
    dttok = scr("dttok", [T, 32], F32)
    uT = scr("uT", [1024, T], F32)
    gT = scr("gT", [3 * D, T], F32)
    oattT = scr("oattT", [1024, T], BF16)
    xsT = scr("xsT", [1024, T], F32)
    xstok = scr("xstok", [T, 1024], F32)
    btok = scr("btok", [T, 512], BF16)
    BTs = scr("BTs", [512, T], BF16)
    CTs = scr("CTs", [512, T], BF16)
    y0T = scr("y0T", [1024, T], F32)
    ossdT = scr("ossdT", [1024, T], BF16)
    XS = scr("XS", [2, 2, 4096, T], BF16)
    os5T = scr("os5T", [1024, T], BF16)
    dbg_out = None
    scr_map = dict(hT=hT, qT=qT, kT=kT, vtok=vtok, zT=zT, xbcT=xbcT, dttok=dttok, uT=uT, gT=gT,
                   oattT=oattT, xsT=xsT, xstok=xstok, btok=btok, BTs=BTs, CTs=CTs, y0T=y0T,
                   ossdT=ossdT, os5T=os5T)

    with ExitStack() as es:
        P = Prog(nc, es)
        P.scopes = bool(dbg and dbg.get("scopes"))
        uid = [0]

        def mk_sb(stack, shape, dt=F32, name=None):
            uid[0] += 1
            n = f"{name or 't'}{uid[0]}"
            t = stack.enter_context(nc.sbuf_tensor(n, list(shape), dt))
            return TL(t, Buf(n))

        psum = []
        for i in range(8):
            t = es.enter_context(nc.psum_tensor(f"ps{i}", [128, 512], F32))
            psum.append(TL(t, Buf(f"ps{i}")))

        DB = P.buf

        @contextmanager
        def phase():
            with ExitStack() as ph:
                yield lambda shape, dt=F32, name=None: mk_sb(ph, shape, dt, name)
                P.barrier()

        cst = mk_sb(es, [128, 10, 128], F32, "cst")
        P.dma(cst[:], consts.ap(), writes=[cst.b])
        IDENT, ONES, TRI_LE, TRI_GE, SGT, SLT, PERM, BLK0, BLK1, TAU = range(10)
        C_ = lambda i: cst[:, i, :]
        ones_bf = mk_sb(es, [128, 128], BF16, "onesbf")
        P.copy("dve", ones_bf[:], C_(ONES), reads=[cst.b], writes=[ones_bf.b])
        modv = mk_sb(es, [128, 144, 2], F32, "modv")
        modc = mk_sb(es, [128, 3, KC, 2], F32, "modc")
        modg = mk_sb(es, [128, 3, KC, 2], F32, "modg")
        lngt = mk_sb(es, [128, 3, KC], F32, "lngt")
        lnbt = mk_sb(es, [128, 3, KC], F32, "lnbt")

        def pm_dst(t, lead, KCt, PW):
            def f(a0, an, c0, cn):
                assert c0 % PW == 0 and cn % PW == 0
                n0, nn = c0 // PW, cn // PW
                off = lead + n0 * 128 * KCt * PW + a0 * PW
                return [AP(t, off + ai * PW, [[KCt * PW, 128], [128 * KCt * PW, nn], [1, PW]]) for ai in range(an)]
            f.PW = PW
            return f

        def pm_panel(t, lead, KCt, PW, n):
            return AP(t, lead + n * 128 * KCt * PW, [[KCt * PW, 128], [1, KCt * PW]])

        def cast3(src, dst, A, C, scale=None):
            with phase() as sb:
                st = [sb([128, 4096], F32, "cst32") for _ in range(3)]
                sbf = [sb([128, 4096], BF16, "cst16") for _ in range(3)]
                it = 0
                cb = min(C, 4096)
                ab = max(1, 4096 // cb)
                for a0 in range(0, A, ab):
                    an = min(ab, A - a0)
                    for c0 in range(0, C, cb):
                        cn = min(cb, C - c0)
                        s32 = st[it % 3]
                        s16 = sbf[it % 3]
                        v32 = s32[:, 0:an * cn].rearrange("p (a c) -> p a c", a=an)
                        v16 = s16[:, 0:an * cn].rearrange("p (a c) -> p a c", a=an)
                        P.dma(v32, src[:, a0:a0 + an, c0:c0 + cn], writes=[s32.b])
                        eng = ("dve", "act", "pool")[it % 3]
                        if scale is not None:
                            P.ts("dve", v16, v32, scale, None, ALU.mult, reads=[s32.b], writes=[s16.b])
                        else:
                            P.copy(eng, v16, v32, reads=[s32.b], writes=[s16.b])
                        if callable(dst):
                            for ai, dap in enumerate(dst(a0, an, c0, cn)):
                                P.dma(dap, v16[:, ai, :].rearrange("p (n c) -> p n c", c=dst.PW), reads=[s16.b], eng="act")
                        else:
                            P.dma(dst[:, a0:a0 + an, c0:c0 + cn], v16, reads=[s16.b], eng="pool")
                        it += 1

        def cast_w(src_ap, dst_ap, R, C, scale=None):
            cast3(src_ap.rearrange("(a p) c -> p a c", p=128),
                  dst_ap if callable(dst_ap) else dst_ap.rearrange("(a p) c -> p a c", p=128), R // 128, C, scale)

        for l in range(0 if not (dbg and dbg.get('nocast')) else 2, 2):
            for j in range(2):
                cast_w(ffn_w1.ap()[l, j], w1b.ap()[l, j], D, DFF)
                cast_w(ffn_w3.ap()[l, j], w3b.ap()[l, j], D, DFF)
                cast_w(ffn_w2.ap()[l, j], w2b.ap()[l, j], DFF, D)
            cast_w(w_in.ap()[l], winb.ap()[l], D, INC)
            for j in range(3):
                cast_w(w_branch.ap()[l, j], wbrb.ap()[l, j], 1024, D)
            cast_w(w_out.ap()[l], woutb.ap()[l], D, D)
            cast_w(glu_w.ap()[l], glub16.ap()[l], 1024, 1024)
            for d in range(2):
                for (s_, d_, sc_) in ((s5bre, sbreb, None), (s5bim, sbimb, None), (s5cre, screb, None),
                                      (s5cim, scimb, -1.0)):
                    cast3(s_.ap()[l, d].rearrange("s p c -> p s c"), d_.ap()[l, d].rearrange("s p c -> p s c"),
                          32, 128, sc_)

        def load_wpanel(tile, wsrc, kc_n, c0, cn):
            P.dma(tile[:, 0:kc_n, 0:cn], wsrc.rearrange("(a p) c -> p a c", p=128)[:, :, c0:c0 + cn],
                  writes=[tile.b])

        def layer_norm(sb_tmp, hx, TB, l, j, pss1, pss2):
            sq = sb_tmp["sq"]
            for kc in range(KC):
                P.mm(pss1[:, 0:TB], C_(ONES), hx[:, kc, 0:TB], kc == 0, kc == KC - 1,
                     reads=[cst.b, hx.b], writes=[pss1.b])
            for kc in range(KC):
                s = sq[kc % 2]
                P.act(s[:, 0:TB], hx[:, kc, 0:TB], AF.Square, reads=[hx.b], writes=[s.b])
                P.mm(pss2[:, 0:TB], C_(ONES), s[:, 0:TB], kc == 0, kc == KC - 1,
                     reads=[cst.b, s.b], writes=[pss2.b])
            mean, rstd, nmr = sb_tmp["mean"], sb_tmp["rstd"], sb_tmp["nmr"]
            P.ts("dve", mean[:, 0:TB], pss1[:, 0:TB], 1.0 / D, None, ALU.mult, reads=[pss1.b], writes=[mean.b])
            P.tt("dve", nmr[:, 0:TB], mean[:, 0:TB], mean[:, 0:TB], ALU.mult, reads=[mean.b], writes=[nmr.b])
            P.stt("dve", rstd[:, 0:TB], pss2[:, 0:TB], 1.0 / D, nmr[:, 0:TB], ALU.mult, ALU.subtract,
                  reads=[pss2.b, nmr.b], writes=[rstd.b])
            P.act(rstd[:, 0:TB], rstd[:, 0:TB], AF.Ln, bias=kct[:, 0:1], reads=[rstd.b, kct.b], writes=[rstd.b])
            P.act(rstd[:, 0:TB], rstd[:, 0:TB], AF.Exp, scale=-0.5, reads=[rstd.b], writes=[rstd.b])
            P.stt("dve", nmr[:, 0:TB], mean[:, 0:TB], -1.0, rstd[:, 0:TB], ALU.mult, ALU.mult,
                  reads=[mean.b, rstd.b], writes=[nmr.b])
            for kc in range(KC):
                e = P.ve()
                P.tt(e, hx[:, kc, 0:TB], hx[:, kc, 0:TB], rstd[:, 0:TB], ALU.mult, reads=[hx.b, rstd.b], writes=[hx.b])
                P.tt(e, hx[:, kc, 0:TB], hx[:, kc, 0:TB], nmr[:, 0:TB], ALU.add, reads=[hx.b, nmr.b], writes=[hx.b])
                P.act(hx[:, kc, 0:TB], hx[:, kc, 0:TB], AF.Identity, bias=lnbt[:, j, kc:kc + 1],
                      scale=lngt[:, j, kc:kc + 1], reads=[hx.b, lngt.b, lnbt.b], writes=[hx.b])

        def hview(dr, t0, TB):
            return dr.ap().rearrange("(a p) t -> p a t", p=128)[:, :, t0:t0 + TB]

        def ln_tmps(sb):
            return dict(sq=[sb([128, 512], F32, "sq") for _ in range(2)], mean=sb([128, 512], F32, "mean"),
                        rstd=sb([128, 512], F32, "rstd"), nmr=sb([128, 512], F32, "nmr"))

        def mod_phase(l):
            with phase() as sb:
                cv = sb([128, KC, 2], F32, "cv")
                sc = sb([128, KC, 2], F32, "sc")
                bm = sb([128, 144], F32, "bm")
                P.dma(cv[:], cvec.ap(), writes=[cv.b])
                P.dma(bm[:], bmod.ap()[l], writes=[bm.b])
                P.dma(lngt[:], lng.ap()[l], writes=[lngt.b])
                P.dma(lnbt[:], lnb.ap()[l], writes=[lnbt.b])
                P.act(sc[:], cv[:], AF.Silu, reads=[cv.b], writes=[sc.b])
                wp = [sb([128, KC, 512], F32, "wmod") for _ in range(2)]
                wv = w_mod.ap()[l].rearrange("(a p) c -> p a c", p=128)
                for cbk in range(36):
                    w = wp[cbk % 2]
                    P.dma(w[:], wv[:, :, cbk * 512:(cbk + 1) * 512], writes=[w.b])
                    pm = psum[cbk % 2]
                    for m in range(4):
                        for kc in range(KC):
                            P.mm(pm[:, 2 * m:2 * m + 2], w[:, kc, m * 128:(m + 1) * 128], sc[:, kc, :],
                                 kc == 0, kc == KC - 1, reads=[w.b, sc.b], writes=[pm.b])
                    for m in range(4):
                        mt = cbk * 4 + m
                        P.ts("dve", modv[:, mt, :], pm[:, 2 * m:2 * m + 2], bm[:, mt:mt + 1], None, ALU.add,
                             reads=[pm.b, bm.b], writes=[modv.b])
                mv = modv[:].rearrange("p (j r k) w -> p j r k w", j=3, r=3)
                P.ts("dve", modc[:], mv[:, :, 1, :, :], 1.0, None, ALU.add, reads=[modv.b], writes=[modc.b])
                for j in range(3):
                    P.ts("dve", modg[:, j], mv[:, j, 2, :, :], 0.5 if j != 1 else 1.0, None, ALU.mult,
                         reads=[modv.b], writes=[modg.b])
            return

        def mshift(j, kc, which):
            return modv[:, (3 * j) * KC + kc, which:which + 1]

        def modulate(xm, hx, j, TB, which):
            for kc in range(KC):
                P.act(xm[:, kc, 0:TB], hx[:, kc, 0:TB], AF.Identity, bias=mshift(j, kc, which),
                      scale=modc[:, j, kc, which:which + 1], reads=[hx.b, modv.b, modc.b], writes=[xm.b])

        def ffn_phase(l, jj, src, dst, skip_ctx, to_out):
            j = 0 if jj == 0 else 2
            W1 = w1b.ap()[l, jj]
            W3 = w3b.ap()[l, jj]
            W2 = w2b.ap()[l, jj]
            with phase() as sb:
                hx = sb([128, KC, 512], F32, "hx")
                xm = sb([128, KC, 512], BF16, "xm")
                g = sb([128, FC, 512], BF16, "g")
                w1p = [sb([128, KC, 256], BF16, "w1p") for _ in range(2)]
                w3p = [sb([128, KC, 256], BF16, "w3p") for _ in range(2)]
                w2p = [sb([128, FC, 128], BF16, "w2p") for _ in range(2)]
                sa = [sb([128, 512], F32, "sa") for _ in range(2)]
                lt = ln_tmps(sb)
                for (t0, TB, which) in blocks:
                    if skip_ctx and which == 1:
                        continue
                    P.dma(hx[:, :, 0:TB], hview(src, t0, TB), reads=[DB(tname[id(src)])], writes=[hx.b])
                    modulate(xm, hx, j, TB, which)
                    for kc in range(KC):
                        P.ts(P.ve(), hx[:, kc, 0:TB], hx[:, kc, 0:TB], ALPHA, None, ALU.mult, reads=[hx.b], writes=[hx.b])
                    for fp in range(FC // 2):
                        a, b = w1p[fp % 2], w3p[fp % 2]
                        load_wpanel(a, W1, KC, fp * 256, 256)
                        load_wpanel(b, W3, KC, fp * 256, 256)
                        for f2 in range(2):
                            f = fp * 2 + f2
                            pa, pb = psum[(f % 2) * 2], psum[(f % 2) * 2 + 1]
                            for kc in range(KC):
                                P.mm(pa[:, 0:TB], a[:, kc, f2 * 128:(f2 + 1) * 128], xm[:, kc, 0:TB], kc == 0, kc == KC - 1,
                                     reads=[a.b, xm.b], writes=[pa.b])
                            for kc in range(KC):
                                P.mm(pb[:, 0:TB], b[:, kc, f2 * 128:(f2 + 1) * 128], xm[:, kc, 0:TB], kc == 0, kc == KC - 1,
                                     reads=[b.b, xm.b], writes=[pb.b])
                            s = sa[f % 2]
                            P.act(s[:, 0:TB], pa[:, 0:TB], AF.Silu, reads=[pa.b], writes=[s.b])
                            P.tt("dve", g[:, f, 0:TB], s[:, 0:TB], pb[:, 0:TB], ALU.mult, reads=[s.b, pb.b], writes=[g.b])
                    for m in range(KC):
                        w = w2p[m % 2]
                        load_wpanel(w, W2, FC, m * 128, 128)
                        py = psum[4 + m % 2]
                        for kc in range(FC):
                            P.mm(py[:, 0:TB], w[:, kc, :], g[:, kc, 0:TB], kc == 0, kc == FC - 1,
                                 reads=[w.b, g.b], writes=[py.b])
                        P.stt("dve", hx[:, m, 0:TB], py[:, 0:TB], modg[:, j, m, which:which + 1], hx[:, m, 0:TB],
                              ALU.mult, ALU.add, reads=[py.b, modg.b, hx.b], writes=[hx.b])
                    layer_norm(lt, hx, TB, l, j, psum[6], psum[7])
                    if to_out:
                        P.dma(yT.ap().rearrange("(a p) t -> p a t", p=128)[:, :, t0 - NCTX:t0 - NCTX + TB],
                              hx[:, :, 0:TB], reads=[hx.b], eng="pool", is_out=True)
                    else:
                        P.dma(hview(dst, t0, TB), hx[:, :, 0:TB], reads=[hx.b], writes=[DB(tname[id(dst)])], eng="pool")

        def inproj_phase(l):
            W = winb.ap()[l]
            fm_specs = [(0, qT, BF16, True), (1024, kT, BF16, True), (3072, zT, F32, False),
                        (4096, xbcT, F32, False), (4096 + 1024, xbcT, F32, False), (6176, uT, F32, False)] + \
                       [(7200 + 1024 * i, gT, F32, False) for i in range(6)]
            with phase() as sb:
                hx = sb([128, KC, 512], F32, "hx")
                xm = sb([128, KC, 512], BF16, "xm")
                wp = [sb([128, KC, 512], BF16, "wp") for _ in range(2)]
                st32 = [sb([128, 4, 512], F32, "st32") for _ in range(2)]
                st16 = [sb([128, 4, 512], BF16, "st16") for _ in range(2)]
                qf = [sb([128, 512], F32, "qf") for _ in range(2)]
                rc = sb([128, 512], F32, "rc")
                rs = sb([128, 512], F32, "rs")
                vst = [sb([128, 1024], BF16, "vst") for _ in range(2)]
                dst_ = [sb([128, 32], F32, "dst") for _ in range(2)]
                wdt = sb([128, KC, 32], BF16, "wdt")
                it = 0
                for (t0, TB, which) in blocks:
                    P.dma(hx[:, :, 0:TB], hview(hT, t0, TB), reads=[DB("hT")], writes=[hx.b])
                    modulate(xm, hx, 1, TB, which)
                    if which == 0:
                        P.dma(rc[:, 0:TB], ropec.ap()[:, t0 - NCTX:t0 - NCTX + TB], writes=[rc.b])
                        P.dma(rs[:, 0:TB], ropes.ap()[:, t0 - NCTX:t0 - NCTX + TB], writes=[rs.b])
                    for si, (c0, dstT, dt, rope) in enumerate(fm_specs):
                        for pn in range(2):
                            w = wp[it % 2]
                            cc = c0 + pn * 512
                            load_wpanel(w, W, KC, cc, 512)
                            stg = (st16 if dt == BF16 else st32)[it % 2]
                            for m in range(4):
                                pm = psum[(it * 4 + m) % 4]
                                for kc in range(KC):
                                    P.mm(pm[:, 0:TB], w[:, kc, m * 128:(m + 1) * 128], xm[:, kc, 0:TB], kc == 0, kc == KC - 1,
                                         reads=[w.b, xm.b], writes=[pm.b])
                                if rope and which == 0:
                                    q_ = qf[m % 2]
                                    P.copy("act", q_[:, 0:TB], pm[:, 0:TB], reads=[pm.b], writes=[q_.b])
                                    pr = psum[4 + m % 2]
                                    P.mm(pr[:, 0:TB], C_(PERM), q_[:, 0:TB], True, True, reads=[cst.b, q_.b], writes=[pr.b])
                                    P.tt("dve", q_[:, 0:TB], q_[:, 0:TB], rc[:, 0:TB], ALU.mult, reads=[q_.b, rc.b], writes=[q_.b])
                                    P.stt("dve", stg[:, m, 0:TB], pr[:, 0:TB], 1.0, rs[:, 0:TB], ALU.mult, ALU.mult,
                                          reads=[pr.b, rs.b], writes=[stg.b])
                                    P.tt("dve", stg[:, m, 0:TB], stg[:, m, 0:TB], q_[:, 0:TB], ALU.add,
                                         reads=[stg.b, q_.b], writes=[stg.b])
                                else:
                                    P.copy("act" if m % 2 else "dve", stg[:, m, 0:TB], pm[:, 0:TB], reads=[pm.b], writes=[stg.b])
                            rows = (cc - c0) + (0 if dstT not in (xbcT, gT) else (c0 - (4096 if dstT is xbcT else 7200)))
                            dv = dstT.ap().rearrange("(a p) t -> p a t", p=128)[:, rows // 128:rows // 128 + 4, t0:t0 + TB]
                            P.dma(dv, stg[:, :, 0:TB], reads=[stg.b], writes=[DB(tname[id(dstT)])], eng="pool")
                            it += 1
                    wv_ = [wp[0], wp[1]]
                    load_wpanel(wv_[0], W, KC, 2048, 512)
                    load_wpanel(wv_[1], W, KC, 2560, 512)
                    P.dma(wdt[:], W.rearrange("(a p) c -> p a c", p=128)[:, :, 6144:6176], writes=[wdt.b])
                    wdtv = wdt
                    for tt_ in range(TB // 128):
                        vs = vst[tt_ % 2]
                        for hh in range(2):
                            pm = psum[(tt_ * 2 + hh) % 4]
                            for kc in range(KC):
                                P.mm(pm[:, :], xm[:, kc, tt_ * 128:(tt_ + 1) * 128], wv_[hh][:, kc, :], kc == 0, kc == KC - 1,
                                     reads=[xm.b, wv_[hh].b], writes=[pm.b])
                            P.copy("act" if hh else "dve", vs[:, hh * 512:(hh + 1) * 512], pm[:, :], reads=[pm.b], writes=[vs.b])
                        P.dma(vtok.ap()[t0 + tt_ * 128:t0 + (tt_ + 1) * 128, :], vs[:], reads=[vs.b], writes=[DB("vtok")], eng="pool")
                        pd = psum[4 + tt_ % 2]
                        for kc in range(KC):
                            P.mm(pd[:, 0:32], xm[:, kc, tt_ * 128:(tt_ + 1) * 128], wdtv[:, kc, :], kc == 0, kc == KC - 1,
                                 reads=[xm.b, wdt.b], writes=[pd.b])
                        ds = dst_[tt_ % 2]
                        P.copy("dve", ds[:], pd[:, 0:32], reads=[pd.b], writes=[ds.b])
                        P.dma(dttok.ap()[t0 + tt_ * 128:t0 + (tt_ + 1) * 128, :], ds[:], reads=[ds.b], writes=[DB("dttok")], eng="pool")

        def att_phase(l, last):
            lam_init = 0.8 - 0.6 * math.exp(-0.3 * l)
            with phase() as sb:
                al = sb([128, 256], F32, "al")
                sw = sb([128, 1], F32, "sw")
                sm = sb([128, 8], F32, "sm")
                P.dma(al[:], attlam.ap()[l], writes=[al.b])
                P.dma(sw[:], subln.ap()[l], writes=[sw.b])
                pr_ = sb([128, 128], F32, "pr_")
                P.tt("dve", pr_[:, 0:64], al[:, 0:64], al[:, 64:128], ALU.mult, reads=[al.b], writes=[pr_.b])
                P.tt("dve", pr_[:, 64:128], al[:, 128:192], al[:, 192:256], ALU.mult, reads=[al.b], writes=[pr_.b])
                P.emit("dve", lambda e: e.tensor_reduce(sm[:, 0:2], pr_[:].rearrange("p (a b) -> p a b", a=2), AX.X, ALU.add),
                       reads=[pr_.b], writes=[sm.b])
                P.act(sm[:, 2:4], sm[:, 0:2], AF.Exp, reads=[sm.b], writes=[sm.b])
                P.tt("dve", sm[:, 4:5], sm[:, 3:4], sm[:, 2:3], ALU.subtract, reads=[sm.b], writes=[sm.b])
                P.ts("dve", sm[:, 5:6], sm[:, 4:5], -lam_init, None, ALU.add, reads=[sm.b], writes=[sm.b])
                P.ts("dve", sm[:, 6:7], sw[:, 0:1], 1.0 - lam_init, None, ALU.mult, reads=[sw.b], writes=[sm.b])
                neglam = sm[:, 5:6]
                swl = sm[:, 6:7]
                qh = [sb([128, T], BF16, "qh") for _ in range(2)]
                kh = [sb([128, T], BF16, "kh") for _ in range(2)]
                vh = [sb([128, NCH, 128], BF16, "vh") for _ in range(2)]
                sqt = [sb([128, 512], F32, "sqt") for _ in range(2)]
                mx = sb([128, 16], F32, "mx")
                negc = sb([128, 2], F32, "negc")
                pt = [sb([128, 512], BF16, "pt") for _ in range(4)]
                o0 = sb([128, 512], F32, "o0")
                o1 = sb([128, 512], F32, "o1")
                r0 = sb([128, 512], F32, "r0")
                ob = [sb([128, 512], BF16, "ob") for _ in range(2)]
                for h in range(8):
                    q_, k_, v_ = qh[h % 2], kh[h % 2], vh[h % 2]
                    P.dma(q_[:], qT.ap()[h * 128:(h + 1) * 128, :], reads=[DB("qT")], writes=[q_.b])
                    P.dma(k_[:], kT.ap()[h * 128:(h + 1) * 128, :], reads=[DB("kT")], writes=[k_.b])
                    P.dma(v_[:], vtok.ap().rearrange("(a p) c -> p a c", p=128)[:, :, h * 128:(h + 1) * 128],
                          reads=[DB("vtok")], writes=[v_.b])
                    P.memset("dve", mx[:], 0.0, writes=[mx.b])
                    it = 0
                    for qi, src_ in enumerate((q_, k_)):
                        for (t0, TB, which) in blocks:
                            s = sqt[it % 2]
                            P.act(s[:, 0:TB], src_[:, t0:t0 + TB], AF.Square, reads=[src_.b], writes=[s.b])
                            for jm in range(2):
                                pm = psum[(it * 2 + jm) % 4]
                                P.mm(pm[:, 0:TB], C_(BLK0 + jm), s[:, 0:TB], True, True, reads=[cst.b, s.b], writes=[pm.b])
                                col = 8 + qi * 2 + jm
                                P.emit("dve", lambda e, pm=pm, TB=TB, col=col: e.tensor_reduce(mx[:, col:col + 1], pm[:, 0:TB], AX.X, ALU.max),
                                       reads=[pm.b], writes=[mx.b])
                                c2 = qi * 2 + jm
                                P.tt("dve", mx[:, c2:c2 + 1], mx[:, c2:c2 + 1], mx[:, col:col + 1], ALU.max, reads=[mx.b], writes=[mx.b])
                            it += 1
                    P.tt("dve", negc[:], mx[:, 0:2], mx[:, 2:4], ALU.mult, reads=[mx.b], writes=[negc.b])
                    P.act(negc[:], negc[:], AF.Ln, reads=[negc.b], writes=[negc.b])
                    P.act(negc[:], negc[:], AF.Exp, scale=0.5, reads=[negc.b], writes=[negc.b])
                    P.ts("dve", negc[:], negc[:], -0.125, None, ALU.mult, reads=[negc.b], writes=[negc.b])
                    for bi, (t0, TB, which) in enumerate(blocks):
                        if which == 1 and last:
                            continue
                        nk = 2 if which == 1 else NCH
                        pO = (psum[4], psum[5])
                        pS = (psum[6], psum[7])
                        def qk(kt):
                            for jm in range(2):
                                pm = psum[(kt % 2) * 2 + jm]
                                P.mm(pm[:, 0:TB], k_[jm * 64:(jm + 1) * 64, kt * 128:(kt + 1) * 128],
                                     q_[jm * 64:(jm + 1) * 64, t0:t0 + TB], True, True, reads=[k_.b, q_.b], writes=[pm.b])

                        def rest(kt):
                            for jm in range(2):
                                pm = psum[(kt % 2) * 2 + jm]
                                p_ = pt[(kt % 2) * 2 + jm]
                                P.act(p_[:, 0:TB], pm[:, 0:TB], AF.Exp, bias=negc[:, jm:jm + 1], scale=0.125,
                                      reads=[pm.b, negc.b], writes=[p_.b])
                            for jm in range(2):
                                p_ = pt[(kt % 2) * 2 + jm]
                                P.mm(pO[jm][:, 0:TB], v_[:, kt, :], p_[:, 0:TB], kt == 0, kt == nk - 1,
                                     reads=[v_.b, p_.b], writes=[pO[jm].b])
                                P.mm(pS[jm][:, 0:TB], ones_bf[:], p_[:, 0:TB], kt == 0, kt == nk - 1,
                                     reads=[ones_bf.b, p_.b], writes=[pS[jm].b])

                        qk(0)
                        for kt in range(nk):
                            if kt + 1 < nk:
                                qk(kt + 1)
                            rest(kt)
                        P.emit("dve", lambda e, TB=TB: e.reciprocal(r0[:, 0:TB], pS[0][:, 0:TB]), reads=[pS[0].b], writes=[r0.b])
                        P.tt("dve", o0[:, 0:TB], pO[0][:, 0:TB], r0[:, 0:TB], ALU.mult, reads=[pO[0].b, r0.b], writes=[o0.b])
                        P.emit("dve", lambda e, TB=TB: e.reciprocal(r0[:, 0:TB], pS[1][:, 0:TB]), reads=[pS[1].b], writes=[r0.b])
                        P.tt("dve", o1[:, 0:TB], pO[1][:, 0:TB], r0[:, 0:TB], ALU.mult, reads=[pO[1].b, r0.b], writes=[o1.b])
                        P.stt("dve", o0[:, 0:TB], o1[:, 0:TB], neglam, o0[:, 0:TB], ALU.mult, ALU.add,
                              reads=[o1.b, o0.b, sm.b], writes=[o0.b])
                        P.act(o1[:, 0:TB], o0[:, 0:TB], AF.Square, reads=[o0.b], writes=[o1.b])
                        pm = psum[0]
                        P.mm(pm[:, 0:TB], C_(ONES), o1[:, 0:TB], True, True, reads=[cst.b, o1.b], writes=[pm.b])
                        P.act(r0[:, 0:TB], pm[:, 0:TB], AF.Ln, bias=kct[:, 1:2], scale=1.0 / 128, reads=[pm.b, kct.b], writes=[r0.b])
                        P.act(r0[:, 0:TB], r0[:, 0:TB], AF.Exp, scale=-0.5, reads=[r0.b], writes=[r0.b])
                        o_ = ob[bi % 2]
                        P.stt("dve", o_[:, 0:TB], o0[:, 0:TB], swl, r0[:, 0:TB], ALU.mult, ALU.mult,
                              reads=[o0.b, sm.b, r0.b], writes=[o_.b])
                        P.dma(oattT.ap()[h * 128:(h + 1) * 128, t0:t0 + TB], o_[:, 0:TB], reads=[o_.b],
                              writes=[DB("oattT")], eng="pool")

        def conv_phase(l):
            with phase() as sb:
                cw = sb([128, 16, 5], F32, "cw")
                cb = sb([128, 16], F32, "cb")
                P.dma(cw[:], convw.ap()[l], writes=[cw.b])
                P.dma(cb[:], convb.ap()[l], writes=[cb.b])
                xp = [sb([128, T + 8], F32, "xp") for _ in range(2)]
                acc = [sb([128, T], F32, "acc") for _ in range(2)]
                o16 = [sb([128, T], BF16, "o16") for _ in range(2)]
                tk32 = [sb([128, NCH, 128], F32, "tk32") for _ in range(1)]
                tk16 = [sb([128, NCH, 128], BF16, "tk16") for _ in range(1)]
                for x_ in xp:
                    P.memset("dve", x_[:], 0.0, writes=[x_.b])
                segs = [(0, NCTX, 2), (NCTX, NX, 6)]
                for ct in range(16):
                    x_, a_ = xp[ct % 2], acc[ct % 2]
                    for (s0, sl, off) in segs:
                        P.dma(x_[:, off + s0:off + s0 + sl], xbcT.ap()[ct * 128:(ct + 1) * 128, s0:s0 + sl],
                              reads=[DB("xbcT")], writes=[x_.b])
                    for (s0, sl, off) in segs:
                        e = "dve"
                        base = off + s0 - 2
                        P.ts(e, a_[:, s0:s0 + sl], x_[:, base:base + sl], cw[:, ct, 0:1], None, ALU.mult,
                             reads=[x_.b, cw.b], writes=[a_.b])
                        for k in range(1, 5):
                            P.stt(e, a_[:, s0:s0 + sl], x_[:, base + k:base + k + sl], cw[:, ct, k:k + 1], a_[:, s0:s0 + sl],
                                  ALU.mult, ALU.add, reads=[x_.b, cw.b, a_.b], writes=[a_.b])
                    P.act(a_[:], a_[:], AF.Silu, bias=cb[:, ct:ct + 1], reads=[a_.b, cb.b], writes=[a_.b])
                    if ct < 8:
                        P.dma(xsT.ap()[ct * 128:(ct + 1) * 128, :], a_[:], reads=[a_.b], writes=[DB("xsT")], eng="pool")
                        tk = tk32[0]
                        for c4 in range(0, NCH, 4):
                            n4 = min(4, NCH - c4)
                            pm = psum[(c4 // 4) % 4]
                            for i in range(n4):
                                P.tr(pm[:, i * 128:(i + 1) * 128], a_[:, (c4 + i) * 128:(c4 + i + 1) * 128], C_(IDENT),
                                     reads=[a_.b, cst.b], writes=[pm.b])
                            P.copy("act" if (c4 // 4) % 2 else "dve", tk[:, c4:c4 + n4, :],
                                   pm[:, 0:n4 * 128].rearrange("p (a c) -> p a c", a=n4), reads=[pm.b], writes=[tk.b])
                        P.dma(xstok.ap().rearrange("(a p) c -> p a c", p=128)[:, :, ct * 128:(ct + 1) * 128], tk[:],
                              reads=[tk.b], writes=[DB("xstok")], eng="pool")
                    else:
                        o_ = o16[ct % 2]
                        P.copy("pool", o_[:], a_[:], reads=[a_.b], writes=[o_.b])
                        if ct < 12:
                            g_ = ct - 8
                            P.dma(BTs.ap()[g_ * 128:(g_ + 1) * 128, :], o_[:], reads=[o_.b], writes=[DB("BTs")], eng="pool")
                            tk = tk16[0]
                            for c4 in range(0, NCH, 4):
                                n4 = min(4, NCH - c4)
                                pm = psum[(c4 // 4) % 4]
                                for i in range(n4):
                                    P.tr(pm[:, i * 128:(i + 1) * 128], a_[:, (c4 + i) * 128:(c4 + i + 1) * 128], C_(IDENT),
                                         reads=[a_.b, cst.b], writes=[pm.b])
                                P.copy("act" if (c4 // 4) % 2 else "dve", tk[:, c4:c4 + n4, :],
                                       pm[:, 0:n4 * 128].rearrange("p (a c) -> p a c", a=n4), reads=[pm.b], writes=[tk.b])
                            P.dma(btok.ap().rearrange("(a p) c -> p a c", p=128)[:, :, g_ * 128:(g_ + 1) * 128], tk[:],
                                  reads=[tk.b], writes=[DB("btok")], eng="pool")
                        else:
                            g_ = ct - 12
                            P.dma(CTs.ap()[g_ * 128:(g_ + 1) * 128, :], o_[:], reads=[o_.b], writes=[DB("CTs")], eng="pool")

        def ssd_phase(l, last):
            nctx_ch = NCTX // 128
            with phase() as sb:
                al_ = sb([128, 32], F32, "al_")
                db_ = sb([128, 32], F32, "db_")
                A_ = sb([128, 32], F32, "A_")
                Dt = sb([128, 8], F32, "Dt")
                nw = sb([128, 8], F32, "nw")
                P.dma(al_[:], alog.ap()[l], writes=[al_.b])
                P.dma(db_[:], dtb.ap()[l], writes=[db_.b])
                P.dma(Dt[:], ssdD.ap()[l], writes=[Dt.b])
                P.dma(nw[:], ssdnw.ap()[l], writes=[nw.b])
                P.act(A_[:], al_[:], AF.Exp, reads=[al_.b], writes=[A_.b])
                P.ts("dve", A_[:], A_[:], -1.0, None, ALU.mult, reads=[A_.b], writes=[A_.b])
                H = sb([128, 1024], F32, "H")
                Hb = sb([128, 1024], BF16, "Hb")
                xs = [sb([128, 1024], F32, "xs") for _ in range(2)]
                bt = [sb([128, 512], BF16, "bt") for _ in range(2)]
                BT = [sb([128, 4, 128], BF16, "BT") for _ in range(2)]
                CT = [sb([128, 4, 128], BF16, "CT") for _ in range(2)]
                dr = [sb([128, 32], F32, "dr") for _ in range(2)]
                sm = sb([128, 8, 32], F32, "sm")
                xdt = sb([128, 1024], BF16, "xdt")
                xde = sb([128, 1024], BF16, "xde")
                Gm = sb([128, 512], F32, "Gm")
                rseg = sb([128, 16, 128], F32, "rseg")
                dcy = [sb([128, 512], F32, "dcy") for _ in range(2)]
                ecs = [sb([128, 512], F32, "ecs") for _ in range(2)]
                Mt = [sb([128, 4, 128], BF16, "Mt") for _ in range(2)]
                Cp = [sb([128, 4, 128], BF16, "Cp") for _ in range(2)]
                ysb = sb([128, 8, 128], F32, "ysb")
                y0 = sb([128, 8, 128], F32, "y0")
                xf = sb([128, 8, 128], F32, "xf")
                zf = sb([128, 8, 128], F32, "zf")
                sq = sb([128, 128], F32, "sq")
                rst = sb([128, 4, 128], F32, "rst")
                ob = sb([128, 8, 128], BF16, "ob")
                for d in range(2):
                    if d == 0:
                        order = list(range(NCH))
                    else:
                        order = list(range(nctx_ch - 1, -1, -1)) + list(range(NCH - 1, nctx_ch - 1, -1))
                    LM = C_(TRI_LE if d == 0 else TRI_GE)
                    U = C_(SGT if d == 0 else SLT)
                    MASK = LM
                    P.memset("dve", H[:], 0.0, writes=[H.b])
                    for oi, c in enumerate(order):
                        if last and d == 1 and c < nctx_ch and False:
                            pass
                        t0 = c * 128
                        x_, b_, B_, C2, d_ = xs[oi % 2], bt[oi % 2], BT[oi % 2], CT[oi % 2], dr[oi % 2]
                        P.dma(x_[:], xstok.ap()[t0:t0 + 128, :], reads=[DB("xstok")], writes=[x_.b])
                        P.dma(b_[:], btok.ap()[t0:t0 + 128, :], reads=[DB("btok")], writes=[b_.b])
                        P.dma(B_[:], BTs.ap().rearrange("(g n) t -> n g t", g=4)[:, :, t0:t0 + 128], reads=[DB("BTs")], writes=[B_.b])
                        P.dma(C2[:], CTs.ap().rearrange("(g n) t -> n g t", g=4)[:, :, t0:t0 + 128], reads=[DB("CTs")], writes=[C2.b])
                        P.dma(d_[:], dttok.ap()[t0:t0 + 128, :], reads=[DB("dttok")], writes=[d_.b])
                        xx, ax, ex, ln_, dtp, adt, toe, w2_ = (sm[:, i, :] for i in range(8))
                        P.tt("dve", xx, d_[:], db_[:], ALU.add, reads=[d_.b, db_.b], writes=[sm.b])
                        P.stt("dve", ax, xx, -1.0, xx, ALU.mult, ALU.max, reads=[sm.b], writes=[sm.b])
                        P.act(ex, ax, AF.Exp, scale=-1.0, reads=[sm.b], writes=[sm.b])
                        P.act(ln_, ex, AF.Ln, bias=kct[:, 2:3], reads=[sm.b, kct.b], writes=[sm.b])
                        P.stt("dve", dtp, xx, 0.0, ln_, ALU.max, ALU.add, reads=[sm.b], writes=[sm.b])
                        P.tt("dve", adt, dtp, A_[:], ALU.mult, reads=[sm.b, A_.b], writes=[sm.b])
                        dsl = slice(d * 16, (d + 1) * 16)
                        pe_ = psum[0]
                        P.mm(pe_[:, 0:16], U, adt[:, dsl], True, True, reads=[cst.b, sm.b], writes=[pe_.b])
                        P.mm(pe_[:, 16:32], C_(ONES), adt[:, dsl], True, True, reads=[cst.b, sm.b], writes=[pe_.b])
                        P.act(toe, pe_[:, 0:32], AF.Exp, reads=[pe_.b], writes=[sm.b])
                        P.tt("dve", w2_[:, 0:16], dtp[:, dsl], toe[:, 0:16], ALU.mult, reads=[sm.b], writes=[sm.b])
                        x3 = x_[:].rearrange("p (h q) -> p h q", h=16)
                        P.tt("dve", xdt[:].rearrange("p (h q) -> p h q", h=16), x3,
                             dtp[:, dsl].unsqueeze(2).to_broadcast([128, 16, 64]), ALU.mult, reads=[x_.b, sm.b], writes=[xdt.b])
                        P.tt("pool", xde[:].rearrange("p (h q) -> p h q", h=16), x3,
                             w2_[:, 0:16].unsqueeze(2).to_broadcast([128, 16, 64]), ALU.mult, reads=[x_.b, sm.b], writes=[xde.b])
                        for g_ in range(4):
                            pS_ = psum[1 + g_ // 2]
                            P.mm(pS_[:, (g_ % 2) * 256:(g_ % 2 + 1) * 256], b_[:, g_ * 128:(g_ + 1) * 128],
                                 xde[:, g_ * 256:(g_ + 1) * 256], True, True, reads=[b_.b, xde.b], writes=[pS_.b])
                        P.copy("act", Hb[:], H[:], reads=[H.b], writes=[Hb.b])
                        pG = psum[3]
                        for g_ in range(4):
                            P.mm(pG[:, g_ * 128:(g_ + 1) * 128], B_[:, g_, :], C2[:, g_, :], True, True,
                                 reads=[B_.b, C2.b], writes=[pG.b])
                        P.tt("dve", Gm[:].rearrange("p (g l) -> p g l", g=4), pG[:].rearrange("p (g l) -> p g l", g=4),
                             MASK.unsqueeze(1).to_broadcast([128, 4, 128]), ALU.mult, reads=[pG.b, cst.b], writes=[Gm.b])
                        P.tt("pool", rseg[:], LM.unsqueeze(1).to_broadcast([128, 16, 128]),
                             adt[:, dsl].unsqueeze(2).to_broadcast([128, 16, 128]), ALU.mult, reads=[cst.b, sm.b], writes=[rseg.b])
                        for g_ in range(4):
                            rr_ = rseg[:, 4 * g_:4 * g_ + 4, :].rearrange("p h l -> p (h l)")
                            pseg = psum[4 + g_ % 2]
                            pcs = psum[6 + g_ % 2]
                            P.mm(pseg[:], U, rr_, True, True, reads=[cst.b, rseg.b], writes=[pseg.b])
                            P.mm(pcs[:], C_(ONES), rr_, True, True, reads=[cst.b, rseg.b], writes=[pcs.b])
                            dc, ec, M_, Cq = dcy[g_ % 2], ecs[g_ % 2], Mt[g_ % 2], Cp[g_ % 2]
                            P.act(dc[:], pseg[:], AF.Exp, reads=[pseg.b], writes=[dc.b])
                            P.act(ec[:], pcs[:], AF.Exp, reads=[pcs.b], writes=[ec.b])
                            P.tt("dve", M_[:], dc[:].rearrange("p (h l) -> p h l", h=4),
                                 Gm[:, g_ * 128:(g_ + 1) * 128].unsqueeze(1).to_broadcast([128, 4, 128]), ALU.mult,
                                 reads=[dc.b, Gm.b], writes=[M_.b])
                            P.tt("pool", Cq[:], ec[:].rearrange("p (h l) -> p h l", h=4),
                                 C2[:, g_, :].unsqueeze(1).to_broadcast([128, 4, 128]), ALU.mult,
                                 reads=[ec.b, C2.b], writes=[Cq.b])
                            pY = psum[0] if g_ < 2 else psum[3]
                            for e_ in range(4):
                                hd = g_ * 4 + e_
                                hp = hd // 2
                                jj_ = hd % 2
                                oo = pY[jj_ * 64:(jj_ + 1) * 64, (hp % 4) * 128:(hp % 4 + 1) * 128]
                                P.mm(oo, xdt[:, hd * 64:(hd + 1) * 64], M_[:, e_, :], True, False,
                                     reads=[xdt.b, M_.b], writes=[pY.b])
                                P.mm(oo, Hb[:, hd * 64:(hd + 1) * 64], Cq[:, e_, :], False, True,
                                     reads=[Hb.b, Cq.b], writes=[pY.b])
                            if g_ % 2 == 1:
                                hp0 = (g_ - 1) * 2
                                P.copy("act", ysb[:, hp0:hp0 + 4, :], pY[:].rearrange("p (a l) -> p a l", a=4),
                                       reads=[pY.b], writes=[ysb.b])
                        P.tt("dve", H[:].rearrange("p (h q) -> p h q", h=16), H[:].rearrange("p (h q) -> p h q", h=16),
                             toe[:, 16:32].unsqueeze(2).to_broadcast([128, 16, 64]), ALU.mult, reads=[H.b, sm.b], writes=[H.b])
                        P.tt("dve", H[:, 0:512], H[:, 0:512], psum[1][:], ALU.add, reads=[H.b, psum[1].b], writes=[H.b])
                        P.tt("dve", H[:, 512:1024], H[:, 512:1024], psum[2][:], ALU.add, reads=[H.b, psum[2].b], writes=[H.b])
                        yv = y0T.ap().rearrange("(a p) t -> p a t", p=128)[:, :, t0:t0 + 128]
                        if d == 0:
                            P.dma(yv, ysb[:], reads=[ysb.b], writes=[DB("y0T")], eng="pool")
                        else:
                            if last and c < nctx_ch:
                                continue
                            P.dma(y0[:], yv, reads=[DB("y0T")], writes=[y0.b])
                            P.dma(xf[:], xsT.ap().rearrange("(a p) t -> p a t", p=128)[:, :, t0:t0 + 128], reads=[DB("xsT")], writes=[xf.b])
                            P.dma(zf[:], zT.ap().rearrange("(a p) t -> p a t", p=128)[:, :, t0:t0 + 128], reads=[DB("zT")], writes=[zf.b])
                            P.tt("dve", ysb[:], ysb[:], y0[:], ALU.add, reads=[ysb.b, y0.b], writes=[ysb.b])
                            P.tt("pool", xf[:], xf[:], Dt[:].unsqueeze(2).to_broadcast([128, 8, 128]), ALU.mult,
                                 reads=[xf.b, Dt.b], writes=[xf.b])
                            P.tt("dve", ysb[:], ysb[:], xf[:], ALU.add, reads=[ysb.b, xf.b], writes=[ysb.b])
                            P.act(zf[:], zf[:], AF.Silu, reads=[zf.b], writes=[zf.b])
                            P.tt("dve", ysb[:], ysb[:], zf[:], ALU.mult, reads=[ysb.b, zf.b], writes=[ysb.b])
                            pn = psum[3]
                            for hp in range(8):
                                P.act(sq[:], ysb[:, hp, :], AF.Square, reads=[ysb.b], writes=[sq.b])
                                P.mm(pn[:, (hp // 2) * 128:(hp // 2 + 1) * 128], C_(ONES), sq[:], hp % 2 == 0, hp % 2 == 1,
                                     reads=[cst.b, sq.b], writes=[pn.b])
                            P.act(rst[:], pn[:].rearrange("p (g l) -> p g l", g=4), AF.Ln, bias=kct[:, 1:2], scale=1.0 / 256,
                                  reads=[pn.b, kct.b], writes=[rst.b])
                            P.act(rst[:], rst[:], AF.Exp, scale=-0.5, reads=[rst.b], writes=[rst.b])
                            for g_ in range(4):
                                P.tt("dve", ysb[:, 2 * g_:2 * g_ + 2, :], ysb[:, 2 * g_:2 * g_ + 2, :],
                                     rst[:, g_, :].unsqueeze(1).to_broadcast([128, 2, 128]), ALU.mult,
                                     reads=[ysb.b, rst.b], writes=[ysb.b])
                            P.tt("pool", ob[:], ysb[:], nw[:].unsqueeze(2).to_broadcast([128, 8, 128]), ALU.mult,
                                 reads=[ysb.b, nw.b], writes=[ob.b])
                            P.dma(ossdT.ap().rearrange("(a p) t -> p a t", p=128)[:, :, t0:t0 + 128], ob[:], reads=[ob.b],
                                  writes=[DB("ossdT")], eng="pool")

        def s5_phase(l, last):
            with phase() as sb:
                prm = sb([128, 2, 24, 32], F32, "prm")
                prmi = sb([128, 128], I32, "prmi")

                def sincos(y, s_out, c_out, t1_, t2_, t3_, ti_, fb, ib, eng="dve"):
                    for off, out_ in ((0.0, s_out), (0.25, c_out)):
                        P.ts(eng, t1_, y, off, None, ALU.add, reads=[fb], writes=[fb])
                        P.copy(eng, ti_, t1_, reads=[fb], writes=[ib])
                        P.copy(eng, t2_, ti_, reads=[ib], writes=[fb])
                        P.tt(eng, t1_, t1_, t2_, ALU.subtract, reads=[fb], writes=[fb])
                        P.ts(eng, t3_, t1_, 0.5, None, ALU.is_gt, reads=[fb], writes=[fb])
                        P.tt(eng, t1_, t1_, t3_, ALU.subtract, reads=[fb], writes=[fb])
                        P.act(out_, t1_, AF.Sin, scale=2 * PI, reads=[fb], writes=[fb])
                for d in range(2):
                    pv = lambda i: prm[:, d, i, :]
                    LRE, LIM, STP, MAG, ANG, T1, ABR, ABI, DEN, KRE, KIM, T2, T3, AL, RL, CL, SL, ANGN, LNR, NLNR = range(20)
                    P.dma(pv(LRE), s5lre.ap()[l, d], writes=[prm.b])
                    P.dma(pv(LIM), s5lim.ap()[l, d], writes=[prm.b])
                    P.dma(pv(STP), s5ls.ap()[l, d], writes=[prm.b])
                    P.act(pv(STP), pv(STP), AF.Exp, reads=[prm.b], writes=[prm.b])
                    P.tt("dve", pv(MAG), pv(LRE), pv(STP), ALU.mult, reads=[prm.b], writes=[prm.b])
                    P.act(pv(MAG), pv(MAG), AF.Exp, reads=[prm.b], writes=[prm.b])
                    P.tt("dve", pv(ANG), pv(LIM), pv(STP), ALU.mult, reads=[prm.b], writes=[prm.b])
                    P.ts("dve", pv(ANGN), pv(ANG), 1.0 / (2 * PI), None, ALU.mult, reads=[prm.b], writes=[prm.b])
                    sincos(pv(ANGN), pv(ABI), pv(ABR), pv(T1), pv(T2), pv(T3), prmi[:, 0:32], prm.b, prmi.b)
                    P.tt("dve", pv(ABR), pv(ABR), pv(MAG), ALU.mult, reads=[prm.b], writes=[prm.b])
                    P.tt("dve", pv(ABI), pv(ABI), pv(MAG), ALU.mult, reads=[prm.b], writes=[prm.b])
                    P.tt("dve", pv(DEN), pv(LRE), pv(LRE), ALU.mult, reads=[prm.b], writes=[prm.b])
                    P.tt("dve", pv(T1), pv(LIM), pv(LIM), ALU.mult, reads=[prm.b], writes=[prm.b])
                    P.tt("dve", pv(DEN), pv(DEN), pv(T1), ALU.add, reads=[prm.b], writes=[prm.b])
                    P.emit("dve", lambda e, d=d: e.reciprocal(prm[:, d, DEN, :], prm[:, d, DEN, :]), reads=[prm.b], writes=[prm.b])
                    P.ts("dve", pv(T2), pv(ABR), -1.0, None, ALU.add, reads=[prm.b], writes=[prm.b])
                    P.tt("dve", pv(KRE), pv(T2), pv(LRE), ALU.mult, reads=[prm.b], writes=[prm.b])
                    P.tt("dve", pv(T1), pv(ABI), pv(LIM), ALU.mult, reads=[prm.b], writes=[prm.b])
                    P.tt("dve", pv(KRE), pv(KRE), pv(T1), ALU.add, reads=[prm.b], writes=[prm.b])
                    P.tt("dve", pv(KRE), pv(KRE), pv(DEN), ALU.mult, reads=[prm.b], writes=[prm.b])
                    P.tt("dve", pv(KIM), pv(ABI), pv(LRE), ALU.mult, reads=[prm.b], writes=[prm.b])
                    P.tt("dve", pv(T1), pv(T2), pv(LIM), ALU.mult, reads=[prm.b], writes=[prm.b])
                    P.tt("dve", pv(KIM), pv(KIM), pv(T1), ALU.subtract, reads=[prm.b], writes=[prm.b])
                    P.tt("dve", pv(KIM), pv(KIM), pv(DEN), ALU.mult, reads=[prm.b], writes=[prm.b])
                    P.ts("dve", pv(T1), pv(ANGN), 128.0, None, ALU.mult, reads=[prm.b], writes=[prm.b])
                    P.copy("dve", prmi[:, 0:32], pv(T1), reads=[prm.b], writes=[prmi.b])
                    P.copy("dve", pv(T3), prmi[:, 0:32], reads=[prmi.b], writes=[prm.b])
                    P.tt("dve", pv(AL), pv(T1), pv(T3), ALU.subtract, reads=[prm.b], writes=[prm.b])
                    P.tt("dve", pv(T1), pv(LRE), pv(STP), ALU.mult, reads=[prm.b], writes=[prm.b])
                    P.act(pv(RL), pv(T1), AF.Exp, scale=128.0, reads=[prm.b], writes=[prm.b])
                    P.copy("dve", pv(LNR), pv(T1), reads=[prm.b], writes=[prm.b])
                    P.ts("dve", pv(NLNR), pv(T1), -1.0, None, ALU.mult, reads=[prm.b], writes=[prm.b])
                rsts = sb([128, T], F32, "rsts")
                P.memset("dve", rsts[:], 1.0, writes=[rsts.b])
                P.memset("dve", rsts[:].rearrange("p (c l) -> p c l", l=128)[:, :, 0:1], 0.0, writes=[rsts.b])
                cidx = sb([128, NCH], F32, "cidx")
                P.copy("dve", cidx[:], cst[:, TAU, 0:NCH], reads=[cst.b], writes=[cidx.b])
                tbs = [sb([128, 12, 128], F32, "tb_") for _ in range(2)]
                prmi2 = sb([128, 128], I32, "prmi2")
                ct_ = sb([128, 12, NCH], F32, "ct_")
                ub = sb([128, T], BF16, "ub")
                wB = [sb([128, 2, 128], BF16, "wB") for _ in range(2)]
                xo = [sb([128, T], BF16, "xo") for _ in range(2)]

                class W2:
                    def __init__(self, nm):
                        self.tl = sb([128, T], F32, nm)
                        self.bp = Buf(nm + "p")
                        self.bd = Buf(nm + "d")
                        self.all = [self.bp, self.bd]

                    def __getitem__(self, k):
                        return self.tl[k]

                R_, I_, A_, B_ = W2("R_"), W2("I_"), W2("A_"), W2("B_")
                xob = [(Buf("xo0p"), Buf("xo0d")), (Buf("xo1p"), Buf("xo1d"))]
                nctx_ch = NCTX // 128
                nxp = max(0, int(round(0.5 * (NX // 128))) - 0)
                if NX // 128 <= 4:
                    nxp = 1
                pieces = [(0, NCTX, "pool", 0, NCTX), (NCTX, NCTX + nxp * 128, "pool", NCTX, NX),
                          (NCTX + nxp * 128, T, "dve", NCTX, NX)]
                pieces = [p for p in pieces if p[1] > p[0]]

                tbh = [None]

                def bsel(w, eng):
                    return w.bp if eng == "pool" else w.bd

                def big(out, in0, tbl_i, in1, op):
                    for (c0, c1, eng, _, _) in pieces:
                        o3 = out[:, c0:c1].rearrange("p (c l) -> p c l", l=128)
                        a3 = in0[:, c0:c1].rearrange("p (c l) -> p c l", l=128)
                        if tbl_i is not None:
                            tcur = tbh[0]
                            b3 = tcur[:, tbl_i, :].unsqueeze(1).to_broadcast([128, (c1 - c0) // 128, 128])
                            P.tt(eng, o3, a3, b3, op, reads=[bsel(in0, eng), tcur.b], writes=[bsel(out, eng)])
                        else:
                            P.tt(eng, o3, a3, in1[:, c0:c1].rearrange("p (c l) -> p c l", l=128), op,
                                 reads=[bsel(in0, eng), bsel(in1, eng)], writes=[bsel(out, eng)])

                for d in range(2):
                    pv = lambda i: prm[:, d, i, :]
                    for st in range(32):
                        ct = st // 4
                        if st % 4 == 0:
                            P.dma(ub[:], uT.ap()[ct * 128:(ct + 1) * 128, :], reads=[DB("uT")], writes=[ub.b], eng="pool")
                        w_ = wB[st % 2]
                        P.dma(w_[:, 0, :], sbreb.ap()[l, d, st], writes=[w_.b])
                        P.dma(w_[:, 1, :], sbimb.ap()[l, d, st], writes=[w_.b])
                        col = lambda i: prm[:, d, i, st:st + 1]
                        tb_ = tbs[(d * 32 + st) % 2]
                        tbh[0] = tb_
                        E_ = "dve"
                        P.ts(E_, tb_[:, 0, :], C_(TAU), col(ANGN), None, ALU.mult, reads=[cst.b, prm.b], writes=[tb_.b])
                        sincos(tb_[:, 0, :], tb_[:, 3, :], tb_[:, 2, :], tb_[:, 1, :], tb_[:, 6, :], tb_[:, 7, :], prmi2[:, 0:128], tb_.b, prmi2.b, eng=E_)
                        P.act(tb_[:, 10, :], C_(TAU), AF.Exp, scale=col(LNR), reads=[cst.b, prm.b], writes=[tb_.b])
                        P.act(tb_[:, 11, :], C_(TAU), AF.Exp, scale=col(NLNR), reads=[cst.b, prm.b], writes=[tb_.b])
                        P.ts(E_, tb_[:, 4, :], tb_[:, 2, :], col(KRE), None, ALU.mult, reads=[tb_.b, prm.b], writes=[tb_.b])
                        P.ts(E_, tb_[:, 6, :], tb_[:, 3, :], col(KIM), None, ALU.mult, reads=[tb_.b, prm.b], writes=[tb_.b])
                        P.tt(E_, tb_[:, 4, :], tb_[:, 4, :], tb_[:, 6, :], ALU.add, reads=[tb_.b], writes=[tb_.b])
                        P.ts(E_, tb_[:, 5, :], tb_[:, 2, :], col(KIM), None, ALU.mult, reads=[tb_.b, prm.b], writes=[tb_.b])
                        P.ts(E_, tb_[:, 6, :], tb_[:, 3, :], col(KRE), None, ALU.mult, reads=[tb_.b, prm.b], writes=[tb_.b])
                        P.tt(E_, tb_[:, 5, :], tb_[:, 5, :], tb_[:, 6, :], ALU.subtract, reads=[tb_.b], writes=[tb_.b])
                        P.tt(E_, tb_[:, 4, :], tb_[:, 4, :], tb_[:, 11, :], ALU.mult, reads=[tb_.b], writes=[tb_.b])
                        P.tt(E_, tb_[:, 5, :], tb_[:, 5, :], tb_[:, 11, :], ALU.mult, reads=[tb_.b], writes=[tb_.b])
                        P.tt(E_, tb_[:, 8, :], tb_[:, 2, :], tb_[:, 10, :], ALU.mult, reads=[tb_.b], writes=[tb_.b])
                        P.tt(E_, tb_[:, 9, :], tb_[:, 3, :], tb_[:, 10, :], ALU.mult, reads=[tb_.b], writes=[tb_.b])
                        for (t0, TB, which) in blocks:
                            for ri, dstb in ((0, R_), (1, I_)):
                                pm = psum[(2 * (t0 // 512) + ri) % 4]
                                P.mm(pm[:, 0:TB], w_[:, ri, :], ub[:, t0:t0 + TB], True, True, reads=[w_.b, ub.b], writes=[pm.b])
                                if d == 0:
                                    P.copy("act", dstb[:, t0:t0 + TB], pm[:, 0:TB], reads=[pm.b], writes=dstb.all)
                                else:
                                    s0, sl = (0, NCTX) if which == 1 else (NCTX, NX)
                                    p0 = s0 + (sl - 1) - (t0 + TB - 1 - s0)
                                    P.copy("act", dstb[:, p0:p0 + TB], rev_last(pm[:, 0:TB]), reads=[pm.b], writes=dstb.all)
                        big(A_, R_, 4, None, ALU.mult)
                        big(B_, I_, 5, None, ALU.mult)
                        big(A_, A_, None, B_, ALU.subtract)
                        big(B_, R_, 5, None, ALU.mult)
                        big(I_, I_, 4, None, ALU.mult)
                        big(I_, I_, None, B_, ALU.add)
                        bc3 = lambda a: a[:].rearrange("p (c l) -> p c l", l=128)
                        er, ei, vr, vi, cc_, ss_, tmp, tmp2 = (ct_[:, i, :] for i in range(8))
                        wl_r = ct_[:, 8, :]
                        wl_i = ct_[:, 9, :]
                        P.emit("dve", lambda e, A_=A_: e.tensor_reduce(ct_[:, 8, :], A_[:].rearrange("p (c l) -> p c l", l=128), AX.X, ALU.add),
                               reads=A_.all, writes=[ct_.b])
                        P.emit("dve", lambda e, I_=I_: e.tensor_reduce(ct_[:, 9, :], I_[:].rearrange("p (c l) -> p c l", l=128), AX.X, ALU.add),
                               reads=I_.all, writes=[ct_.b])
                        c127 = tb_[:, 8, 127:128]
                        s127 = tb_[:, 9, 127:128]
                        P.ts("dve", er, wl_r, c127, None, ALU.mult, reads=[ct_.b, tb_.b], writes=[ct_.b])
                        P.ts("dve", tmp, wl_i, s127, None, ALU.mult, reads=[ct_.b, tb_.b], writes=[ct_.b])
                        P.tt("dve", er, er, tmp, ALU.subtract, reads=[ct_.b], writes=[ct_.b])
                        P.ts("dve", ei, wl_i, c127, None, ALU.mult, reads=[ct_.b, tb_.b], writes=[ct_.b])
                        P.ts("dve", tmp, wl_r, s127, None, ALU.mult, reads=[ct_.b, tb_.b], writes=[ct_.b])
                        P.tt("dve", ei, ei, tmp, ALU.add, reads=[ct_.b], writes=[ct_.b])
                        P.ts("dve", tmp, cidx[:], col(AL), None, ALU.mult, reads=[cidx.b, prm.b], writes=[ct_.b])
                        sincos(tmp, ss_, cc_, tmp2, ct_[:, 10, :], ct_[:, 11, :], prmi[:, 0:NCH], ct_.b, prmi.b)
                        P.tt("dve", vr, er, cc_, ALU.mult, reads=[ct_.b], writes=[ct_.b])
                        P.tt("dve", tmp, ei, ss_, ALU.mult, reads=[ct_.b], writes=[ct_.b])
                        P.tt("dve", vr, vr, tmp, ALU.add, reads=[ct_.b], writes=[ct_.b])
                        P.tt("dve", vi, ei, cc_, ALU.mult, reads=[ct_.b], writes=[ct_.b])
                        P.tt("dve", tmp, er, ss_, ALU.mult, reads=[ct_.b], writes=[ct_.b])
                        P.tt("dve", vi, vi, tmp, ALU.subtract, reads=[ct_.b], writes=[ct_.b])
                        P.ts("dve", tmp2, cidx[:], 0.0, col(RL), ALU.mult, ALU.add, reads=[cidx.b, prm.b], writes=[ct_.b])
                        P.scan("dve", er, tmp2, vr, 0.0, reads=[ct_.b], writes=[ct_.b])
                        P.scan("dve", ei, tmp2, vi, 0.0, reads=[ct_.b], writes=[ct_.b])
                        P.tt("dve", vr, er, cc_, ALU.mult, reads=[ct_.b], writes=[ct_.b])
                        P.tt("dve", tmp, ei, ss_, ALU.mult, reads=[ct_.b], writes=[ct_.b])
                        P.tt("dve", vr, vr, tmp, ALU.subtract, reads=[ct_.b], writes=[ct_.b])
                        P.tt("dve", vi, ei, cc_, ALU.mult, reads=[ct_.b], writes=[ct_.b])
                        P.tt("dve", tmp, er, ss_, ALU.mult, reads=[ct_.b], writes=[ct_.b])
                        P.tt("dve", vi, vi, tmp, ALU.add, reads=[ct_.b], writes=[ct_.b])
                        P.ts("dve", er, vr, col(ABR), None, ALU.mult, reads=[ct_.b, prm.b], writes=[ct_.b])
                        P.ts("dve", tmp, vi, col(ABI), None, ALU.mult, reads=[ct_.b, prm.b], writes=[ct_.b])
                        P.tt("dve", er, er, tmp, ALU.subtract, reads=[ct_.b], writes=[ct_.b])
                        P.ts("dve", ei, vi, col(ABR), None, ALU.mult, reads=[ct_.b, prm.b], writes=[ct_.b])
                        P.ts("dve", tmp, vr, col(ABI), None, ALU.mult, reads=[ct_.b, prm.b], writes=[ct_.b])
                        P.tt("dve", ei, ei, tmp, ALU.add, reads=[ct_.b], writes=[ct_.b])
                        P.tt("dve", bc3(A_)[:, 1:NCH, 0], bc3(A_)[:, 1:NCH, 0], er[:, 0:NCH - 1], ALU.add, reads=A_.all + [ct_.b], writes=A_.all)
                        P.tt("dve", bc3(I_)[:, 1:NCH, 0], bc3(I_)[:, 1:NCH, 0], ei[:, 0:NCH - 1], ALU.add, reads=I_.all + [ct_.b], writes=I_.all)
                        P.scan("dve", R_[:], rsts[:], A_[:], 0.0, reads=[rsts.b] + A_.all, writes=R_.all)
                        P.scan("dve", B_[:], rsts[:], I_[:], 0.0, reads=[rsts.b] + I_.all, writes=B_.all)
                        for ri in range(2):
                            if ri == 0:
                                big(A_, R_, 8, None, ALU.mult)
                                big(I_, B_, 9, None, ALU.mult)
                                op = ALU.subtract
                            else:
                                big(A_, B_, 8, None, ALU.mult)
                                big(I_, R_, 9, None, ALU.mult)
                                op = ALU.add
                            xo_ = xo[ri]
                            for (c0, c1, eng, s0, sl) in pieces:
                                xb = xob[ri][0 if eng == "pool" else 1]
                                if d == 0:
                                    oo = xo_[:, c0:c1]
                                else:
                                    n0 = 2 * s0 + sl - c1
                                    oo = rev_last(xo_[:, n0:n0 + (c1 - c0)])
                                P.tt(eng, oo, A_[:, c0:c1], I_[:, c0:c1], op, reads=[bsel(A_, eng), bsel(I_, eng)], writes=[xb])
                            P.dma(XS.ap()[d, ri, st * 128:(st + 1) * 128, :], xo_[:], reads=list(xob[ri]), writes=[DB("XS")], eng="pool")
            with phase() as sb:
                sd = sb([128, 8], F32, "sd")
                gb = sb([128, 8], F32, "gb")
                P.dma(sd[:], s5d.ap()[l], writes=[sd.b])
                P.dma(gb[:], glub.ap()[l], writes=[gb.b])
                cw_ = sb([128, 2, 2, 32, 128], BF16, "cw_")
                for d in range(2):
                    P.dma(cw_[:, d, 0], screb.ap()[l, d].rearrange("s p c -> p s c"), writes=[cw_.b])
                    P.dma(cw_[:, d, 1], scimb.ap()[l, d].rearrange("s p c -> p s c"), writes=[cw_.b])
                gw = sb([128, 8, 1024], BF16, "gw")
                P.dma(gw[:], glub16.ap()[l].rearrange("(a p) c -> p a c", p=128), writes=[gw.b])
                xs_ = [sb([128, 16, 512], BF16, "xs_") for _ in range(2)]
                uu = [sb([128, 512], F32, "uu") for _ in range(2)]
                tt32 = sb([128, 8, 512], F32, "tt32")
                tt16 = sb([128, 8, 512], BF16, "tt16")
                sg = [sb([128, 512], F32, "sg") for _ in range(2)]
                og = [sb([128, 8, 512], BF16, "og") for _ in range(1)]
                for (t0, TB, which) in blocks:
                    if last and which == 1:
                        continue
                    for ct in range(8):
                        x_ = xs_[ct % 2]
                        u_ = uu[ct % 2]
                        for d in range(2):
                            for ri in range(2):
                                P.dma(x_[:, (d * 2 + ri) * 4:(d * 2 + ri) * 4 + 4, 0:TB],
                                      XS.ap()[d, ri, ct * 512:(ct + 1) * 512, t0:t0 + TB].rearrange("(a p) t -> p a t", p=128),
                                      reads=[DB("XS")], writes=[x_.b])
                        P.dma(u_[:, 0:TB], uT.ap()[ct * 128:(ct + 1) * 128, t0:t0 + TB], reads=[DB("uT")], writes=[u_.b])
                        pm = psum[ct % 4]
                        n = 0
                        for d in range(2):
                            for ri in range(2):
                                for s4 in range(4):
                                    P.mm(pm[:, 0:TB], cw_[:, d, ri, ct * 4 + s4, :], x_[:, (d * 2 + ri) * 4 + s4, 0:TB], n == 0, n == 15,
                                         reads=[cw_.b, x_.b], writes=[pm.b])
                                    n += 1
                        P.stt("dve", tt32[:, ct, 0:TB], u_[:, 0:TB], sd[:, ct:ct + 1], pm[:, 0:TB], ALU.mult, ALU.add,
                              reads=[u_.b, sd.b, pm.b], writes=[tt32.b])
                        P.act(tt32[:, ct, 0:TB], tt32[:, ct, 0:TB], AF.Gelu, reads=[tt32.b], writes=[tt32.b])
                        P.copy("pool", tt16[:, ct, 0:TB], tt32[:, ct, 0:TB], reads=[tt32.b], writes=[tt16.b])
                    o_ = og[0]
                    for m in range(8):
                        pm = psum[4 + m % 4]
                        for kc in range(8):
                            P.mm(pm[:, 0:TB], gw[:, kc, m * 128:(m + 1) * 128], tt16[:, kc, 0:TB], kc == 0, kc == 7,
                                 reads=[gw.b, tt16.b], writes=[pm.b])
                        s_ = sg[m % 2]
                        P.act(s_[:, 0:TB], pm[:, 0:TB], AF.Sigmoid, bias=gb[:, m:m + 1], reads=[pm.b, gb.b], writes=[s_.b])
                        P.tt("dve", o_[:, m, 0:TB], tt32[:, m, 0:TB], s_[:, 0:TB], ALU.mult, reads=[tt32.b, s_.b], writes=[o_.b])
                    P.dma(os5T.ap().rearrange("(a p) t -> p a t", p=128)[:, :, t0:t0 + TB], o_[:, :, 0:TB], reads=[o_.b],
                          writes=[DB("os5T")], eng="pool")

        def merge_phase(l, last):

            brs = (oattT, ossdT, os5T)
            with phase() as sb:
                hx = sb([128, KC, 512], F32, "hx")
                brt = [sb([128, 8, 512], BF16, "brt") for _ in range(3)]
                wbp = [sb([128, 3, 8, 128], BF16, "wbp") for _ in range(2)]
                wop = [sb([128, KC, 128], BF16, "wop") for _ in range(2)]
                gl = [sb([128, 3, 512], F32, "gl") for _ in range(2)]
                acc = sb([128, 512], F32, "acc")
                tm = sb([128, 512], F32, "tm")
                mixed = sb([128, KC, 512], BF16, "mixed")
                lt = ln_tmps(sb)
                for (t0, TB, which) in blocks:
                    if last and which == 1:
                        continue
                    P.dma(hx[:, :, 0:TB], hview(hT, t0, TB), reads=[DB("hT")], writes=[hx.b])
                    for j in range(3):
                        P.dma(brt[j][:, :, 0:TB], brs[j].ap().rearrange("(a p) t -> p a t", p=128)[:, :, t0:t0 + TB],
                              reads=[DB(tname[id(brs[j])])], writes=[brt[j].b])
                    for kc in range(KC):
                        P.ts(P.ve(), hx[:, kc, 0:TB], hx[:, kc, 0:TB], ALPHA, None, ALU.mult, reads=[hx.b], writes=[hx.b])
                    for m in range(KC):
                        w = wbp[m % 2]
                        g_ = gl[m % 2]
                        for j in range(3):
                            P.dma(w[:, j], wbrb.ap()[l, j].rearrange("(a p) c -> p a c", p=128)[:, :, m * 128:(m + 1) * 128], writes=[w.b])
                        P.dma(g_[:, :, 0:TB], gT.ap().rearrange("(j a p) t -> p j a t", j=3, p=128)[:, :, m, t0:t0 + TB],
                              reads=[DB("gT")], writes=[g_.b])
                        P.act(g_[:, :, 0:TB], g_[:, :, 0:TB], AF.Sigmoid, reads=[g_.b], writes=[g_.b])
                        for j in range(3):
                            pm = psum[(m * 3 + j) % 4]
                            for kc in range(8):
                                P.mm(pm[:, 0:TB], w[:, j, kc, :], brt[j][:, kc, 0:TB], kc == 0, kc == 7,
                                     reads=[w.b, brt[j].b], writes=[pm.b])
                            if j == 0:
                                P.tt("dve", acc[:, 0:TB], pm[:, 0:TB], g_[:, 0, 0:TB], ALU.mult, reads=[pm.b, g_.b], writes=[acc.b])
                            else:
                                P.tt("dve", tm[:, 0:TB], pm[:, 0:TB], g_[:, j, 0:TB], ALU.mult, reads=[pm.b, g_.b], writes=[tm.b])
                                if j == 1:
                                    P.tt("dve", acc[:, 0:TB], acc[:, 0:TB], tm[:, 0:TB], ALU.add, reads=[acc.b, tm.b], writes=[acc.b])
                                else:
                                    P.tt("dve", mixed[:, m, 0:TB], acc[:, 0:TB], tm[:, 0:TB], ALU.add, reads=[acc.b, tm.b], writes=[mixed.b])
                    for m in range(KC):
                        w = wop[m % 2]
                        load_wpanel(w, woutb.ap()[l], KC, m * 128, 128)
                        py = psum[4 + m % 2]
                        for kc in range(KC):
                            P.mm(py[:, 0:TB], w[:, kc, :], mixed[:, kc, 0:TB], kc == 0, kc == KC - 1, reads=[w.b, mixed.b], writes=[py.b])
                        P.stt("dve", hx[:, m, 0:TB], py[:, 0:TB], modg[:, 1, m, which:which + 1], hx[:, m, 0:TB],
                              ALU.mult, ALU.add, reads=[py.b, modg.b, hx.b], writes=[hx.b])
                    layer_norm(lt, hx, TB, l, 1, psum[6], psum[7])
                    P.dma(hview(hT, t0, TB), hx[:, :, 0:TB], reads=[hx.b], writes=[DB("hT")], eng="pool")

        negpi = mk_sb(es, [128, 1], F32, "negpi")
        P.memset("dve", negpi[:], -PI, writes=[negpi.b])
        kct = mk_sb(es, [128, 4], F32, "kct")
        P.memset("dve", kct[:, 0:1], 1e-5, writes=[kct.b])
        P.memset("dve", kct[:, 1:2], 1e-6, writes=[kct.b])
        P.memset("dve", kct[:, 2:3], 1.0, writes=[kct.b])
        stages = dbg.get("stages") if dbg else None

        def on(name, l):
            return stages is None or (name, l) in stages or name in stages

        for l in range(DEPTH):
            last = (l == DEPTH - 1)
            if on("mod", l):
                P.phase_name = 'mod_phase' + str(l)
                mod_phase(l)
            if on("ffn1", l):
                P.phase_name = 'ffn1_' + str(l)
                ffn_phase(l, 0, xT if l == 0 else hT, hT, False, False)
            if on("inproj", l):
                P.phase_name = 'inproj_phase' + str(l)
                inproj_phase(l)
            if on("att", l):
                P.phase_name = 'att_phase' + str(l)
                att_phase(l, last)
            if on("conv", l):
                P.phase_name = 'conv_phase' + str(l)
                conv_phase(l)
            if on("ssd", l):
                P.phase_name = 'ssd_phase' + str(l)
                ssd_phase(l, last)
            if on("s5", l):
                P.phase_name = 's5_phase' + str(l)
                s5_phase(l, last)
            if on("merge", l):
                P.phase_name = 'merge_phase' + str(l)
                merge_phase(l, last)
            if on("ffn2", l):
                P.phase_name = 'ffn2_' + str(l)
                ffn_phase(l, 1, hT, hT, last, last)
        if dbg and dbg.get("dump"):
            for nm in dbg["dump"]:
                s_ = scr_map[nm]
                shp = list(s_.ap().shape)
                o = nc.dram_tensor("dbg_" + nm, shp, s_.ap().dtype, kind="ExternalOutput")
                rows = shp[0]
                for r0 in range(0, rows, 128):
                    with ExitStack() as e2:
                        tl = mk_sb(e2, [128, shp[1]], s_.ap().dtype, "dbgt")
                        P.dma(tl[:], s_.ap()[r0:r0 + 128, :], reads=[DB(nm)], writes=[tl.b])
                        P.dma(o.ap()[r0:r0 + 128, :], tl[:], reads=[tl.b], eng="pool", is_out=True)
                        P.barrier()
        P.finish()
    return nc


def _consts(NX):
    c = np.zeros((128, 10, 128), np.float32)
    i = np.arange(128)
    c[:, 0] = np.eye(128)
    c[:, 1] = 1.0
    c[:, 2] = (i[:, None] <= i[None, :])
    c[:, 3] = (i[:, None] >= i[None, :])
    c[:, 4] = (i[:, None] > i[None, :])
    c[:, 5] = (i[:, None] < i[None, :])
    pm = np.zeros((128, 128), np.float32)
    for d in range(128):
        dd = d % 64
        half = (dd % 32) // 16
        partner = d + 16 if half == 0 else d - 16
        pm[partner, d] = 1.0
    c[:, 6] = pm
    c[:, 7] = (i[:, None] < 64) * 1.0
    c[:, 8] = (i[:, None] >= 64) * 1.0
    c[:, 9] = i[None, :].astype(np.float32)
    t = np.arange(NX)
    row = (t // 64).astype(np.float32)
    colp = (t % 64).astype(np.float32)
    inv = (10000.0 ** (-np.arange(16, dtype=np.float32) / 16)).astype(np.float32)
    rc = np.zeros((128, NX), np.float32)
    rs = np.zeros((128, NX), np.float32)
    for d in range(128):
        dd = d % 64
        axis = dd // 32
        half = (dd % 32) // 16
        f = dd % 16
        pos = row if axis == 0 else colp
        ang = (pos * inv[f]).astype(np.float32)
        rc[d] = np.cos(ang)
        rs[d] = np.sin(ang) * (-1.0 if half == 0 else 1.0)
    return c, rc, rs


def prep_shared(inp, NX):
    f = lambda a: np.ascontiguousarray(np.asarray(a, dtype=np.float32))
    out = {}
    c, rc, rs = _consts(NX)
    out["consts"], out["ropec"], out["ropes"] = c, rc, rs
    out["w_mod"] = f(inp["w_mod"])
    out["bmod"] = f(np.asarray(inp["b_mod"]).reshape(2, 144, 128).transpose(0, 2, 1))
    out["lng"] = f(np.asarray(inp["ln_g"]).reshape(2, 3, 16, 128).transpose(0, 3, 1, 2))
    out["lnb"] = f(np.asarray(inp["ln_b"]).reshape(2, 3, 16, 128).transpose(0, 3, 1, 2))
    for k in ("ffn_w1", "ffn_w3", "ffn_w2", "w_in", "glu_w", "w_branch", "w_out"):
        out[k] = f(inp["s5_glu_w"] if k == "glu_w" else inp[k])
    out["attlam"] = f(np.broadcast_to(np.asarray(inp["att_lam"]).reshape(2, 1, 256), (2, 128, 256)))
    out["subln"] = f(np.asarray(inp["att_subln"]).reshape(2, 128, 1))
    out["convw"] = f(np.asarray(inp["ssd_conv_w"]).reshape(2, 5, 16, 128).transpose(0, 3, 2, 1))
    out["convb"] = f(np.asarray(inp["ssd_conv_b"]).reshape(2, 16, 128).transpose(0, 2, 1))
    out["alog"] = f(np.broadcast_to(np.asarray(inp["ssd_a_log"]).reshape(2, 1, 32), (2, 128, 32)))
    out["dtb"] = f(np.broadcast_to(np.asarray(inp["ssd_dt_bias"]).reshape(2, 1, 32), (2, 128, 32)))
    sd = np.asarray(inp["ssd_d"]).reshape(2, 8, 2)
    out["ssdD"] = f(np.repeat(sd, 64, axis=2).transpose(0, 2, 1))
    out["ssdnw"] = f(np.asarray(inp["ssd_norm"]).reshape(2, 8, 128).transpose(0, 2, 1))
    out["s5lre"] = f(np.asarray(inp["s5_lam_re"]).reshape(2, 2, 32, 128).transpose(0, 1, 3, 2))
    out["s5lim"] = f(np.asarray(inp["s5_lam_im"]).reshape(2, 2, 32, 128).transpose(0, 1, 3, 2))
    ls = np.repeat(np.asarray(inp["s5_log_step"]).reshape(2, 2, 64, 1), 64, axis=3)
    out["s5ls"] = f(ls.reshape(2, 2, 32, 128).transpose(0, 1, 3, 2))
    bre = np.asarray(inp["s5_b_re"]); bim = np.asarray(inp["s5_b_im"])
    cre = np.asarray(inp["s5_c_re"]); cim = np.asarray(inp["s5_c_im"])
    Bre = np.zeros((2, 2, 32, 128, 128), np.float32); Bim = np.zeros_like(Bre)
    Cre = np.zeros_like(Bre); Cim = np.zeros_like(Bre)
    for st in range(32):
        for g2 in range(2):
            g = 2 * st + g2
            gl = g % 8
            Bre[:, :, st, gl * 16:(gl + 1) * 16, g2 * 64:(g2 + 1) * 64] = bre[:, :, g].transpose(0, 1, 3, 2)
            Bim[:, :, st, gl * 16:(gl + 1) * 16, g2 * 64:(g2 + 1) * 64] = bim[:, :, g].transpose(0, 1, 3, 2)
            Cre[:, :, st, g2 * 64:(g2 + 1) * 64, gl * 16:(gl + 1) * 16] = cre[:, :, g].transpose(0, 1, 3, 2)
            Cim[:, :, st, g2 * 64:(g2 + 1) * 64, gl * 16:(gl + 1) * 16] = cim[:, :, g].transpose(0, 1, 3, 2)
    out["s5bre"], out["s5bim"], out["s5cre"], out["s5cim"] = Bre, Bim, Cre, Cim
    out["s5d"] = f(np.asarray(inp["s5_d"]).reshape(2, 8, 128).transpose(0, 2, 1))
    out["glub"] = f(np.asarray(inp["s5_glu_b"]).reshape(2, 8, 128).transpose(0, 2, 1))
    return out


def prep_core(inp, b):
    x = np.asarray(inp["x"][b], dtype=np.float32)
    ctx = np.asarray(inp["ctx"][b], dtype=np.float32)
    xT = np.ascontiguousarray(np.concatenate([ctx, x], axis=0).T)
    cv = np.stack([np.asarray(inp["c"][b]).reshape(16, 128).T, np.asarray(inp["c_ctx"]).reshape(16, 128).T], axis=2)
    return {"xT": xT, "cvec": np.ascontiguousarray(cv.astype(np.float32))}


def kernel(**inputs):
    NX = int(np.asarray(inputs["x"]).shape[1])
    B = int(np.asarray(inputs["x"]).shape[0])
    shared = prep_shared(inputs, NX)
    nc = build(NX)
    in_maps = []
    for core in range(8):
        m = dict(shared)
        m.update(prep_core(inputs, core % B))
        in_maps.append(m)
    res = run_bass_kernel_spmd(nc, in_maps, core_ids=list(range(8)))
    out = np.stack([np.ascontiguousarray(res.results[b]["yT"].T) for b in range(B)], axis=0)
    return out.astype(np.float32)
```

```python
import math
from contextlib import ExitStack, contextmanager
import numpy as np
import concourse.bass as bass
import concourse.mybir as mybir
from concourse.bass_utils import run_bass_kernel_spmd
from concourse.ap import AP

F32 = mybir.dt.float32
BF16 = mybir.dt.bfloat16
I32 = mybir.dt.int32
AF = mybir.ActivationFunctionType
ALU = mybir.AluOpType
AX = mybir.AxisListType

EPOCH = 16000
NDS = 48
D = 2048
KC = 16
DFF = 5632
FC = 44
NCTX = 256
INC = 13344
DEPTH = 2
ALPHA = (2 * DEPTH) ** 0.25
PI = math.pi


class Buf:
    __slots__ = ("name", "w", "r")

    def __init__(self, name=""):
        self.name = name
        self.w = None
        self.r = {}


class TL:
    def __init__(self, t, b):
        self.t = t
        self.b = b

    def __getitem__(self, k):
        return self.t[k]


class Prog:
    ENGS = ("pe", "act", "dve", "pool", "sp")

    def __init__(self, nc, es):
        self.nc = nc
        self.es = es
        self.q = {e: [] for e in self.ENGS}
        self.cnt = {e: 0 for e in self.ENGS}
        self.csem = {e: [] for e in self.ENGS}
        self.waited = {e: {} for e in self.ENGS}
        self.dsems = [es.enter_context(nc.semaphore(f"dq{i}")) for i in range(NDS)]
        self.dcnt = [0] * NDS
        self.dlast = [None] * NDS
        self.dnext = 0
        self.bufs = {}
        self.out_events = []
        self.rr = 0
        self.phase_name = "init"
        self.scopes = False

    def buf(self, key):
        b = self.bufs.get(key)
        if b is None:
            b = self.bufs[key] = Buf(str(key))
        return b

    def _csem(self, e, ep):
        while len(self.csem[e]) <= ep:
            self.csem[e].append(
                self.es.enter_context(self.nc.semaphore(f"c{e}{len(self.csem[e])}")))
        return self.csem[e][ep]

    def emit(self, eng, fn, reads=(), writes=(), dma=False, is_out=False):
        deps = {}

        def add(ev):
            if ev is None:
                return
            k = ev[0]
            if k not in deps or deps[k][2] < ev[2]:
                deps[k] = ev

        for b in reads:
            add(b.w)
        for b in writes:
            add(b.w)
            for ev in b.r.values():
                add(ev)
        if dma:
            i = self.dnext
            self.dnext = (self.dnext + 1) % NDS
            add(self.dlast[i])
            self.dcnt[i] += 1
            ev = (("d", i), self.dsems[i], 16 * self.dcnt[i])
            self.dlast[i] = ev
            inc = 16
        else:
            self.cnt[eng] += 1
            ep = (self.cnt[eng] - 1) // EPOCH
            ev = ((eng, ep), self._csem(eng, ep), self.cnt[eng] - ep * EPOCH)
            inc = 1
        waits = []
        wd = self.waited[eng]
        for k, (_, s, v) in deps.items():
            if eng == "pe" and k[0] == "pe":
                continue
            if wd.get(k, 0) >= v:
                continue
            wd[k] = v
            waits.append((s, v))
        self.q[eng].append((waits, fn, ev[1], inc, self.phase_name))
        for b in writes:
            b.w = ev
            b.r = {}
        for b in reads:
            k = ev[0]
            if k not in b.r or b.r[k][2] < ev[2]:
                b.r[k] = ev
        if is_out:
            self.out_events.append(ev)
        return ev

    def barrier(self):
        evs = []
        for e in self.ENGS:
            c = self.cnt[e]
            if c > 0:
                ep = (c - 1) // EPOCH
                evs.append(((e, ep), self._csem(e, ep), c - ep * EPOCH))
        for i in range(NDS):
            if self.dlast[i] is not None:
                evs.append(self.dlast[i])
        for e in self.ENGS:
            waits = []
            wd = self.waited[e]
            for (k, s, v) in evs:
                if k[0] == e:
                    continue
                if wd.get(k, 0) >= v:
                    continue
                wd[k] = v
                waits.append((s, v))
            if waits:
                self.q[e].append((waits, None, None, 0, self.phase_name))

    def finish(self):
        self.barrier()
        with self.nc.Block() as block:
            def mk(e):
                def f(eo):
                    cur = None
                    cm = None
                    for waits, fn, sem, inc, ph in self.q[e]:
                        if self.scopes and ph != cur:
                            if cm is not None:
                                cm.__exit__(None, None, None)
                            cm = self.nc.named_scope(ph)
                            cm.__enter__()
                            cur = ph
                        for (s, v) in waits:
                            eo.wait_ge(s, v)
                        if fn is not None:
                            fn(eo).then_inc(sem, inc)
                    if cm is not None:
                        cm.__exit__(None, None, None)
                return f
            block.tensor(mk("pe"))
            block.scalar(mk("act"))
            block.vector(mk("dve"))
            block.gpsimd(mk("pool"))
            block.sync(mk("sp"))

    def dma(self, out, in_, reads=(), writes=(), eng="sp", is_out=False):
        return self.emit(eng, lambda e: e.dma_start(out=out, in_=in_), reads, writes,
                         dma=True, is_out=is_out)

    def mm(self, out, lhsT, rhs, start, stop, reads=(), writes=()):
        return self.emit("pe", lambda e: e.matmul(out, lhsT, rhs, start=start, stop=stop),
                         reads, writes)

    def tr(self, out, in_, ident, reads=(), writes=()):
        return self.emit("pe", lambda e: e.transpose(out, in_, ident), reads, writes)

    def act(self, out, in_, func, bias=0.0, scale=1.0, reads=(), writes=()):
        return self.emit("act", lambda e: e.activation(out, in_, func, bias=bias, scale=scale),
                         reads, writes)

    def tt(self, eng, out, in0, in1, op, reads=(), writes=()):
        return self.emit(eng, lambda e: e.tensor_tensor(out, in0, in1, op), reads, writes)

    def ts(self, eng, out, in0, s1, s2, op0, op1=None, reads=(), writes=()):
        if op1 is None:
            return self.emit(eng, lambda e: e.tensor_scalar(out, in0, s1, None, op0), reads, writes)
        return self.emit(eng, lambda e: e.tensor_scalar(out, in0, s1, s2, op0, op1), reads, writes)

    def stt(self, eng, out, in0, scalar, in1, op0, op1, reads=(), writes=()):
        return self.emit(eng, lambda e: e.scalar_tensor_tensor(out, in0, scalar, in1, op0, op1),
                         reads, writes)

    def copy(self, eng, out, in_, reads=(), writes=()):
        if eng == "act":
            return self.emit(eng, lambda e: e.copy(out, in_), reads, writes)
        return self.emit(eng, lambda e: e.tensor_copy(out, in_), reads, writes)

    def memset(self, eng, ap, val, writes=()):
        return self.emit(eng, lambda e: e.memset(ap, val), (), writes)

    def scan(self, eng, out, d0, d1, init, reads=(), writes=()):
        return self.emit(eng, lambda e: e.tensor_tensor_scan(out, d0, d1, init, ALU.mult, ALU.add),
                         reads, writes)

    def ve(self):
        self.rr += 1
        return ("dve", "pool")[self.rr % 2]


def rev_last(a):
    ap = [list(x) for x in a.ap]
    st, n = ap[-1]
    ap[-1] = [-st, n]
    return AP(a.tensor, a.offset + (n - 1) * st, ap)


def build(NX=4096, dbg=None):
    T = NCTX + NX
    NCH = T // 128
    blocks = [(0, 256, 1)] + [(NCTX + 512 * i, 512, 0) for i in range(NX // 512)]
    nc = bass.Bass("TRN2", target_bir_lowering=False)
    din = {}

    tname = {}

    def inp(name, shape, dt=F32):
        din[name] = nc.dram_tensor(name, list(shape), dt, kind="ExternalInput")
        tname[id(din[name])] = name
        return din[name]

    xT = inp("xT", [D, T])
    cvec = inp("cvec", [128, KC, 2])
    w_mod = inp("w_mod", [2, D, 9 * D])
    bmod = inp("bmod", [2, 128, 144])
    lng = inp("lng", [2, 128, 3, KC])
    lnb = inp("lnb", [2, 128, 3, KC])
    ffn_w1 = inp("ffn_w1", [2, 2, D, DFF])
    ffn_w3 = inp("ffn_w3", [2, 2, D, DFF])
    ffn_w2 = inp("ffn_w2", [2, 2, DFF, D])
    w_in = inp("w_in", [2, D, INC])
    attlam = inp("attlam", [2, 128, 256])
    subln = inp("subln", [2, 128, 1])
    convw = inp("convw", [2, 128, 16, 5])
    convb = inp("convb", [2, 128, 16])
    alog = inp("alog", [2, 128, 32])
    dtb = inp("dtb", [2, 128, 32])
    ssdD = inp("ssdD", [2, 128, 8])
    ssdnw = inp("ssdnw", [2, 128, 8])
    s5lre = inp("s5lre", [2, 2, 128, 32])
    s5lim = inp("s5lim", [2, 2, 128, 32])
    s5ls = inp("s5ls", [2, 2, 128, 32])
    s5bre = inp("s5bre", [2, 2, 32, 128, 128])
    s5bim = inp("s5bim", [2, 2, 32, 128, 128])
    s5cre = inp("s5cre", [2, 2, 32, 128, 128])
    s5cim = inp("s5cim", [2, 2, 32, 128, 128])
    s5d = inp("s5d", [2, 128, 8])
    glu_w = inp("glu_w", [2, 1024, 1024])
    glub = inp("glub", [2, 128, 8])
    w_branch = inp("w_branch", [2, 3, 1024, D])
    w_out = inp("w_out", [2, D, D])
    consts = inp("consts", [128, 10, 128])
    ropec = inp("ropec", [128, NX])
    ropes = inp("ropes", [128, NX])
    yT = nc.dram_tensor("yT", [D, NX], F32, kind="ExternalOutput")

    def scr(name, shape, dt):
        t_ = nc.dram_tensor(name, list(shape), dt, kind="Internal")
        tname[id(t_)] = name
        return t_

    w1b = scr("w1b", [2, 2, D, DFF], BF16)
    w3b = scr("w3b", [2, 2, D, DFF], BF16)
    w2b = scr("w2b", [2, 2, DFF, D], BF16)
    winb = scr("winb", [2, D, INC], BF16)
    wbrb = scr("wbrb", [2, 3, 1024, D], BF16)
    woutb = scr("woutb", [2, D, D], BF16)
    glub16 = scr("glub16", [2, 1024, 1024], BF16)
    sbreb = scr("sbreb", [2, 2, 32, 128, 128], BF16)
    sbimb = scr("sbimb", [2, 2, 32, 128, 128], BF16)
    screb = scr("screb", [2, 2, 32, 128, 128], BF16)
    scimb = scr("scimb", [2, 2, 32, 128, 128], BF16)
    hT = scr("hT", [D, T], F32)
    qT = scr("qT", [1024, T], BF16)
    kT = scr("kT", [1024, T], BF16)
    vtok = scr("vtok", [T, 1024], BF16)
    zT = scr("zT", [1024, T], F32)
    xbcT = scr("xbcT", [2048, T], F32)
    dttok = scr("dttok", [T, 32], F32)
    uT = scr("uT", [1024, T], F32)
    gT = scr("gT", [3 * D, T], F32)
    oattT = scr("oattT", [1024, T], BF16)
    xsT = scr("xsT", [1024, T], F32)
    xstok = scr("xstok", [T, 1024], F32)
    btok = scr("btok", [T, 512], BF16)
    BTs = scr("BTs", [512, T], BF16)
    CTs = scr("CTs", [512, T], BF16)
    y0T = scr("y0T", [1024, T], F32)
    ossdT = scr("ossdT", [1024, T], BF16)
    XS = scr("XS", [2, 2, 4096, T], BF16)
    os5T = scr("os5T", [1024, T], BF16)
    dbg_out = None
    scr_map = dict(hT=hT, qT=qT, kT=kT, vtok=vtok, zT=zT, xbcT=xbcT, dttok=dttok, uT=uT, gT=gT,
                   oattT=oattT, xsT=xsT, xstok=xstok, btok=btok, BTs=BTs, CTs=CTs, y0T=y0T,
                   ossdT=ossdT, os5T=os5T)

    with ExitStack() as es:
        P = Prog(nc, es)
        P.scopes = bool(dbg and dbg.get("scopes"))
        uid = [0]

        def mk_sb(stack, shape, dt=F32, name=None):
            uid[0] += 1
            n = f"{name or 't'}{uid[0]}"
            t = stack.enter_context(nc.sbuf_tensor(n, list(shape), dt))
            return TL(t, Buf(n))

        psum = []
        for i in range(8):
            t = es.enter_context(nc.psum_tensor(f"ps{i}", [128, 512], F32))
            psum.append(TL(t, Buf(f"ps{i}")))

        DB = P.buf

        @contextmanager
        def phase():
            with ExitStack() as ph:
                yield lambda shape, dt=F32, name=None: mk_sb(ph, shape, dt, name)
                P.barrier()

        cst = mk_sb(es, [128, 10, 128], F32, "cst")
        P.dma(cst[:], consts.ap(), writes=[cst.b])
        IDENT, ONES, TRI_LE, TRI_GE, SGT, SLT, PERM, BLK0, BLK1, TAU = range(10)
        C_ = lambda i: cst[:, i, :]
        ones_bf = mk_sb(es, [128, 128], BF16, "onesbf")
        P.copy("dve", ones_bf[:], C_(ONES), reads=[cst.b], writes=[ones_bf.b])
        modv = mk_sb(es, [128, 144, 2], F32, "modv")
        modc = mk_sb(es, [128, 3, KC, 2], F32, "modc")
        modg = mk_sb(es, [128, 3, KC, 2], F32, "modg")
        lngt = mk_sb(es, [128, 3, KC], F32, "lngt")
        lnbt = mk_sb(es, [128, 3, KC], F32, "lnbt")

        def pm_dst(t, lead, KCt, PW):
            def f(a0, an, c0, cn):
                assert c0 % PW == 0 and cn % PW == 0
                n0, nn = c0 // PW, cn // PW
                off = lead + n0 * 128 * KCt * PW + a0 * PW
                return [AP(t, off + ai * PW, [[KCt * PW, 128], [128 * KCt * PW, nn], [1, PW]]) for ai in range(an)]
            f.PW = PW
            return f

        def pm_panel(t, lead, KCt, PW, n):
            return AP(t, lead + n * 128 * KCt * PW, [[KCt * PW, 128], [1, KCt * PW]])

        def cast3(src, dst, A, C, scale=None):
            with phase() as sb:
                st = [sb([128, 4096], F32, "cst32") for _ in range(3)]
                sbf = [sb([128, 4096], BF16, "cst16") for _ in range(3)]
                it = 0
                cb = min(C, 4096)
                ab = max(1, 4096 // cb)
                for a0 in range(0, A, ab):
                    an = min(ab, A - a0)
                    for c0 in range(0, C, cb):
                        cn = min(cb, C - c0)
                        s32 = st[it % 3]
                        s16 = sbf[it % 3]
                        v32 = s32[:, 0:an * cn].rearrange("p (a c) -> p a c", a=an)
                        v16 = s16[:, 0:an * cn].rearrange("p (a c) -> p a c", a=an)
                        P.dma(v32, src[:, a0:a0 + an, c0:c0 + cn], writes=[s32.b])
                        eng = ("dve", "act", "pool")[it % 3]
                        if scale is not None:
                            P.ts("dve", v16, v32, scale, None, ALU.mult, reads=[s32.b], writes=[s16.b])
                        else:
                            P.copy(eng, v16, v32, reads=[s32.b], writes=[s16.b])
                        if callable(dst):
                            for ai, dap in enumerate(dst(a0, an, c0, cn)):
                                P.dma(dap, v16[:, ai, :].rearrange("p (n c) -> p n c", c=dst.PW), reads=[s16.b], eng="act")
                        else:
                            P.dma(dst[:, a0:a0 + an, c0:c0 + cn], v16, reads=[s16.b], eng="pool")
                        it += 1

        def cast_w(src_ap, dst_ap, R, C, scale=None):
            cast3(src_ap.rearrange("(a p) c -> p a c", p=128),
                  dst_ap if callable(dst_ap) else dst_ap.rearrange("(a p) c -> p a c", p=128), R // 128, C, scale)

        for l in range(0 if not (dbg and dbg.get('nocast')) else 2, 2):
            for j in range(2):
                cast_w(ffn_w1.ap()[l, j], w1b.ap()[l, j], D, DFF)
                cast_w(ffn_w3.ap()[l, j], w3b.ap()[l, j], D, DFF)
                cast_w(ffn_w2.ap()[l, j], w2b.ap()[l, j], DFF, D)
            cast_w(w_in.ap()[l], winb.ap()[l], D, INC)
            for j in range(3):
                cast_w(w_branch.ap()[l, j], wbrb.ap()[l, j], 1024, D)
            cast_w(w_out.ap()[l], woutb.ap()[l], D, D)
            cast_w(glu_w.ap()[l], glub16.ap()[l], 1024, 1024)
            for d in range(2):
                for (s_, d_, sc_) in ((s5bre, sbreb, None), (s5bim, sbimb, None), (s5cre, screb, None),
                                      (s5cim, scimb, -1.0)):
                    cast3(s_.ap()[l, d].rearrange("s p c -> p s c"), d_.ap()[l, d].rearrange("s p c -> p s c"),
                          32, 128, sc_)

        def load_wpanel(tile, wsrc, kc_n, c0, cn):
            P.dma(tile[:, 0:kc_n, 0:cn], wsrc.rearrange("(a p) c -> p a c", p=128)[:, :, c0:c0 + cn],
                  writes=[tile.b])

        def layer_norm(sb_tmp, hx, TB, l, j, pss1, pss2):
            sq = sb_tmp["sq"]
            for kc in range(KC):
                P.mm(pss1[:, 0:TB], C_(ONES), hx[:, kc, 0:TB], kc == 0, kc == KC - 1,
                     reads=[cst.b, hx.b], writes=[pss1.b])
            for kc in range(KC):
                s = sq[kc % 2]
                P.act(s[:, 0:TB], hx[:, kc, 0:TB], AF.Square, reads=[hx.b], writes=[s.b])
                P.mm(pss2[:, 0:TB], C_(ONES), s[:, 0:TB], kc == 0, kc == KC - 1,
                     reads=[cst.b, s.b], writes=[pss2.b])
            mean, rstd, nmr = sb_tmp["mean"], sb_tmp["rstd"], sb_tmp["nmr"]
            P.ts("dve", mean[:, 0:TB], pss1[:, 0:TB], 1.0 / D, None, ALU.mult, reads=[pss1.b], writes=[mean.b])
            P.tt("dve", nmr[:, 0:TB], mean[:, 0:TB], mean[:, 0:TB], ALU.mult, reads=[mean.b], writes=[nmr.b])
            P.stt("dve", rstd[:, 0:TB], pss2[:, 0:TB], 1.0 / D, nmr[:, 0:TB], ALU.mult, ALU.subtract,
                  reads=[pss2.b, nmr.b], writes=[rstd.b])
            P.act(rstd[:, 0:TB], rstd[:, 0:TB], AF.Ln, bias=kct[:, 0:1], reads=[rstd.b, kct.b], writes=[rstd.b])
            P.act(rstd[:, 0:TB], rstd[:, 0:TB], AF.Exp, scale=-0.5, reads=[rstd.b], writes=[rstd.b])
            P.stt("dve", nmr[:, 0:TB], mean[:, 0:TB], -1.0, rstd[:, 0:TB], ALU.mult, ALU.mult,
                  reads=[mean.b, rstd.b], writes=[nmr.b])
            for kc in range(KC):
                e = P.ve()
                P.tt(e, hx[:, kc, 0:TB], hx[:, kc, 0:TB], rstd[:, 0:TB], ALU.mult, reads=[hx.b, rstd.b], writes=[hx.b])
                P.tt(e, hx[:, kc, 0:TB], hx[:, kc, 0:TB], nmr[:, 0:TB], ALU.add, reads=[hx.b, nmr.b], writes=[hx.b])
                P.act(hx[:, kc, 0:TB], hx[:, kc, 0:TB], AF.Identity, bias=lnbt[:, j, kc:kc + 1],
                      scale=lngt[:, j, kc:kc + 1], reads=[hx.b, lngt.b, lnbt.b], writes=[hx.b])

        def hview(dr, t0, TB):
            return dr.ap().rearrange("(a p) t -> p a t", p=128)[:, :, t0:t0 + TB]

        def ln_tmps(sb):
            return dict(sq=[sb([128, 512], F32, "sq") for _ in range(2)], mean=sb([128, 512], F32, "mean"),
                        rstd=sb([128, 512], F32, "rstd"), nmr=sb([128, 512], F32, "nmr"))

        def mod_phase(l):
            with phase() as sb:
                cv = sb([128, KC, 2], F32, "cv")
                sc = sb([128, KC, 2], F32, "sc")
                bm = sb([128, 144], F32, "bm")
                P.dma(cv[:], cvec.ap(), writes=[cv.b])
                P.dma(bm[:], bmod.ap()[l], writes=[bm.b])
                P.dma(lngt[:], lng.ap()[l], writes=[lngt.b])
                P.dma(lnbt[:], lnb.ap()[l], writes=[lnbt.b])
                P.act(sc[:], cv[:], AF.Silu, reads=[cv.b], writes=[sc.b])
                wp = [sb([128, KC, 512], F32, "wmod") for _ in range(2)]
                wv = w_mod.ap()[l].rearrange("(a p) c -> p a c", p=128)
                for cbk in range(36):
                    w = wp[cbk % 2]
                    P.dma(w[:], wv[:, :, cbk * 512:(cbk + 1) * 512], writes=[w.b])
                    pm = psum[cbk % 2]
                    for m in range(4):
                        for kc in range(KC):
                            P.mm(pm[:, 2 * m:2 * m + 2], w[:, kc, m * 128:(m + 1) * 128], sc[:, kc, :],
                                 kc == 0, kc == KC - 1, reads=[w.b, sc.b], writes=[pm.b])
                    for m in range(4):
                        mt = cbk * 4 + m
                        P.ts("dve", modv[:, mt, :], pm[:, 2 * m:2 * m + 2], bm[:, mt:mt + 1], None, ALU.add,
                             reads=[pm.b, bm.b], writes=[modv.b])
                mv = modv[:].rearrange("p (j r k) w -> p j r k w", j=3, r=3)
                P.ts("dve", modc[:], mv[:, :, 1, :, :], 1.0, None, ALU.add, reads=[modv.b], writes=[modc.b])
                for j in range(3):
                    P.ts("dve", modg[:, j], mv[:, j, 2, :, :], 0.5 if j != 1 else 1.0, None, ALU.mult,
                         reads=[modv.b], writes=[modg.b])
            return

        def mshift(j, kc, which):
            return modv[:, (3 * j) * KC + kc, which:which + 1]

        def modulate(xm, hx, j, TB, which):
            for kc in range(KC):
                P.act(xm[:, kc, 0:TB], hx[:, kc, 0:TB], AF.Identity, bias=mshift(j, kc, which),
                      scale=modc[:, j, kc, which:which + 1], reads=[hx.b, modv.b, modc.b], writes=[xm.b])

        def ffn_phase(l, jj, src, dst, skip_ctx, to_out):
            j = 0 if jj == 0 else 2
            W1 = w1b.ap()[l, jj]
            W3 = w3b.ap()[l, jj]
            W2 = w2b.ap()[l, jj]
            with phase() as sb:
                hx = sb([128, KC, 512], F32, "hx")
                xm = sb([128, KC, 512], BF16, "xm")
                g = sb([128, FC, 512], BF16, "g")
                w1p = [sb([128, KC, 256], BF16, "w1p") for _ in range(2)]
                w3p = [sb([128, KC, 256], BF16, "w3p") for _ in range(2)]
                w2p = [sb([128, FC, 128], BF16, "w2p") for _ in range(2)]
                sa = [sb([128, 512], F32, "sa") for _ in range(2)]
                lt = ln_tmps(sb)
                for (t0, TB, which) in blocks:
                    if skip_ctx and which == 1:
                        continue
                    P.dma(hx[:, :, 0:TB], hview(src, t0, TB), reads=[DB(tname[id(src)])], writes=[hx.b])
                    modulate(xm, hx, j, TB, which)
                    for kc in range(KC):
                        P.ts(P.ve(), hx[:, kc, 0:TB], hx[:, kc, 0:TB], ALPHA, None, ALU.mult, reads=[hx.b], writes=[hx.b])
                    for fp in range(FC // 2):
                        a, b = w1p[fp % 2], w3p[fp % 2]
                        load_wpanel(a, W1, KC, fp * 256, 256)
                        load_wpanel(b, W3, KC, fp * 256, 256)
                        for f2 in range(2):
                            f = fp * 2 + f2
                            pa, pb = psum[(f % 2) * 2], psum[(f % 2) * 2 + 1]
                            for kc in range(KC):
                                P.mm(pa[:, 0:TB], a[:, kc, f2 * 128:(f2 + 1) * 128], xm[:, kc, 0:TB], kc == 0, kc == KC - 1,
                                     reads=[a.b, xm.b], writes=[pa.b])
                            for kc in range(KC):
                                P.mm(pb[:, 0:TB], b[:, kc, f2 * 128:(f2 + 1) * 128], xm[:, kc, 0:TB], kc == 0, kc == KC - 1,
                                     reads=[b.b, xm.b], writes=[pb.b])
                            s = sa[f % 2]
                            P.act(s[:, 0:TB], pa[:, 0:TB], AF.Silu, reads=[pa.b], writes=[s.b])
                            P.tt("dve", g[:, f, 0:TB], s[:, 0:TB], pb[:, 0:TB], ALU.mult, reads=[s.b, pb.b], writes=[g.b])
                    for m in range(KC):
                        w = w2p[m % 2]
                        load_wpanel(w, W2, FC, m * 128, 128)
                        py = psum[4 + m % 2]
                        for kc in range(FC):
                            P.mm(py[:, 0:TB], w[:, kc, :], g[:, kc, 0:TB], kc == 0, kc == FC - 1,
                                 reads=[w.b, g.b], writes=[py.b])
                        P.stt("dve", hx[:, m, 0:TB], py[:, 0:TB], modg[:, j, m, which:which + 1], hx[:, m, 0:TB],
                              ALU.mult, ALU.add, reads=[py.b, modg.b, hx.b], writes=[hx.b])
                    layer_norm(lt, hx, TB, l, j, psum[6], psum[7])
                    if to_out:
                        P.dma(yT.ap().rearrange("(a p) t -> p a t", p=128)[:, :, t0 - NCTX:t0 - NCTX + TB],
                              hx[:, :, 0:TB], reads=[hx.b], eng="pool", is_out=True)
                    else:
                        P.dma(hview(dst, t0, TB), hx[:, :, 0:TB], reads=[hx.b], writes=[DB(tname[id(dst)])], eng="pool")

        def inproj_phase(l):
            W = winb.ap()[l]
            fm_specs = [(0, qT, BF16, True), (1024, kT, BF16, True), (3072, zT, F32, False),
                        (4096, xbcT, F32, False), (4096 + 1024, xbcT, F32, False), (6176, uT, F32, False)] + \
                       [(7200 + 1024 * i, gT, F32, False) for i in range(6)]
            with phase() as sb:
                hx = sb([128, KC, 512], F32, "hx")
                xm = sb([128, KC, 512], BF16, "xm")
                wp = [sb([128, KC, 512], BF16, "wp") for _ in range(2)]
                st32 = [sb([128, 4, 512], F32, "st32") for _ in range(2)]
                st16 = [sb([128, 4, 512], BF16, "st16") for _ in range(2)]
                qf = [sb([128, 512], F32, "qf") for _ in range(2)]
                rc = sb([128, 512], F32, "rc")
                rs = sb([128, 512], F32, "rs")
                vst = [sb([128, 1024], BF16, "vst") for _ in range(2)]
                dst_ = [sb([128, 32], F32, "dst") for _ in range(2)]
                wdt = sb([128, KC, 32], BF16, "wdt")
                it = 0
                for (t0, TB, which) in blocks:
                    P.dma(hx[:, :, 0:TB], hview(hT, t0, TB), reads=[DB("hT")], writes=[hx.b])
                    modulate(xm, hx, 1, TB, which)
                    if which == 0:
                        P.dma(rc[:, 0:TB], ropec.ap()[:, t0 - NCTX:t0 - NCTX + TB], writes=[rc.b])
                        P.dma(rs[:, 0:TB], ropes.ap()[:, t0 - NCTX:t0 - NCTX + TB], writes=[rs.b])
                    for si, (c0, dstT, dt, rope) in enumerate(fm_specs):
                        for pn in range(2):
                            w = wp[it % 2]
                            cc = c0 + pn * 512
                            load_wpanel(w, W, KC, cc, 512)
                            stg = (st16 if dt == BF16 else st32)[it % 2]
                            for m in range(4):
                                pm = psum[(it * 4 + m) % 4]
                                for kc in range(KC):
                                    P.mm(pm[:, 0:TB], w[:, kc, m * 128:(m + 1) * 128], xm[:, kc, 0:TB], kc == 0, kc == KC - 1,
                                         reads=[w.b, xm.b], writes=[pm.b])
                                if rope and which == 0:
                                    q_ = qf[m % 2]
                                    P.copy("act", q_[:, 0:TB], pm[:, 0:TB], reads=[pm.b], writes=[q_.b])
                                    pr = psum[4 + m % 2]
                                    P.mm(pr[:, 0:TB], C_(PERM), q_[:, 0:TB], True, True, reads=[cst.b, q_.b], writes=[pr.b])
                                    P.tt("dve", q_[:, 0:TB], q_[:, 0:TB], rc[:, 0:TB], ALU.mult, reads=[q_.b, rc.b], writes=[q_.b])
                                    P.stt("dve", stg[:, m, 0:TB], pr[:, 0:TB], 1.0, rs[:, 0:TB], ALU.mult, ALU.mult,
                                          reads=[pr.b, rs.b], writes=[stg.b])
                                    P.tt("dve", stg[:, m, 0:TB], stg[:, m, 0:TB], q_[:, 0:TB], ALU.add,
                                         reads=[stg.b, q_.b], writes=[stg.b])
                                else:
                                    P.copy("act" if m % 2 else "dve", stg[:, m, 0:TB], pm[:, 0:TB], reads=[pm.b], writes=[stg.b])
                            rows = (cc - c0) + (0 if dstT not in (xbcT, gT) else (c0 - (4096 if dstT is xbcT else 7200)))
                            dv = dstT.ap().rearrange("(a p) t -> p a t", p=128)[:, rows // 128:rows // 128 + 4, t0:t0 + TB]
                            P.dma(dv, stg[:, :, 0:TB], reads=[stg.b], writes=[DB(tname[id(dstT)])], eng="pool")
                            it += 1
                    wv_ = [wp[0], wp[1]]
                    load_wpanel(wv_[0], W, KC, 2048, 512)
                    load_wpanel(wv_[1], W, KC, 2560, 512)
                    P.dma(wdt[:], W.rearrange("(a p) c -> p a c", p=128)[:, :, 6144:6176], writes=[wdt.b])
                    wdtv = wdt
                    for tt_ in range(TB // 128):
                        vs = vst[tt_ % 2]
                        for hh in range(2):
                            pm = psum[(tt_ * 2 + hh) % 4]
                            for kc in range(KC):
                                P.mm(pm[:, :], xm[:, kc, tt_ * 128:(tt_ + 1) * 128], wv_[hh][:, kc, :], kc == 0, kc == KC - 1,
                                     reads=[xm.b, wv_[hh].b], writes=[pm.b])
                            P.copy("act" if hh else "dve", vs[:, hh * 512:(hh + 1) * 512], pm[:, :], reads=[pm.b], writes=[vs.b])
                        P.dma(vtok.ap()[t0 + tt_ * 128:t0 + (tt_ + 1) * 128, :], vs[:], reads=[vs.b], writes=[DB("vtok")], eng="pool")
                        pd = psum[4 + tt_ % 2]
                        for kc in range(KC):
                            P.mm(pd[:, 0:32], xm[:, kc, tt_ * 128:(tt_ + 1) * 128], wdtv[:, kc, :], kc == 0, kc == KC - 1,
                                 reads=[xm.b, wdt.b], writes=[pd.b])
                        ds = dst_[tt_ % 2]
                        P.copy("dve", ds[:], pd[:, 0:32], reads=[pd.b], writes=[ds.b])
                        P.dma(dttok.ap()[t0 + tt_ * 128:t0 + (tt_ + 1) * 128, :], ds[:], reads=[ds.b], writes=[DB("dttok")], eng="pool")

        def att_phase(l, last):
            lam_init = 0.8 - 0.6 * math.exp(-0.3 * l)
            with phase() as sb:
                al = sb([128, 256], F32, "al")
                sw = sb([128, 1], F32, "sw")
                sm = sb([128, 8], F32, "sm")
                P.dma(al[:], attlam.ap()[l], writes=[al.b])
                P.dma(sw[:], subln.ap()[l], writes=[sw.b])
                pr_ = sb([128, 128], F32, "pr_")
                P.tt("dve", pr_[:, 0:64], al[:, 0:64], al[:, 64:128], ALU.mult, reads=[al.b], writes=[pr_.b])
                P.tt("dve", pr_[:, 64:128], al[:, 128:192], al[:, 192:256], ALU.mult, reads=[al.b], writes=[pr_.b])
                P.emit("dve", lambda e: e.tensor_reduce(sm[:, 0:2], pr_[:].rearrange("p (a b) -> p a b", a=2), AX.X, ALU.add),
                       reads=[pr_.b], writes=[sm.b])
                P.act(sm[:, 2:4], sm[:, 0:2], AF.Exp, reads=[sm.b], writes=[sm.b])
                P.tt("dve", sm[:, 4:5], sm[:, 3:4], sm[:, 2:3], ALU.subtract, reads=[sm.b], writes=[sm.b])
                P.ts("dve", sm[:, 5:6], sm[:, 4:5], -lam_init, None, ALU.add, reads=[sm.b], writes=[sm.b])
                P.ts("dve", sm[:, 6:7], sw[:, 0:1], 1.0 - lam_init, None, ALU.mult, reads=[sw.b], writes=[sm.b])
                neglam = sm[:, 5:6]
                swl = sm[:, 6:7]
                qh = [sb([128, T], BF16, "qh") for _ in range(2)]
                kh = [sb([128, T], BF16, "kh") for _ in range(2)]
                vh = [sb([128, NCH, 128], BF16, "vh") for _ in range(2)]
                sqt = [sb([128, 512], F32, "sqt") for _ in range(2)]
                mx = sb([128, 16], F32, "mx")
                negc = sb([128, 2], F32, "negc")
                pt = [sb([128, 512], BF16, "pt") for _ in range(4)]
                o0 = sb([128, 512], F32, "o0")
                o1 = sb([128, 512], F32, "o1")
                r0 = sb([128, 512], F32, "r0")
                ob = [sb([128, 512], BF16, "ob") for _ in range(2)]
                for h in range(8):
                    q_, k_, v_ = qh[h % 2], kh[h % 2], vh[h % 2]
                    P.dma(q_[:], qT.ap()[h * 128:(h + 1) * 128, :], reads=[DB("qT")], writes=[q_.b])
                    P.dma(k_[:], kT.ap()[h * 128:(h + 1) * 128, :], reads=[DB("kT")], writes=[k_.b])
                    P.dma(v_[:], vtok.ap().rearrange("(a p) c -> p a c", p=128)[:, :, h * 128:(h + 1) * 128],
                          reads=[DB("vtok")], writes=[v_.b])
                    P.memset("dve", mx[:], 0.0, writes=[mx.b])
                    it = 0
                    for qi, src_ in enumerate((q_, k_)):
                        for (t0, TB, which) in blocks:
                            s = sqt[it % 2]
                            P.act(s[:, 0:TB], src_[:, t0:t0 + TB], AF.Square, reads=[src_.b], writes=[s.b])
                            for jm in range(2):
                                pm = psum[(it * 2 + jm) % 4]
                                P.mm(pm[:, 0:TB], C_(BLK0 + jm), s[:, 0:TB], True, True, reads=[cst.b, s.b], writes=[pm.b])
                                col = 8 + qi * 2 + jm
                                P.emit("dve", lambda e, pm=pm, TB=TB, col=col: e.tensor_reduce(mx[:, col:col + 1], pm[:, 0:TB], AX.X, ALU.max),
                                       reads=[pm.b], writes=[mx.b])
                                c2 = qi * 2 + jm
                                P.tt("dve", mx[:, c2:c2 + 1], mx[:, c2:c2 + 1], mx[:, col:col + 1], ALU.max, reads=[mx.b], writes=[mx.b])
                            it += 1
                    P.tt("dve", negc[:], mx[:, 0:2], mx[:, 2:4], ALU.mult, reads=[mx.b], writes=[negc.b])
                    P.act(negc[:], negc[:], AF.Ln, reads=[negc.b], writes=[negc.b])
                    P.act(negc[:], negc[:], AF.Exp, scale=0.5, reads=[negc.b], writes=[negc.b])
                    P.ts("dve", negc[:], negc[:], -0.125, None, ALU.mult, reads=[negc.b], writes=[negc.b])
                    for bi, (t0, TB, which) in enumerate(blocks):
                        if which == 1 and last:
                            continue
                        nk = 2 if which == 1 else NCH
                        pO = (psum[4], psum[5])
                        pS = (psum[6], psum[7])
                        def qk(kt):
                            for jm in range(2):
                                pm = psum[(kt % 2) * 2 + jm]
                                P.mm(pm[:, 0:TB], k_[jm * 64:(jm + 1) * 64, kt * 128:(kt + 1) * 128],
                                     q_[jm * 64:(jm + 1) * 64, t0:t0 + TB], True, True, reads=[k_.b, q_.b], writes=[pm.b])

                        def rest(kt):
                            for jm in range(2):
                                pm = psum[(kt % 2) * 2 + jm]
                                p_ = pt[(kt % 2) * 2 + jm]
                                P.act(p_[:, 0:TB], pm[:, 0:TB], AF.Exp, bias=negc[:, jm:jm + 1], scale=0.125,
                                      reads=[pm.b, negc.b], writes=[p_.b])
                            for jm in range(2):
                                p_ = pt[(kt % 2) * 2 + jm]
                                P.mm(pO[jm][:, 0:TB], v_[:, kt, :], p_[:, 0:TB], kt == 0, kt == nk - 1,
                                     reads=[v_.b, p_.b], writes=[pO[jm].b])
                                P.mm(pS[jm][:, 0:TB], ones_bf[:], p_[:, 0:TB], kt == 0, kt == nk - 1,
                                     reads=[ones_bf.b, p_.b], writes=[pS[jm].b])

                        qk(0)
                        for kt in range(nk):
                            if kt + 1 < nk:
                                qk(kt + 1)
                            rest(kt)
                        P.emit("dve", lambda e, TB=TB: e.reciprocal(r0[:, 0:TB], pS[0][:, 0:TB]), reads=[pS[0].b], writes=[r0.b])
                        P.tt("dve", o0[:, 0:TB], pO[0][:, 0:TB], r0[:, 0:TB], ALU.mult, reads=[pO[0].b, r0.b], writes=[o0.b])
                        P.emit("dve", lambda e, TB=TB: e.reciprocal(r0[:, 0:TB], pS[1][:, 0:TB]), reads=[pS[1].b], writes=[r0.b])
                        P.tt("dve", o1[:, 0:TB], pO[1][:, 0:TB], r0[:, 0:TB], ALU.mult, reads=[pO[1].b, r0.b], writes=[o1.b])
                        P.stt("dve", o0[:, 0:TB], o1[:, 0:TB], neglam, o0[:, 0:TB], ALU.mult, ALU.add,
                              reads=[o1.b, o0.b, sm.b], writes=[o0.b])
                        P.act(o1[:, 0:TB], o0[:, 0:TB], AF.Square, reads=[o0.b], writes=[o1.b])
                        pm = psum[0]
                        P.mm(pm[:, 0:TB], C_(ONES), o1[:, 0:TB], True, True, reads=[cst.b, o1.b], writes=[pm.b])
                        P.act(r0[:, 0:TB], pm[:, 0:TB], AF.Ln, bias=kct[:, 1:2], scale=1.0 / 128, reads=[pm.b, kct.b], writes=[r0.b])
                        P.act(r0[:, 0:TB], r0[:, 0:TB], AF.Exp, scale=-0.5, reads=[r0.b], writes=[r0.b])
                        o_ = ob[bi % 2]
                        P.stt("dve", o_[:, 0:TB], o0[:, 0:TB], swl, r0[:, 0:TB], ALU.mult, ALU.mult,
                              reads=[o0.b, sm.b, r0.b], writes=[o_.b])
                        P.dma(oattT.ap()[h * 128:(h + 1) * 128, t0:t0 + TB], o_[:, 0:TB], reads=[o_.b],
                              writes=[DB("oattT")], eng="pool")

        def conv_phase(l):
            with phase() as sb:
                cw = sb([128, 16, 5], F32, "cw")
                cb = sb([128, 16], F32, "cb")
                P.dma(cw[:], convw.ap()[l], writes=[cw.b])
                P.dma(cb[:], convb.ap()[l], writes=[cb.b])
                xp = [sb([128, T + 8], F32, "xp") for _ in range(2)]
                acc = [sb([128, T], F32, "acc") for _ in range(2)]
                o16 = [sb([128, T], BF16, "o16") for _ in range(2)]
                tk32 = [sb([128, NCH, 128], F32, "tk32") for _ in range(1)]
                tk16 = [sb([128, NCH, 128], BF16, "tk16") for _ in range(1)]
                for x_ in xp:
                    P.memset("dve", x_[:], 0.0, writes=[x_.b])
                segs = [(0, NCTX, 2), (NCTX, NX, 6)]
                for ct in range(16):
                    x_, a_ = xp[ct % 2], acc[ct % 2]
                    for (s0, sl, off) in segs:
                        P.dma(x_[:, off + s0:off + s0 + sl], xbcT.ap()[ct * 128:(ct + 1) * 128, s0:s0 + sl],
                              reads=[DB("xbcT")], writes=[x_.b])
                    for (s0, sl, off) in segs:
                        e = "dve"
                        base = off + s0 - 2
                        P.ts(e, a_[:, s0:s0 + sl], x_[:, base:base + sl], cw[:, ct, 0:1], None, ALU.mult,
                             reads=[x_.b, cw.b], writes=[a_.b])
                        for k in range(1, 5):
                            P.stt(e, a_[:, s0:s0 + sl], x_[:, base + k:base + k + sl], cw[:, ct, k:k + 1], a_[:, s0:s0 + sl],
                                  ALU.mult, ALU.add, reads=[x_.b, cw.b, a_.b], writes=[a_.b])
                    P.act(a_[:], a_[:], AF.Silu, bias=cb[:, ct:ct + 1], reads=[a_.b, cb.b], writes=[a_.b])
                    if ct < 8:
                        P.dma(xsT.ap()[ct * 128:(ct + 1) * 128, :], a_[:], reads=[a_.b], writes=[DB("xsT")], eng="pool")
                        tk = tk32[0]
                        for c4 in range(0, NCH, 4):
                            n4 = min(4, NCH - c4)
                            pm = psum[(c4 // 4) % 4]
                            for i in range(n4):
                                P.tr(pm[:, i * 128:(i + 1) * 128], a_[:, (c4 + i) * 128:(c4 + i + 1) * 128], C_(IDENT),
                                     reads=[a_.b, cst.b], writes=[pm.b])
                            P.copy("act" if (c4 // 4) % 2 else "dve", tk[:, c4:c4 + n4, :],
                                   pm[:, 0:n4 * 128].rearrange("p (a c) -> p a c", a=n4), reads=[pm.b], writes=[tk.b])
                        P.dma(xstok.ap().rearrange("(a p) c -> p a c", p=128)[:, :, ct * 128:(ct + 1) * 128], tk[:],
                              reads=[tk.b], writes=[DB("xstok")], eng="pool")
                    else:
                        o_ = o16[ct % 2]
                        P.copy("pool", o_[:], a_[:], reads=[a_.b], writes=[o_.b])
                        if ct < 12:
                            g_ = ct - 8
                            P.dma(BTs.ap()[g_ * 128:(g_ + 1) * 128, :], o_[:], reads=[o_.b], writes=[DB("BTs")], eng="pool")
                            tk = tk16[0]
                            for c4 in range(0, NCH, 4):
                                n4 = min(4, NCH - c4)
                                pm = psum[(c4 // 4) % 4]
                                for i in range(n4):
                                    P.tr(pm[:, i * 128:(i + 1) * 128], a_[:, (c4 + i) * 128:(c4 + i + 1) * 128], C_(IDENT),
                                         reads=[a_.b, cst.b], writes=[pm.b])
                                P.copy("act" if (c4 // 4) % 2 else "dve", tk[:, c4:c4 + n4, :],
                                       pm[:, 0:n4 * 128].rearrange("p (a c) -> p a c", a=n4), reads=[pm.b], writes=[tk.b])
                            P.dma(btok.ap().rearrange("(a p) c -> p a c", p=128)[:, :, g_ * 128:(g_ + 1) * 128], tk[:],
                                  reads=[tk.b], writes=[DB("btok")], eng="pool")
                        else:
                            g_ = ct - 12
                            P.dma(CTs.ap()[g_ * 128:(g_ + 1) * 128, :], o_[:], reads=[o_.b], writes=[DB("CTs")], eng="pool")

        def ssd_phase(l, last):
            nctx_ch = NCTX // 128
            with phase() as sb:
                al_ = sb([128, 32], F32, "al_")
                db_ = sb([128, 32], F32, "db_")
                A_ = sb([128, 32], F32, "A_")
                Dt = sb([128, 8], F32, "Dt")
                nw = sb([128, 8], F32, "nw")
                P.dma(al_[:], alog.ap()[l], writes=[al_.b])
                P.dma(db_[:], dtb.ap()[l], writes=[db_.b])
                P.dma(Dt[:], ssdD.ap()[l], writes=[Dt.b])
                P.dma(nw[:], ssdnw.ap()[l], writes=[nw.b])
                P.act(A_[:], al_[:], AF.Exp, reads=[al_.b], writes=[A_.b])
                P.ts("dve", A_[:], A_[:], -1.0, None, ALU.mult, reads=[A_.b], writes=[A_.b])
                H = sb([128, 1024], F32, "H")
                Hb = sb([128, 1024], BF16, "Hb")
                xs = [sb([128, 1024], F32, "xs") for _ in range(2)]
                bt = [sb([128, 512], BF16, "bt") for _ in range(2)]
                BT = [sb([128, 4, 128], BF16, "BT") for _ in range(2)]
                CT = [sb([128, 4, 128], BF16, "CT") for _ in range(2)]
                dr = [sb([128, 32], F32, "dr") for _ in range(2)]
                sm = sb([128, 8, 32], F32, "sm")
                xdt = sb([128, 1024], BF16, "xdt")
                xde = sb([128, 1024], BF16, "xde")
                Gm = sb([128, 512], F32, "Gm")
                rseg = sb([128, 16, 128], F32, "rseg")
                dcy = [sb([128, 512], F32, "dcy") for _ in range(2)]
                ecs = [sb([128, 512], F32, "ecs") for _ in range(2)]
                Mt = [sb([128, 4, 128], BF16, "Mt") for _ in range(2)]
                Cp = [sb([128, 4, 128], BF16, "Cp") for _ in range(2)]
                ysb = sb([128, 8, 128], F32, "ysb")
                y0 = sb([128, 8, 128], F32, "y0")
                xf = sb([128, 8, 128], F32, "xf")
                zf = sb([128, 8, 128], F32, "zf")
                sq = sb([128, 128], F32, "sq")
                rst = sb([128, 4, 128], F32, "rst")
                ob = sb([128, 8, 128], BF16, "ob")
                for d in range(2):
                    if d == 0:
                        order = list(range(NCH))
                    else:
                        order = list(range(nctx_ch - 1, -1, -1)) + list(range(NCH - 1, nctx_ch - 1, -1))
                    LM = C_(TRI_LE if d == 0 else TRI_GE)
                    U = C_(SGT if d == 0 else SLT)
                    MASK = LM
                    P.memset("dve", H[:], 0.0, writes=[H.b])
                    for oi, c in enumerate(order):
                        if last and d == 1 and c < nctx_ch and False:
                            pass
                        t0 = c * 128
                        x_, b_, B_, C2, d_ = xs[oi % 2], bt[oi % 2], BT[oi % 2], CT[oi % 2], dr[oi % 2]
                        P.dma(x_[:], xstok.ap()[t0:t0 + 128, :], reads=[DB("xstok")], writes=[x_.b])
                        P.dma(b_[:], btok.ap()[t0:t0 + 128, :], reads=[DB("btok")], writes=[b_.b])
                        P.dma(B_[:], BTs.ap().rearrange("(g n) t -> n g t", g=4)[:, :, t0:t0 + 128], reads=[DB("BTs")], writes=[B_.b])
                        P.dma(C2[:], CTs.ap().rearrange("(g n) t -> n g t", g=4)[:, :, t0:t0 + 128], reads=[DB("CTs")], writes=[C2.b])
                        P.dma(d_[:], dttok.ap()[t0:t0 + 128, :], reads=[DB("dttok")], writes=[d_.b])
                        xx, ax, ex, ln_, dtp, adt, toe, w2_ = (sm[:, i, :] for i in range(8))
                        P.tt("dve", xx, d_[:], db_[:], ALU.add, reads=[d_.b, db_.b], writes=[sm.b])
                        P.stt("dve", ax, xx, -1.0, xx, ALU.mult, ALU.max, reads=[sm.b], writes=[sm.b])
                        P.act(ex, ax, AF.Exp, scale=-1.0, reads=[sm.b], writes=[sm.b])
                        P.act(ln_, ex, AF.Ln, bias=kct[:, 2:3], reads=[sm.b, kct.b], writes=[sm.b])
                        P.stt("dve", dtp, xx, 0.0, ln_, ALU.max, ALU.add, reads=[sm.b], writes=[sm.b])
                        P.tt("dve", adt, dtp, A_[:], ALU.mult, reads=[sm.b, A_.b], writes=[sm.b])
                        dsl = slice(d * 16, (d + 1) * 16)
                        pe_ = psum[0]
                        P.mm(pe_[:, 0:16], U, adt[:, dsl], True, True, reads=[cst.b, sm.b], writes=[pe_.b])
                        P.mm(pe_[:, 16:32], C_(ONES), adt[:, dsl], True, True, reads=[cst.b, sm.b], writes=[pe_.b])
                        P.act(toe, pe_[:, 0:32], AF.Exp, reads=[pe_.b], writes=[sm.b])
                        P.tt("dve", w2_[:, 0:16], dtp[:, dsl], toe[:, 0:16], ALU.mult, reads=[sm.b], writes=[sm.b])
                        x3 = x_[:].rearrange("p (h q) -> p h q", h=16)
                        P.tt("dve", xdt[:].rearrange("p (h q) -> p h q", h=16), x3,
                             dtp[:, dsl].unsqueeze(2).to_broadcast([128, 16, 64]), ALU.mult, reads=[x_.b, sm.b], writes=[xdt.b])
                        P.tt("pool", xde[:].rearrange("p (h q) -> p h q", h=16), x3,
                             w2_[:, 0:16].unsqueeze(2).to_broadcast([128, 16, 64]), ALU.mult, reads=[x_.b, sm.b], writes=[xde.b])
                        for g_ in range(4):
                            pS_ = psum[1 + g_ // 2]
                            P.mm(pS_[:, (g_ % 2) * 256:(g_ % 2 + 1) * 256], b_[:, g_ * 128:(g_ + 1) * 128],
                                 xde[:, g_ * 256:(g_ + 1) * 256], True, True, reads=[b_.b, xde.b], writes=[pS_.b])
                        P.copy("act", Hb[:], H[:], reads=[H.b], writes=[Hb.b])
                        pG = psum[3]
                        for g_ in range(4):
                            P.mm(pG[:, g_ * 128:(g_ + 1) * 128], B_[:, g_, :], C2[:, g_, :], True, True,
                                 reads=[B_.b, C2.b], writes=[pG.b])
                        P.tt("dve", Gm[:].rearrange("p (g l) -> p g l", g=4), pG[:].rearrange("p (g l) -> p g l", g=4),
                             MASK.unsqueeze(1).to_broadcast([128, 4, 128]), ALU.mult, reads=[pG.b, cst.b], writes=[Gm.b])
                        P.tt("pool", rseg[:], LM.unsqueeze(1).to_broadcast([128, 16, 128]),
                             adt[:, dsl].unsqueeze(2).to_broadcast([128, 16, 128]), ALU.mult, reads=[cst.b, sm.b], writes=[rseg.b])
                        for g_ in range(4):
                            rr_ = rseg[:, 4 * g_:4 * g_ + 4, :].rearrange("p h l -> p (h l)")
                            pseg = psum[4 + g_ % 2]
                            pcs = psum[6 + g_ % 2]
                            P.mm(pseg[:], U, rr_, True, True, reads=[cst.b, rseg.b], writes=[pseg.b])
                            P.mm(pcs[:], C_(ONES), rr_, True, True, reads=[cst.b, rseg.b], writes=[pcs.b])
                            dc, ec, M_, Cq = dcy[g_ % 2], ecs[g_ % 2], Mt[g_ % 2], Cp[g_ % 2]
                            P.act(dc[:], pseg[:], AF.Exp, reads=[pseg.b], writes=[dc.b])
                            P.act(ec[:], pcs[:], AF.Exp, reads=[pcs.b], writes=[ec.b])
                            P.tt("dve", M_[:], dc[:].rearrange("p (h l) -> p h l", h=4),
                                 Gm[:, g_ * 128:(g_ + 1) * 128].unsqueeze(1).to_broadcast([128, 4, 128]), ALU.mult,
                                 reads=[dc.b, Gm.b], writes=[M_.b])
                            P.tt("pool", Cq[:], ec[:].rearrange("p (h l) -> p h l", h=4),
                                 C2[:, g_, :].unsqueeze(1).to_broadcast([128, 4, 128]), ALU.mult,
                                 reads=[ec.b, C2.b], writes=[Cq.b])
                            pY = psum[0] if g_ < 2 else psum[3]
                            for e_ in range(4):
                                hd = g_ * 4 + e_
                                hp = hd // 2
                                jj_ = hd % 2
                                oo = pY[jj_ * 64:(jj_ + 1) * 64, (hp % 4) * 128:(hp % 4 + 1) * 128]
                                P.mm(oo, xdt[:, hd * 64:(hd + 1) * 64], M_[:, e_, :], True, False,
                                     reads=[xdt.b, M_.b], writes=[pY.b])
                                P.mm(oo, Hb[:, hd * 64:(hd + 1) * 64], Cq[:, e_, :], False, True,
                                     reads=[Hb.b, Cq.b], writes=[pY.b])
                            if g_ % 2 == 1:
                                hp0 = (g_ - 1) * 2
                                P.copy("act", ysb[:, hp0:hp0 + 4, :], pY[:].rearrange("p (a l) -> p a l", a=4),
                                       reads=[pY.b], writes=[ysb.b])
                        P.tt("dve", H[:].rearrange("p (h q) -> p h q", h=16), H[:].rearrange("p (h q) -> p h q", h=16),
                             toe[:, 16:32].unsqueeze(2).to_broadcast([128, 16, 64]), ALU.mult, reads=[H.b, sm.b], writes=[H.b])
                        P.tt("dve", H[:, 0:512], H[:, 0:512], psum[1][:], ALU.add, reads=[H.b, psum[1].b], writes=[H.b])
                        P.tt("dve", H[:, 512:1024], H[:, 512:1024], psum[2][:], ALU.add, reads=[H.b, psum[2].b], writes=[H.b])
                        yv = y0T.ap().rearrange("(a p) t -> p a t", p=128)[:, :, t0:t0 + 128]
                        if d == 0:
                            P.dma(yv, ysb[:], reads=[ysb.b], writes=[DB("y0T")], eng="pool")
                        else:
                            if last and c < nctx_ch:
                                continue
                            P.dma(y0[:], yv, reads=[DB("y0T")], writes=[y0.b])
                            P.dma(xf[:], xsT.ap().rearrange("(a p) t -> p a t", p=128)[:, :, t0:t0 + 128], reads=[DB("xsT")], writes=[xf.b])
                            P.dma(zf[:], zT.ap().rearrange("(a p) t -> p a t", p=128)[:, :, t0:t0 + 128], reads=[DB("zT")], writes=[zf.b])
                            P.tt("dve", ysb[:], ysb[:], y0[:], ALU.add, reads=[ysb.b, y0.b], writes=[ysb.b])
                            P.tt("pool", xf[:], xf[:], Dt[:].unsqueeze(2).to_broadcast([128, 8, 128]), ALU.mult,
                                 reads=[xf.b, Dt.b], writes=[xf.b])
                            P.tt("dve", ysb[:], ysb[:], xf[:], ALU.add, reads=[ysb.b, xf.b], writes=[ysb.b])
                            P.act(zf[:], zf[:], AF.Silu, reads=[zf.b], writes=[zf.b])
                            P.tt("dve", ysb[:], ysb[:], zf[:], ALU.mult, reads=[ysb.b, zf.b], writes=[ysb.b])
                            pn = psum[3]
                            for hp in range(8):
                                P.act(sq[:], ysb[:, hp, :], AF.Square, reads=[ysb.b], writes=[sq.b])
                                P.mm(pn[:, (hp // 2) * 128:(hp // 2 + 1) * 128], C_(ONES), sq[:], hp % 2 == 0, hp % 2 == 1,
                                     reads=[cst.b, sq.b], writes=[pn.b])
                            P.act(rst[:], pn[:].rearrange("p (g l) -> p g l", g=4), AF.Ln, bias=kct[:, 1:2], scale=1.0 / 256,
                                  reads=[pn.b, kct.b], writes=[rst.b])
                            P.act(rst[:], rst[:], AF.Exp, scale=-0.5, reads=[rst.b], writes=[rst.b])
                            for g_ in range(4):
                                P.tt("dve", ysb[:, 2 * g_:2 * g_ + 2, :], ysb[:, 2 * g_:2 * g_ + 2, :],
                                     rst[:, g_, :].unsqueeze(1).to_broadcast([128, 2, 128]), ALU.mult,
                                     reads=[ysb.b, rst.b], writes=[ysb.b])
                            P.tt("pool", ob[:], ysb[:], nw[:].unsqueeze(2).to_broadcast([128, 8, 128]), ALU.mult,
                                 reads=[ysb.b, nw.b], writes=[ob.b])
                            P.dma(ossdT.ap().rearrange("(a p) t -> p a t", p=128)[:, :, t0:t0 + 128], ob[:], reads=[ob.b],
                                  writes=[DB("ossdT")], eng="pool")

        def s5_phase(l, last):
            with phase() as sb:
                prm = sb([128, 2, 24, 32], F32, "prm")
                prmi = sb([128, 128], I32, "prmi")

                def sincos(y, s_out, c_out, t1_, t2_, t3_, ti_, fb, ib, eng="dve"):
                    for off, out_ in ((0.0, s_out), (0.25, c_out)):
                        P.ts(eng, t1_, y, off, None, ALU.add, reads=[fb], writes=[fb])
                        P.copy(eng, ti_, t1_, reads=[fb], writes=[ib])
                        P.copy(eng, t2_, ti_, reads=[ib], writes=[fb])
                        P.tt(eng, t1_, t1_, t2_, ALU.subtract, reads=[fb], writes=[fb])
                        P.ts(eng, t3_, t1_, 0.5, None, ALU.is_gt, reads=[fb], writes=[fb])
                        P.tt(eng, t1_, t1_, t3_, ALU.subtract, reads=[fb], writes=[fb])
                        P.act(out_, t1_, AF.Sin, scale=2 * PI, reads=[fb], writes=[fb])
                for d in range(2):
                    pv = lambda i: prm[:, d, i, :]
                    LRE, LIM, STP, MAG, ANG, T1, ABR, ABI, DEN, KRE, KIM, T2, T3, AL, RL, CL, SL, ANGN, LNR, NLNR = range(20)
                    P.dma(pv(LRE), s5lre.ap()[l, d], writes=[prm.b])
                    P.dma(pv(LIM), s5lim.ap()[l, d], writes=[prm.b])
                    P.dma(pv(STP), s5ls.ap()[l, d], writes=[prm.b])
                    P.act(pv(STP), pv(STP), AF.Exp, reads=[prm.b], writes=[prm.b])
                    P.tt("dve", pv(MAG), pv(LRE), pv(STP), ALU.mult, reads=[prm.b], writes=[prm.b])
                    P.act(pv(MAG), pv(MAG), AF.Exp, reads=[prm.b], writes=[prm.b])
                    P.tt("dve", pv(ANG), pv(LIM), pv(STP), ALU.mult, reads=[prm.b], writes=[prm.b])
                    P.ts("dve", pv(ANGN), pv(ANG), 1.0 / (2 * PI), None, ALU.mult, reads=[prm.b], writes=[prm.b])
                    sincos(pv(ANGN), pv(ABI), pv(ABR), pv(T1), pv(T2), pv(T3), prmi[:, 0:32], prm.b, prmi.b)
                    P.tt("dve", pv(ABR), pv(ABR), pv(MAG), ALU.mult, reads=[prm.b], writes=[prm.b])
                    P.tt("dve", pv(ABI), pv(ABI), pv(MAG), ALU.mult, reads=[prm.b], writes=[prm.b])
                    P.tt("dve", pv(DEN), pv(LRE), pv(LRE), ALU.mult, reads=[prm.b], writes=[prm.b])
                    P.tt("dve", pv(T1), pv(LIM), pv(LIM), ALU.mult, reads=[prm.b], writes=[prm.b])
                    P.tt("dve", pv(DEN), pv(DEN), pv(T1), ALU.add, reads=[prm.b], writes=[prm.b])
                    P.emit("dve", lambda e, d=d: e.reciprocal(prm[:, d, DEN, :], prm[:, d, DEN, :]), reads=[prm.b], writes=[prm.b])
                    P.ts("dve", pv(T2), pv(ABR), -1.0, None, ALU.add, reads=[prm.b], writes=[prm.b])
                    P.tt("dve", pv(KRE), pv(T2), pv(LRE), ALU.mult, reads=[prm.b], writes=[prm.b])
                    P.tt("dve", pv(T1), pv(ABI), pv(LIM), ALU.mult, reads=[prm.b], writes=[prm.b])
                    P.tt("dve", pv(KRE), pv(KRE), pv(T1), ALU.add, reads=[prm.b], writes=[prm.b])
                    P.tt("dve", pv(KRE), pv(KRE), pv(DEN), ALU.mult, reads=[prm.b], writes=[prm.b])
                    P.tt("dve", pv(KIM), pv(ABI), pv(LRE), ALU.mult, reads=[prm.b], writes=[prm.b])
                    P.tt("dve", pv(T1), pv(T2), pv(LIM), ALU.mult, reads=[prm.b], writes=[prm.b])
                    P.tt("dve", pv(KIM), pv(KIM), pv(T1), ALU.subtract, reads=[prm.b], writes=[prm.b])
                    P.tt("dve", pv(KIM), pv(KIM), pv(DEN), ALU.mult, reads=[prm.b], writes=[prm.b])
                    P.ts("dve", pv(T1), pv(ANGN), 128.0, None, ALU.mult, reads=[prm.b], writes=[prm.b])
                    P.copy("dve", prmi[:, 0:32], pv(T1), reads=[prm.b], writes=[prmi.b])
                    P.copy("dve", pv(T3), prmi[:, 0:32], reads=[prmi.b], writes=[prm.b])
                    P.tt("dve", pv(AL), pv(T1), pv(T3), ALU.subtract, reads=[prm.b], writes=[prm.b])
                    P.tt("dve", pv(T1), pv(LRE), pv(STP), ALU.mult, reads=[prm.b], writes=[prm.b])
                    P.act(pv(RL), pv(T1), AF.Exp, scale=128.0, reads=[prm.b], writes=[prm.b])
                    P.copy("dve", pv(LNR), pv(T1), reads=[prm.b], writes=[prm.b])
                    P.ts("dve", pv(NLNR), pv(T1), -1.0, None, ALU.mult, reads=[prm.b], writes=[prm.b])
                rsts = sb([128, T], F32, "rsts")
                P.memset("dve", rsts[:], 1.0, writes=[rsts.b])
                P.memset("dve", rsts[:].rearrange("p (c l) -> p c l", l=128)[:, :, 0:1], 0.0, writes=[rsts.b])
                cidx = sb([128, NCH], F32, "cidx")
                P.copy("dve", cidx[:], cst[:, TAU, 0:NCH], reads=[cst.b], writes=[cidx.b])
                tbs = [sb([128, 12, 128], F32, "tb_") for _ in range(2)]
                prmi2 = sb([128, 128], I32, "prmi2")
                ct_ = sb([128, 12, NCH], F32, "ct_")
                ub = sb([128, T], BF16, "ub")
                wB = [sb([128, 2, 128], BF16, "wB") for _ in range(2)]
                xo = [sb([128, T], BF16, "xo") for _ in range(2)]

                class W2:
                    def __init__(self, nm):
                        self.tl = sb([128, T], F32, nm)
                        self.bp = Buf(nm + "p")
                        self.bd = Buf(nm + "d")
                        self.all = [self.bp, self.bd]

                    def __getitem__(self, k):
                        return self.tl[k]

                R_, I_, A_, B_ = W2("R_"), W2("I_"), W2("A_"), W2("B_")
                xob = [(Buf("xo0p"), Buf("xo0d")), (Buf("xo1p"), Buf("xo1d"))]
                nctx_ch = NCTX // 128
                nxp = max(0, int(round(0.5 * (NX // 128))) - 0)
                if NX // 128 <= 4:
                    nxp = 1
                pieces = [(0, NCTX, "dve", 0, NCTX), (NCTX, T, "dve", NCTX, NX)]
                pieces = [p for p in pieces if p[1] > p[0]]

                tbh = [None]

                def bsel(w, eng):
                    return w.bp if eng == "pool" else w.bd

                def big(out, in0, tbl_i, in1, op):
                    for (c0, c1, eng, _, _) in pieces:
                        o3 = out[:, c0:c1].rearrange("p (c l) -> p c l", l=128)
                        a3 = in0[:, c0:c1].rearrange("p (c l) -> p c l", l=128)
                        if tbl_i is not None:
                            tcur = tbh[0]
                            b3 = tcur[:, tbl_i, :].unsqueeze(1).to_broadcast([128, (c1 - c0) // 128, 128])
                            P.tt(eng, o3, a3, b3, op, reads=[bsel(in0, eng), tcur.b], writes=[bsel(out, eng)])
                        else:
                            P.tt(eng, o3, a3, in1[:, c0:c1].rearrange("p (c l) -> p c l", l=128), op,
                                 reads=[bsel(in0, eng), bsel(in1, eng)], writes=[bsel(out, eng)])

                for d in range(2):
                    pv = lambda i: prm[:, d, i, :]
                    for st in range(32):
                        ct = st // 4
                        if st % 4 == 0:
                            P.dma(ub[:], uT.ap()[ct * 128:(ct + 1) * 128, :], reads=[DB("uT")], writes=[ub.b], eng="pool")
                        w_ = wB[st % 2]
                        P.dma(w_[:, 0, :], sbreb.ap()[l, d, st], writes=[w_.b])
                        P.dma(w_[:, 1, :], sbimb.ap()[l, d, st], writes=[w_.b])
                        col = lambda i: prm[:, d, i, st:st + 1]
                        tb_ = tbs[(d * 32 + st) % 2]
                        tbh[0] = tb_
                        E_ = "dve"
                        P.ts(E_, tb_[:, 0, :], C_(TAU), col(ANGN), None, ALU.mult, reads=[cst.b, prm.b], writes=[tb_.b])
                        sincos(tb_[:, 0, :], tb_[:, 3, :], tb_[:, 2, :], tb_[:, 1, :], tb_[:, 6, :], tb_[:, 7, :], prmi2[:, 0:128], tb_.b, prmi2.b, eng=E_)
                        P.act(tb_[:, 10, :], C_(TAU), AF.Exp, scale=col(LNR), reads=[cst.b, prm.b], writes=[tb_.b])
                        P.act(tb_[:, 11, :], C_(TAU), AF.Exp, scale=col(NLNR), reads=[cst.b, prm.b], writes=[tb_.b])
                        P.ts(E_, tb_[:, 4, :], tb_[:, 2, :], col(KRE), None, ALU.mult, reads=[tb_.b, prm.b], writes=[tb_.b])
                        P.ts(E_, tb_[:, 6, :], tb_[:, 3, :], col(KIM), None, ALU.mult, reads=[tb_.b, prm.b], writes=[tb_.b])
                        P.tt(E_, tb_[:, 4, :], tb_[:, 4, :], tb_[:, 6, :], ALU.add, reads=[tb_.b], writes=[tb_.b])
                        P.ts(E_, tb_[:, 5, :], tb_[:, 2, :], col(KIM), None, ALU.mult, reads=[tb_.b, prm.b], writes=[tb_.b])
                        P.ts(E_, tb_[:, 6, :], tb_[:, 3, :], col(KRE), None, ALU.mult, reads=[tb_.b, prm.b], writes=[tb_.b])
                        P.tt(E_, tb_[:, 5, :], tb_[:, 5, :], tb_[:, 6, :], ALU.subtract, reads=[tb_.b], writes=[tb_.b])
                        P.tt(E_, tb_[:, 4, :], tb_[:, 4, :], tb_[:, 11, :], ALU.mult, reads=[tb_.b], writes=[tb_.b])
                        P.tt(E_, tb_[:, 5, :], tb_[:, 5, :], tb_[:, 11, :], ALU.mult, reads=[tb_.b], writes=[tb_.b])
                        P.tt(E_, tb_[:, 8, :], tb_[:, 2, :], tb_[:, 10, :], ALU.mult, reads=[tb_.b], writes=[tb_.b])
                        P.tt(E_, tb_[:, 9, :], tb_[:, 3, :], tb_[:, 10, :], ALU.mult, reads=[tb_.b], writes=[tb_.b])
                        for (t0, TB, which) in blocks:
                            for ri, dstb in ((0, R_), (1, I_)):
                                pm = psum[(2 * (t0 // 512) + ri) % 4]
                                P.mm(pm[:, 0:TB], w_[:, ri, :], ub[:, t0:t0 + TB], True, True, reads=[w_.b, ub.b], writes=[pm.b])
                                if d == 0:
                                    P.copy("act", dstb[:, t0:t0 + TB], pm[:, 0:TB], reads=[pm.b], writes=dstb.all)
                                else:
                                    s0, sl = (0, NCTX) if which == 1 else (NCTX, NX)
                                    p0 = s0 + (sl - 1) - (t0 + TB - 1 - s0)
                                    P.copy("act", dstb[:, p0:p0 + TB], rev_last(pm[:, 0:TB]), reads=[pm.b], writes=dstb.all)
                        big(A_, R_, 4, None, ALU.mult)
                        big(B_, I_, 5, None, ALU.mult)
                        big(A_, A_, None, B_, ALU.subtract)
                        big(B_, R_, 5, None, ALU.mult)
                        big(I_, I_, 4, None, ALU.mult)
                        big(I_, I_, None, B_, ALU.add)
                        bc3 = lambda a: a[:].rearrange("p (c l) -> p c l", l=128)
                        er, ei, vr, vi, cc_, ss_, tmp, tmp2 = (ct_[:, i, :] for i in range(8))
                        wl_r = ct_[:, 8, :]
                        wl_i = ct_[:, 9, :]
                        P.emit("dve", lambda e, A_=A_: e.tensor_reduce(ct_[:, 8, :], A_[:].rearrange("p (c l) -> p c l", l=128), AX.X, ALU.add),
                               reads=A_.all, writes=[ct_.b])
                        P.emit("dve", lambda e, I_=I_: e.tensor_reduce(ct_[:, 9, :], I_[:].rearrange("p (c l) -> p c l", l=128), AX.X, ALU.add),
                               reads=I_.all, writes=[ct_.b])
                        c127 = tb_[:, 8, 127:128]
                        s127 = tb_[:, 9, 127:128]
                        P.ts("dve", er, wl_r, c127, None, ALU.mult, reads=[ct_.b, tb_.b], writes=[ct_.b])
                        P.ts("dve", tmp, wl_i, s127, None, ALU.mult, reads=[ct_.b, tb_.b], writes=[ct_.b])
                        P.tt("dve", er, er, tmp, ALU.subtract, reads=[ct_.b], writes=[ct_.b])
                        P.ts("dve", ei, wl_i, c127, None, ALU.mult, reads=[ct_.b, tb_.b], writes=[ct_.b])
                        P.ts("dve", tmp, wl_r, s127, None, ALU.mult, reads=[ct_.b, tb_.b], writes=[ct_.b])
                        P.tt("dve", ei, ei, tmp, ALU.add, reads=[ct_.b], writes=[ct_.b])
                        P.ts("dve", tmp, cidx[:], col(AL), None, ALU.mult, reads=[cidx.b, prm.b], writes=[ct_.b])
                        sincos(tmp, ss_, cc_, tmp2, ct_[:, 10, :], ct_[:, 11, :], prmi[:, 0:NCH], ct_.b, prmi.b)
                        P.tt("dve", vr, er, cc_, ALU.mult, reads=[ct_.b], writes=[ct_.b])
                        P.tt("dve", tmp, ei, ss_, ALU.mult, reads=[ct_.b], writes=[ct_.b])
                        P.tt("dve", vr, vr, tmp, ALU.add, reads=[ct_.b], writes=[ct_.b])
                        P.tt("dve", vi, ei, cc_, ALU.mult, reads=[ct_.b], writes=[ct_.b])
                        P.tt("dve", tmp, er, ss_, ALU.mult, reads=[ct_.b], writes=[ct_.b])
                        P.tt("dve", vi, vi, tmp, ALU.subtract, reads=[ct_.b], writes=[ct_.b])
                        P.ts("dve", tmp2, cidx[:], 0.0, col(RL), ALU.mult, ALU.add, reads=[cidx.b, prm.b], writes=[ct_.b])
                        P.scan("dve", er, tmp2, vr, 0.0, reads=[ct_.b], writes=[ct_.b])
                        P.scan("dve", ei, tmp2, vi, 0.0, reads=[ct_.b], writes=[ct_.b])
                        P.tt("dve", vr, er, cc_, ALU.mult, reads=[ct_.b], writes=[ct_.b])
                        P.tt("dve", tmp, ei, ss_, ALU.mult, reads=[ct_.b], writes=[ct_.b])
                        P.tt("dve", vr, vr, tmp, ALU.subtract, reads=[ct_.b], writes=[ct_.b])
                        P.tt("dve", vi, ei, cc_, ALU.mult, reads=[ct_.b], writes=[ct_.b])
                        P.tt("dve", tmp, er, ss_, ALU.mult, reads=[ct_.b], writes=[ct_.b])
                        P.tt("dve", vi, vi, tmp, ALU.add, reads=[ct_.b], writes=[ct_.b])
                        P.ts("dve", er, vr, col(ABR), None, ALU.mult, reads=[ct_.b, prm.b], writes=[ct_.b])
                        P.ts("dve", tmp, vi, col(ABI), None, ALU.mult, reads=[ct_.b, prm.b], writes=[ct_.b])
                        P.tt("dve", er, er, tmp, ALU.subtract, reads=[ct_.b], writes=[ct_.b])
                        P.ts("dve", ei, vi, col(ABR), None, ALU.mult, reads=[ct_.b, prm.b], writes=[ct_.b])
                        P.ts("dve", tmp, vr, col(ABI), None, ALU.mult, reads=[ct_.b, prm.b], writes=[ct_.b])
                        P.tt("dve", ei, ei, tmp, ALU.add, reads=[ct_.b], writes=[ct_.b])
                        P.tt("dve", bc3(A_)[:, 1:NCH, 0], bc3(A_)[:, 1:NCH, 0], er[:, 0:NCH - 1], ALU.add, reads=A_.all + [ct_.b], writes=A_.all)
                        P.tt("dve", bc3(I_)[:, 1:NCH, 0], bc3(I_)[:, 1:NCH, 0], ei[:, 0:NCH - 1], ALU.add, reads=I_.all + [ct_.b], writes=I_.all)
                        P.scan("dve", R_[:], rsts[:], A_[:], 0.0, reads=[rsts.b] + A_.all, writes=R_.all)
                        P.scan("dve", B_[:], rsts[:], I_[:], 0.0, reads=[rsts.b] + I_.all, writes=B_.all)
                        for ri in range(2):
                            if ri == 0:
                                big(A_, R_, 8, None, ALU.mult)
                                big(I_, B_, 9, None, ALU.mult)
                                op = ALU.subtract
                            else:
                                big(A_, B_, 8, None, ALU.mult)
                                big(I_, R_, 9, None, ALU.mult)
                                op = ALU.add
                            xo_ = xo[ri]
                            for (c0, c1, eng, s0, sl) in pieces:
                                xb = xob[ri][0 if eng == "pool" else 1]
                                if d == 0:
                                    oo = xo_[:, c0:c1]
                                else:
                                    n0 = 2 * s0 + sl - c1
                                    oo = rev_last(xo_[:, n0:n0 + (c1 - c0)])
                                P.tt(eng, oo, A_[:, c0:c1], I_[:, c0:c1], op, reads=[bsel(A_, eng), bsel(I_, eng)], writes=[xb])
                            P.dma(XS.ap()[d, ri, st * 128:(st + 1) * 128, :], xo_[:], reads=list(xob[ri]), writes=[DB("XS")], eng="pool")
            with phase() as sb:
                sd = sb([128, 8], F32, "sd")
                gb = sb([128, 8], F32, "gb")
                P.dma(sd[:], s5d.ap()[l], writes=[sd.b])
                P.dma(gb[:], glub.ap()[l], writes=[gb.b])
                cw_ = sb([128, 2, 2, 32, 128], BF16, "cw_")
                for d in range(2):
                    P.dma(cw_[:, d, 0], screb.ap()[l, d].rearrange("s p c -> p s c"), writes=[cw_.b])
                    P.dma(cw_[:, d, 1], scimb.ap()[l, d].rearrange("s p c -> p s c"), writes=[cw_.b])
                gw = sb([128, 8, 1024], BF16, "gw")
                P.dma(gw[:], glub16.ap()[l].rearrange("(a p) c -> p a c", p=128), writes=[gw.b])
                xs_ = [sb([128, 16, 512], BF16, "xs_") for _ in range(2)]
                uu = [sb([128, 512], F32, "uu") for _ in range(2)]
                tt32 = sb([128, 8, 512], F32, "tt32")
                tt16 = sb([128, 8, 512], BF16, "tt16")
                sg = [sb([128, 512], F32, "sg") for _ in range(2)]
                og = [sb([128, 8, 512], BF16, "og") for _ in range(1)]
                for (t0, TB, which) in blocks:
                    if last and which == 1:
                        continue
                    for ct in range(8):
                        x_ = xs_[ct % 2]
                        u_ = uu[ct % 2]
                        for d in range(2):
                            for ri in range(2):
                                P.dma(x_[:, (d * 2 + ri) * 4:(d * 2 + ri) * 4 + 4, 0:TB],
                                      XS.ap()[d, ri, ct * 512:(ct + 1) * 512, t0:t0 + TB].rearrange("(a p) t -> p a t", p=128),
                                      reads=[DB("XS")], writes=[x_.b])
                        P.dma(u_[:, 0:TB], uT.ap()[ct * 128:(ct + 1) * 128, t0:t0 + TB], reads=[DB("uT")], writes=[u_.b])
                        pm = psum[ct % 4]
                        n = 0
                        for d in range(2):
                            for ri in range(2):
                                for s4 in range(4):
                                    P.mm(pm[:, 0:TB], cw_[:, d, ri, ct * 4 + s4, :], x_[:, (d * 2 + ri) * 4 + s4, 0:TB], n == 0, n == 15,
                                         reads=[cw_.b, x_.b], writes=[pm.b])
                                    n += 1
                        P.stt("dve", tt32[:, ct, 0:TB], u_[:, 0:TB], sd[:, ct:ct + 1], pm[:, 0:TB], ALU.mult, ALU.add,
                              reads=[u_.b, sd.b, pm.b], writes=[tt32.b])
                        P.act(tt32[:, ct, 0:TB], tt32[:, ct, 0:TB], AF.Gelu, reads=[tt32.b], writes=[tt32.b])
                        P.copy("pool", tt16[:, ct, 0:TB], tt32[:, ct, 0:TB], reads=[tt32.b], writes=[tt16.b])
                    o_ = og[0]
                    for m in range(8):
                        pm = psum[4 + m % 4]
                        for kc in range(8):
                            P.mm(pm[:, 0:TB], gw[:, kc, m * 128:(m + 1) * 128], tt16[:, kc, 0:TB], kc == 0, kc == 7,
                                 reads=[gw.b, tt16.b], writes=[pm.b])
                        s_ = sg[m % 2]
                        P.act(s_[:, 0:TB], pm[:, 0:TB], AF.Sigmoid, bias=gb[:, m:m + 1], reads=[pm.b, gb.b], writes=[s_.b])
                        P.tt("dve", o_[:, m, 0:TB], tt32[:, m, 0:TB], s_[:, 0:TB], ALU.mult, reads=[tt32.b, s_.b], writes=[o_.b])
                    P.dma(os5T.ap().rearrange("(a p) t -> p a t", p=128)[:, :, t0:t0 + TB], o_[:, :, 0:TB], reads=[o_.b],
                          writes=[DB("os5T")], eng="pool")

        def merge_phase(l, last):

            brs = (oattT, ossdT, os5T)
            with phase() as sb:
                hx = sb([128, KC, 512], F32, "hx")
                brt = [sb([128, 8, 512], BF16, "brt") for _ in range(3)]
                wbp = [sb([128, 3, 8, 128], BF16, "wbp") for _ in range(2)]
                wop = [sb([128, KC, 128], BF16, "wop") for _ in range(2)]
                gl = [sb([128, 3, 512], F32, "gl") for _ in range(2)]
                acc = sb([128, 512], F32, "acc")
                tm = sb([128, 512], F32, "tm")
                mixed = sb([128, KC, 512], BF16, "mixed")
                lt = ln_tmps(sb)
                for (t0, TB, which) in blocks:
                    if last and which == 1:
                        continue
                    P.dma(hx[:, :, 0:TB], hview(hT, t0, TB), reads=[DB("hT")], writes=[hx.b])
                    for j in range(3):
                        P.dma(brt[j][:, :, 0:TB], brs[j].ap().rearrange("(a p) t -> p a t", p=128)[:, :, t0:t0 + TB],
                              reads=[DB(tname[id(brs[j])])], writes=[brt[j].b])
                    for kc in range(KC):
                        P.ts(P.ve(), hx[:, kc, 0:TB], hx[:, kc, 0:TB], ALPHA, None, ALU.mult, reads=[hx.b], writes=[hx.b])
                    for m in range(KC):
                        w = wbp[m % 2]
                        g_ = gl[m % 2]
                        for j in range(3):
                            P.dma(w[:, j], wbrb.ap()[l, j].rearrange("(a p) c -> p a c", p=128)[:, :, m * 128:(m + 1) * 128], writes=[w.b])
                        P.dma(g_[:, :, 0:TB], gT.ap().rearrange("(j a p) t -> p j a t", j=3, p=128)[:, :, m, t0:t0 + TB],
                              reads=[DB("gT")], writes=[g_.b])
                        P.act(g_[:, :, 0:TB], g_[:, :, 0:TB], AF.Sigmoid, reads=[g_.b], writes=[g_.b])
                        for j in range(3):
                            pm = psum[(m * 3 + j) % 4]
                            for kc in range(8):
                                P.mm(pm[:, 0:TB], w[:, j, kc, :], brt[j][:, kc, 0:TB], kc == 0, kc == 7,
                                     reads=[w.b, brt[j].b], writes=[pm.b])
                            if j == 0:
                                P.tt("dve", acc[:, 0:TB], pm[:, 0:TB], g_[:, 0, 0:TB], ALU.mult, reads=[pm.b, g_.b], writes=[acc.b])
                            else:
                                P.tt("dve", tm[:, 0:TB], pm[:, 0:TB], g_[:, j, 0:TB], ALU.mult, reads=[pm.b, g_.b], writes=[tm.b])
                                if j == 1:
                                    P.tt("dve", acc[:, 0:TB], acc[:, 0:TB], tm[:, 0:TB], ALU.add, reads=[acc.b, tm.b], writes=[acc.b])
                                else:
                                    P.tt("dve", mixed[:, m, 0:TB], acc[:, 0:TB], tm[:, 0:TB], ALU.add, reads=[acc.b, tm.b], writes=[mixed.b])
                    for m in range(KC):
                        w = wop[m % 2]
                        load_wpanel(w, woutb.ap()[l], KC, m * 128, 128)
                        py = psum[4 + m % 2]
                        for kc in range(KC):
                            P.mm(py[:, 0:TB], w[:, kc, :], mixed[:, kc, 0:TB], kc == 0, kc == KC - 1, reads=[w.b, mixed.b], writes=[py.b])
                        P.stt("dve", hx[:, m, 0:TB], py[:, 0:TB], modg[:, 1, m, which:which + 1], hx[:, m, 0:TB],
                              ALU.mult, ALU.add, reads=[py.b, modg.b, hx.b], writes=[hx.b])
                    layer_norm(lt, hx, TB, l, 1, psum[6], psum[7])
                    P.dma(hview(hT, t0, TB), hx[:, :, 0:TB], reads=[hx.b], writes=[DB("hT")], eng="pool")

        negpi = mk_sb(es, [128, 1], F32, "negpi")
        P.memset("dve", negpi[:], -PI, writes=[negpi.b])
        kct = mk_sb(es, [128, 4], F32, "kct")
        P.memset("dve", kct[:, 0:1], 1e-5, writes=[kct.b])
        P.memset("dve", kct[:, 1:2], 1e-6, writes=[kct.b])
        P.memset("dve", kct[:, 2:3], 1.0, writes=[kct.b])
        stages = dbg.get("stages") if dbg else None

        def on(name, l):
            return stages is None or (name, l) in stages or name in stages

        for l in range(DEPTH):
            last = (l == DEPTH - 1)
            if on("mod", l):
                P.phase_name = 'mod_phase' + str(l)
                mod_phase(l)
            if on("ffn1", l):
                P.phase_name = 'ffn1_' + str(l)
                ffn_phase(l, 0, xT if l == 0 else hT, hT, False, False)
            if on("inproj", l):
                P.phase_name = 'inproj_phase' + str(l)
                inproj_phase(l)
            if on("att", l):
                P.phase_name = 'att_phase' + str(l)
                att_phase(l, last)
            if on("conv", l):
                P.phase_name = 'conv_phase' + str(l)
                conv_phase(l)
            if on("ssd", l):
                P.phase_name = 'ssd_phase' + str(l)
                ssd_phase(l, last)
            if on("s5", l):
                P.phase_name = 's5_phase' + str(l)
                s5_phase(l, last)
            if on("merge", l):
                P.phase_name = 'merge_phase' + str(l)
                merge_phase(l, last)
            if on("ffn2", l):
                P.phase_name = 'ffn2_' + str(l)
                ffn_phase(l, 1, hT, hT, last, last)
        if dbg and dbg.get("dump"):
            for nm in dbg["dump"]:
                s_ = scr_map[nm]
                shp = list(s_.ap().shape)
                o = nc.dram_tensor("dbg_" + nm, shp, s_.ap().dtype, kind="ExternalOutput")
                rows = shp[0]
                for r0 in range(0, rows, 128):
                    with ExitStack() as e2:
                        tl = mk_sb(e2, [128, shp[1]], s_.ap().dtype, "dbgt")
                        P.dma(tl[:], s_.ap()[r0:r0 + 128, :], reads=[DB(nm)], writes=[tl.b])
                        P.dma(o.ap()[r0:r0 + 128, :], tl[:], reads=[tl.b], eng="pool", is_out=True)
                        P.barrier()
        P.finish()
    return nc


def _consts(NX):
    c = np.zeros((128, 10, 128), np.float32)
    i = np.arange(128)
    c[:, 0] = np.eye(128)
    c[:, 1] = 1.0
    c[:, 2] = (i[:, None] <= i[None, :])
    c[:, 3] = (i[:, None] >= i[None, :])
    c[:, 4] = (i[:, None] > i[None, :])
    c[:, 5] = (i[:, None] < i[None, :])
    pm = np.zeros((128, 128), np.float32)
    for d in range(128):
        dd = d % 64
        half = (dd % 32) // 16
        partner = d + 16 if half == 0 else d - 16
        pm[partner, d] = 1.0
    c[:, 6] = pm
    c[:, 7] = (i[:, None] < 64) * 1.0
    c[:, 8] = (i[:, None] >= 64) * 1.0
    c[:, 9] = i[None, :].astype(np.float32)
    t = np.arange(NX)
    row = (t // 64).astype(np.float32)
    colp = (t % 64).astype(np.float32)
    inv = (10000.0 ** (-np.arange(16, dtype=np.float32) / 16)).astype(np.float32)
    rc = np.zeros((128, NX), np.float32)
    rs = np.zeros((128, NX), np.float32)
    for d in range(128):
        dd = d % 64
        axis = dd // 32
        half = (dd % 32) // 16
        f = dd % 16
        pos = row if axis == 0 else colp
        ang = (pos * inv[f]).astype(np.float32)
        rc[d] = np.cos(ang)
        rs[d] = np.sin(ang) * (-1.0 if half == 0 else 1.0)
    return c, rc, rs


def prep_shared(inp, NX):
    f = lambda a: np.ascontiguousarray(np.asarray(a, dtype=np.float32))
    out = {}
    c, rc, rs = _consts(NX)
    out["consts"], out["ropec"], out["ropes"] = c, rc, rs
    out["w_mod"] = f(inp["w_mod"])
    out["bmod"] = f(np.asarray(inp["b_mod"]).reshape(2, 144, 128).transpose(0, 2, 1))
    out["lng"] = f(np.asarray(inp["ln_g"]).reshape(2, 3, 16, 128).transpose(0, 3, 1, 2))
    out["lnb"] = f(np.asarray(inp["ln_b"]).reshape(2, 3, 16, 128).transpose(0, 3, 1, 2))
    for k in ("ffn_w1", "ffn_w3", "ffn_w2", "w_in", "glu_w", "w_branch", "w_out"):
        out[k] = f(inp["s5_glu_w"] if k == "glu_w" else inp[k])
    out["attlam"] = f(np.broadcast_to(np.asarray(inp["att_lam"]).reshape(2, 1, 256), (2, 128, 256)))
    out["subln"] = f(np.asarray(inp["att_subln"]).reshape(2, 128, 1))
    out["convw"] = f(np.asarray(inp["ssd_conv_w"]).reshape(2, 5, 16, 128).transpose(0, 3, 2, 1))
    out["convb"] = f(np.asarray(inp["ssd_conv_b"]).reshape(2, 16, 128).transpose(0, 2, 1))
    out["alog"] = f(np.broadcast_to(np.asarray(inp["ssd_a_log"]).reshape(2, 1, 32), (2, 128, 32)))
    out["dtb"] = f(np.broadcast_to(np.asarray(inp["ssd_dt_bias"]).reshape(2, 1, 32), (2, 128, 32)))
    sd = np.asarray(inp["ssd_d"]).reshape(2, 8, 2)
    out["ssdD"] = f(np.repeat(sd, 64, axis=2).transpose(0, 2, 1))
    out["ssdnw"] = f(np.asarray(inp["ssd_norm"]).reshape(2, 8, 128).transpose(0, 2, 1))
    out["s5lre"] = f(np.asarray(inp["s5_lam_re"]).reshape(2, 2, 32, 128).transpose(0, 1, 3, 2))
    out["s5lim"] = f(np.asarray(inp["s5_lam_im"]).reshape(2, 2, 32, 128).transpose(0, 1, 3, 2))
    ls = np.repeat(np.asarray(inp["s5_log_step"]).reshape(2, 2, 64, 1), 64, axis=3)
    out["s5ls"] = f(ls.reshape(2, 2, 32, 128).transpose(0, 1, 3, 2))
    bre = np.asarray(inp["s5_b_re"]); bim = np.asarray(inp["s5_b_im"])
    cre = np.asarray(inp["s5_c_re"]); cim = np.asarray(inp["s5_c_im"])
    Bre = np.zeros((2, 2, 32, 128, 128), np.float32); Bim = np.zeros_like(Bre)
    Cre = np.zeros_like(Bre); Cim = np.zeros_like(Bre)
    for st in range(32):
        for g2 in range(2):
            g = 2 * st + g2
            gl = g % 8
            Bre[:, :, st, gl * 16:(gl + 1) * 16, g2 * 64:(g2 + 1) * 64] = bre[:, :, g].transpose(0, 1, 3, 2)
            Bim[:, :, st, gl * 16:(gl + 1) * 16, g2 * 64:(g2 + 1) * 64] = bim[:, :, g].transpose(0, 1, 3, 2)
            Cre[:, :, st, g2 * 64:(g2 + 1) * 64, gl * 16:(gl + 1) * 16] = cre[:, :, g].transpose(0, 1, 3, 2)
            Cim[:, :, st, g2 * 64:(g2 + 1) * 64, gl * 16:(gl + 1) * 16] = cim[:, :, g].transpose(0, 1, 3, 2)
    out["s5bre"], out["s5bim"], out["s5cre"], out["s5cim"] = Bre, Bim, Cre, Cim
    out["s5d"] = f(np.asarray(inp["s5_d"]).reshape(2, 8, 128).transpose(0, 2, 1))
    out["glub"] = f(np.asarray(inp["s5_glu_b"]).reshape(2, 8, 128).transpose(0, 2, 1))
    return out


def prep_core(inp, b):
    x = np.asarray(inp["x"][b], dtype=np.float32)
    ctx = np.asarray(inp["ctx"][b], dtype=np.float32)
    xT = np.ascontiguousarray(np.concatenate([ctx, x], axis=0).T)
    cv = np.stack([np.asarray(inp["c"][b]).reshape(16, 128).T, np.asarray(inp["c_ctx"]).reshape(16, 128).T], axis=2)
    return {"xT": xT, "cvec": np.ascontiguousarray(cv.astype(np.float32))}


def kernel(**inputs):
    NX = int(np.asarray(inputs["x"]).shape[1])
    B = int(np.asarray(inputs["x"]).shape[0])
    shared = prep_shared(inputs, NX)
    nc = build(NX)
    in_maps = []
    for core in range(8):
        m = dict(shared)
        m.update(prep_core(inputs, core % B))
        in_maps.append(m)
    res = run_bass_kernel_spmd(nc, in_maps, core_ids=list(range(8)))
    out = np.stack([np.ascontiguousarray(res.results[b]["yT"].T) for b in range(B)], axis=0)
    return out.astype(np.float32)
```

```python
import math
from contextlib import ExitStack, contextmanager
import numpy as np
import concourse.bass as bass
import concourse.mybir as mybir
from concourse.bass_utils import run_bass_kernel_spmd
from concourse.ap import AP

F32 = mybir.dt.float32
BF16 = mybir.dt.bfloat16
I32 = mybir.dt.int32
AF = mybir.ActivationFunctionType
ALU = mybir.AluOpType
AX = mybir.AxisListType

EPOCH = 16000
NDS = 48
D = 2048
KC = 16
DFF = 5632
FC = 44
NCTX = 256
INC = 13344
DEPTH = 2
ALPHA = (2 * DEPTH) ** 0.25
PI = math.pi


class Buf:
    __slots__ = ("name", "w", "r")

    def __init__(self, name=""):
        self.name = name
        self.w = None
        self.r = {}


class TL:
    def __init__(self, t, b):
        self.t = t
        self.b = b

    def __getitem__(self, k):
        return self.t[k]


class Prog:
    ENGS = ("pe", "act", "dve", "pool", "sp")

    def __init__(self, nc, es):
        self.nc = nc
        self.es = es
        self.q = {e: [] for e in self.ENGS}
        self.cnt = {e: 0 for e in self.ENGS}
        self.csem = {e: [] for e in self.ENGS}
        self.waited = {e: {} for e in self.ENGS}
        self.dsems = [es.enter_context(nc.semaphore(f"dq{i}")) for i in range(NDS)]
        self.dcnt = [0] * NDS
        self.dlast = [None] * NDS
        self.dnext = 0
        self.bufs = {}
        self.out_events = []
        self.rr = 0
        self.phase_name = "init"
        self.scopes = False

    def buf(self, key):
        b = self.bufs.get(key)
        if b is None:
            b = self.bufs[key] = Buf(str(key))
        return b

    def _csem(self, e, ep):
        while len(self.csem[e]) <= ep:
            self.csem[e].append(
                self.es.enter_context(self.nc.semaphore(f"c{e}{len(self.csem[e])}")))
        return self.csem[e][ep]

    def emit(self, eng, fn, reads=(), writes=(), dma=False, is_out=False):
        deps = {}

        def add(ev):
            if ev is None:
                return
            k = ev[0]
            if k not in deps or deps[k][2] < ev[2]:
                deps[k] = ev

        for b in reads:
            add(b.w)
        for b in writes:
            add(b.w)
            for ev in b.r.values():
                add(ev)
        if dma:
            i = self.dnext
            self.dnext = (self.dnext + 1) % NDS
            add(self.dlast[i])
            self.dcnt[i] += 1
            ev = (("d", i), self.dsems[i], 16 * self.dcnt[i])
            self.dlast[i] = ev
            inc = 16
        else:
            self.cnt[eng] += 1
            ep = (self.cnt[eng] - 1) // EPOCH
            ev = ((eng, ep), self._csem(eng, ep), self.cnt[eng] - ep * EPOCH)
            inc = 1
        waits = []
        wd = self.waited[eng]
        for k, (_, s, v) in deps.items():
            if eng == "pe" and k[0] == "pe":
                continue
            if wd.get(k, 0) >= v:
                continue
            wd[k] = v
            waits.append((s, v))
        self.q[eng].append((waits, fn, ev[1], inc, self.phase_name))
        for b in writes:
            b.w = ev
            b.r = {}
        for b in reads:
            k = ev[0]
            if k not in b.r or b.r[k][2] < ev[2]:
                b.r[k] = ev
        if is_out:
            self.out_events.append(ev)
        return ev

    def barrier(self):
        evs = []
        for e in self.ENGS:
            c = self.cnt[e]
            if c > 0:
                ep = (c - 1) // EPOCH
                evs.append(((e, ep), self._csem(e, ep), c - ep * EPOCH))
        for i in range(NDS):
            if self.dlast[i] is not None:
                evs.append(self.dlast[i])
        for e in self.ENGS:
            waits = []
            wd = self.waited[e]
            for (k, s, v) in evs:
                if k[0] == e:
                    continue
                if wd.get(k, 0) >= v:
                    continue
                wd[k] = v
                waits.append((s, v))
            if waits:
                self.q[e].append((waits, None, None, 0, self.phase_name))

    def finish(self):
        self.barrier()
        with self.nc.Block() as block:
            def mk(e):
                def f(eo):
                    cur = None
                    cm = None
                    for waits, fn, sem, inc, ph in self.q[e]:
                        if self.scopes and ph != cur:
                            if cm is not None:
                                cm.__exit__(None, None, None)
                            cm = self.nc.named_scope(ph)
                            cm.__enter__()
                            cur = ph
                        for (s, v) in waits:
                            eo.wait_ge(s, v)
                        if fn is not None:
                            fn(eo).then_inc(sem, inc)
                    if cm is not None:
                        cm.__exit__(None, None, None)
                return f
            block.tensor(mk("pe"))
            block.scalar(mk("act"))
            block.vector(mk("dve"))
            block.gpsimd(mk("pool"))
            block.sync(mk("sp"))

    def dma(self, out, in_, reads=(), writes=(), eng="sp", is_out=False):
        return self.emit(eng, lambda e: e.dma_start(out=out, in_=in_), reads, writes,
                         dma=True, is_out=is_out)

    def mm(self, out, lhsT, rhs, start, stop, reads=(), writes=()):
        return self.emit("pe", lambda e: e.matmul(out, lhsT, rhs, start=start, stop=stop),
                         reads, writes)

    def tr(self, out, in_, ident, reads=(), writes=()):
        return self.emit("pe", lambda e: e.transpose(out, in_, ident), reads, writes)

    def act(self, out, in_, func, bias=0.0, scale=1.0, reads=(), writes=()):
        return self.emit("act", lambda e: e.activation(out, in_, func, bias=bias, scale=scale),
                         reads, writes)

    def tt(self, eng, out, in0, in1, op, reads=(), writes=()):
        return self.emit(eng, lambda e: e.tensor_tensor(out, in0, in1, op), reads, writes)

    def ts(self, eng, out, in0, s1, s2, op0, op1=None, reads=(), writes=()):
        if op1 is None:
            return self.emit(eng, lambda e: e.tensor_scalar(out, in0, s1, None, op0), reads, writes)
        return self.emit(eng, lambda e: e.tensor_scalar(out, in0, s1, s2, op0, op1), reads, writes)

    def stt(self, eng, out, in0, scalar, in1, op0, op1, reads=(), writes=()):
        return self.emit(eng, lambda e: e.scalar_tensor_tensor(out, in0, scalar, in1, op0, op1),
                         reads, writes)

    def copy(self, eng, out, in_, reads=(), writes=()):
        if eng == "act":
            return self.emit(eng, lambda e: e.copy(out, in_), reads, writes)
        return self.emit(eng, lambda e: e.tensor_copy(out, in_), reads, writes)

    def memset(self, eng, ap, val, writes=()):
        return self.emit(eng, lambda e: e.memset(ap, val), (), writes)

    def scan(self, eng, out, d0, d1, init, reads=(), writes=()):
        return self.emit(eng, lambda e: e.tensor_tensor_scan(out, d0, d1, init, ALU.mult, ALU.add),
                         reads, writes)

    def ve(self):
        return "dve"


def rev_last(a):
    ap = [list(x) for x in a.ap]
    st, n = ap[-1]
    ap[-1] = [-st, n]
    return AP(a.tensor, a.offset + (n - 1) * st, ap)


def build(NX=4096, dbg=None):
    T = NCTX + NX
    NCH = T // 128
    blocks = [(0, 256, 1)] + [(NCTX + 512 * i, 512, 0) for i in range(NX // 512)]
    nc = bass.Bass("TRN2", target_bir_lowering=False)
    din = {}

    tname = {}

    def inp(name, shape, dt=F32):
        din[name] = nc.dram_tensor(name, list(shape), dt, kind="ExternalInput")
        tname[id(din[name])] = name
        return din[name]

    xT = inp("xT", [D, T])
    cvec = inp("cvec", [128, KC, 2])
    w_mod = inp("w_mod", [2, D, 9 * D])
    bmod = inp("bmod", [2, 128, 144])
    lng = inp("lng", [2, 128, 3, KC])
    lnb = inp("lnb", [2, 128, 3, KC])
    ffn_w1 = inp("ffn_w1", [2, 2, D, DFF])
    ffn_w3 = inp("ffn_w3", [2, 2, D, DFF])
    ffn_w2 = inp("ffn_w2", [2, 2, DFF, D])
    w_in = inp("w_in", [2, D, INC])
    attlam = inp("attlam", [2, 128, 256])
    subln = inp("subln", [2, 128, 1])
    convw = inp("convw", [2, 128, 16, 5])
    convb = inp("convb", [2, 128, 16])
    alog = inp("alog", [2, 128, 32])
    dtb = inp("dtb", [2, 128, 32])
    ssdD = inp("ssdD", [2, 128, 8])
    ssdnw = inp("ssdnw", [2, 128, 8])
    s5lre = inp("s5lre", [2, 2, 128, 32])
    s5lim = inp("s5lim", [2, 2, 128, 32])
    s5ls = inp("s5ls", [2, 2, 128, 32])
    s5bre = inp("s5bre", [2, 2, 32, 128, 128])
    s5bim = inp("s5bim", [2, 2, 32, 128, 128])
    s5cre = inp("s5cre", [2, 2, 32, 128, 128])
    s5cim = inp("s5cim", [2, 2, 32, 128, 128])
    s5d = inp("s5d", [2, 128, 8])
    glu_w = inp("glu_w", [2, 1024, 1024])
    glub = inp("glub", [2, 128, 8])
    w_branch = inp("w_branch", [2, 3, 1024, D])
    w_out = inp("w_out", [2, D, D])
    consts = inp("consts", [128, 10, 128])
    ropec = inp("ropec", [128, NX])
    ropes = inp("ropes", [128, NX])
    yT = nc.dram_tensor("yT", [D, NX], F32, kind="ExternalOutput")

    def scr(name, shape, dt):
        t_ = nc.dram_tensor(name, list(shape), dt, kind="Internal")
        tname[id(t_)] = name
        return t_

    w1b = scr("w1b", [2, 2, D, DFF], BF16)
    w3b = scr("w3b", [2, 2, D, DFF], BF16)
    w2b = scr("w2b", [2, 2, DFF, D], BF16)
    winb = scr("winb", [2, D, INC], BF16)
    wbrb = scr("wbrb", [2, 3, 1024, D], BF16)
    woutb = scr("woutb", [2, D, D], BF16)
    glub16 = scr("glub16", [2, 1024, 1024], BF16)
    sbreb = scr("sbreb", [2, 2, 32, 128, 128], BF16)
    sbimb = scr("sbimb", [2, 2, 32, 128, 128], BF16)
    screb = scr("screb", [2, 2, 32, 128, 128], BF16)
    scimb = scr("scimb", [2, 2, 32, 128, 128], BF16)
    hT = scr("hT", [D, T], F32)
    qT = scr("qT", [1024, T], BF16)
    kT = scr("kT", [1024, T], BF16)
    vtok = scr("vtok", [T, 1024], BF16)
    zT = scr("zT", [1024, T], F32)
    xbcT = scr("xbcT", [2048, T], F32)
    dttok = scr("dttok", [T, 32], F32)
    uT = scr("uT", [1024, T], F32)
    gT = scr("gT", [3 * D, T], F32)
    oattT = scr("oattT", [1024, T], BF16)
    xsT = scr("xsT", [1024, T], F32)
    xstok = scr("xstok", [T, 1024], F32)
    btok = scr("btok", [T, 512], BF16)
    BTs = scr("BTs", [512, T], BF16)
    CTs = scr("CTs", [512, T], BF16)
    y0T = scr("y0T", [1024, T], F32)
    ossdT = scr("ossdT", [1024, T], BF16)
    XS = scr("XS", [2, 2, 4096, T], BF16)
    os5T = scr("os5T", [1024, T], BF16)
    dbg_out = None
    scr_map = dict(hT=hT, qT=qT, kT=kT, vtok=vtok, zT=zT, xbcT=xbcT, dttok=dttok, uT=uT, gT=gT,
                   oattT=oattT, xsT=xsT, xstok=xstok, btok=btok, BTs=BTs, CTs=CTs, y0T=y0T,
                   ossdT=ossdT, os5T=os5T)

    with ExitStack() as es:
        P = Prog(nc, es)
        P.scopes = bool(dbg and dbg.get("scopes"))
        uid = [0]

        def mk_sb(stack, shape, dt=F32, name=None):
            uid[0] += 1
            n = f"{name or 't'}{uid[0]}"
            t = stack.enter_context(nc.sbuf_tensor(n, list(shape), dt))
            return TL(t, Buf(n))

        psum = []
        for i in range(8):
            t = es.enter_context(nc.psum_tensor(f"ps{i}", [128, 512], F32))
            psum.append(TL(t, Buf(f"ps{i}")))

        DB = P.buf

        @contextmanager
        def phase():
            with ExitStack() as ph:
                yield lambda shape, dt=F32, name=None: mk_sb(ph, shape, dt, name)
                P.barrier()

        cst = mk_sb(es, [128, 10, 128], F32, "cst")
        P.dma(cst[:], consts.ap(), writes=[cst.b])
        IDENT, ONES, TRI_LE, TRI_GE, SGT, SLT, PERM, BLK0, BLK1, TAU = range(10)
        C_ = lambda i: cst[:, i, :]
        ones_bf = mk_sb(es, [128, 128], BF16, "onesbf")
        P.copy("dve", ones_bf[:], C_(ONES), reads=[cst.b], writes=[ones_bf.b])
        modv = mk_sb(es, [128, 144, 2], F32, "modv")
        modc = mk_sb(es, [128, 3, KC, 2], F32, "modc")
        modg = mk_sb(es, [128, 3, KC, 2], F32, "modg")
        lngt = mk_sb(es, [128, 3, KC], F32, "lngt")
        lnbt = mk_sb(es, [128, 3, KC], F32, "lnbt")

        def pm_dst(t, lead, KCt, PW):
            def f(a0, an, c0, cn):
                assert c0 % PW == 0 and cn % PW == 0
                n0, nn = c0 // PW, cn // PW
                off = lead + n0 * 128 * KCt * PW + a0 * PW
                return [AP(t, off + ai * PW, [[KCt * PW, 128], [128 * KCt * PW, nn], [1, PW]]) for ai in range(an)]
            f.PW = PW
            return f

        def pm_panel(t, lead, KCt, PW, n):
            return AP(t, lead + n * 128 * KCt * PW, [[KCt * PW, 128], [1, KCt * PW]])

        def cast3(src, dst, A, C, scale=None):
            with phase() as sb:
                st = [sb([128, 4096], F32, "cst32") for _ in range(3)]
                sbf = [sb([128, 4096], BF16, "cst16") for _ in range(3)]
                it = 0
                cb = min(C, 4096)
                ab = max(1, 4096 // cb)
                for a0 in range(0, A, ab):
                    an = min(ab, A - a0)
                    for c0 in range(0, C, cb):
                        cn = min(cb, C - c0)
                        s32 = st[it % 3]
                        s16 = sbf[it % 3]
                        v32 = s32[:, 0:an * cn].rearrange("p (a c) -> p a c", a=an)
                        v16 = s16[:, 0:an * cn].rearrange("p (a c) -> p a c", a=an)
                        P.dma(v32, src[:, a0:a0 + an, c0:c0 + cn], writes=[s32.b])
                        eng = ("dve", "act")[it % 2]
                        if scale is not None:
                            P.ts("dve", v16, v32, scale, None, ALU.mult, reads=[s32.b], writes=[s16.b])
                        else:
                            P.copy(eng, v16, v32, reads=[s32.b], writes=[s16.b])
                        if callable(dst):
                            for ai, dap in enumerate(dst(a0, an, c0, cn)):
                                P.dma(dap, v16[:, ai, :].rearrange("p (n c) -> p n c", c=dst.PW), reads=[s16.b], eng="act")
                        else:
                            P.dma(dst[:, a0:a0 + an, c0:c0 + cn], v16, reads=[s16.b], eng="pool")
                        it += 1

        def cast_w(src_ap, dst_ap, R, C, scale=None):
            cast3(src_ap.rearrange("(a p) c -> p a c", p=128),
                  dst_ap if callable(dst_ap) else dst_ap.rearrange("(a p) c -> p a c", p=128), R // 128, C, scale)

        for l in range(0 if not (dbg and dbg.get('nocast')) else 2, 2):
            for j in range(2):
                cast_w(ffn_w1.ap()[l, j], w1b.ap()[l, j], D, DFF)
                cast_w(ffn_w3.ap()[l, j], w3b.ap()[l, j], D, DFF)
                cast_w(ffn_w2.ap()[l, j], w2b.ap()[l, j], DFF, D)
            cast_w(w_in.ap()[l], winb.ap()[l], D, INC)
            for j in range(3):
                cast_w(w_branch.ap()[l, j], wbrb.ap()[l, j], 1024, D)
            cast_w(w_out.ap()[l], woutb.ap()[l], D, D)
            cast_w(glu_w.ap()[l], glub16.ap()[l], 1024, 1024)
            for d in range(2):
                for (s_, d_, sc_) in ((s5bre, sbreb, None), (s5bim, sbimb, None), (s5cre, screb, None),
                                      (s5cim, scimb, -1.0)):
                    cast3(s_.ap()[l, d].rearrange("s p c -> p s c"), d_.ap()[l, d].rearrange("s p c -> p s c"),
                          32, 128, sc_)

        def load_wpanel(tile, wsrc, kc_n, c0, cn):
            P.dma(tile[:, 0:kc_n, 0:cn], wsrc.rearrange("(a p) c -> p a c", p=128)[:, :, c0:c0 + cn],
                  writes=[tile.b])

        def layer_norm(sb_tmp, hx, TB, l, j, pss1, pss2):
            sq = sb_tmp["sq"]
            for kc in range(KC):
                P.mm(pss1[:, 0:TB], C_(ONES), hx[:, kc, 0:TB], kc == 0, kc == KC - 1,
                     reads=[cst.b, hx.b], writes=[pss1.b])
            for kc in range(KC):
                s = sq[kc % 2]
                P.act(s[:, 0:TB], hx[:, kc, 0:TB], AF.Square, reads=[hx.b], writes=[s.b])
                P.mm(pss2[:, 0:TB], C_(ONES), s[:, 0:TB], kc == 0, kc == KC - 1,
                     reads=[cst.b, s.b], writes=[pss2.b])
            mean, rstd, nmr = sb_tmp["mean"], sb_tmp["rstd"], sb_tmp["nmr"]
            P.ts("dve", mean[:, 0:TB], pss1[:, 0:TB], 1.0 / D, None, ALU.mult, reads=[pss1.b], writes=[mean.b])
            P.tt("dve", nmr[:, 0:TB], mean[:, 0:TB], mean[:, 0:TB], ALU.mult, reads=[mean.b], writes=[nmr.b])
            P.stt("dve", rstd[:, 0:TB], pss2[:, 0:TB], 1.0 / D, nmr[:, 0:TB], ALU.mult, ALU.subtract,
                  reads=[pss2.b, nmr.b], writes=[rstd.b])
            P.act(rstd[:, 0:TB], rstd[:, 0:TB], AF.Ln, bias=kct[:, 0:1], reads=[rstd.b, kct.b], writes=[rstd.b])
            P.act(rstd[:, 0:TB], rstd[:, 0:TB], AF.Exp, scale=-0.5, reads=[rstd.b], writes=[rstd.b])
            P.stt("dve", nmr[:, 0:TB], mean[:, 0:TB], -1.0, rstd[:, 0:TB], ALU.mult, ALU.mult,
                  reads=[mean.b, rstd.b], writes=[nmr.b])
            for kc in range(KC):
                e = P.ve()
                P.tt(e, hx[:, kc, 0:TB], hx[:, kc, 0:TB], rstd[:, 0:TB], ALU.mult, reads=[hx.b, rstd.b], writes=[hx.b])
                P.tt(e, hx[:, kc, 0:TB], hx[:, kc, 0:TB], nmr[:, 0:TB], ALU.add, reads=[hx.b, nmr.b], writes=[hx.b])
                P.act(hx[:, kc, 0:TB], hx[:, kc, 0:TB], AF.Identity, bias=lnbt[:, j, kc:kc + 1],
                      scale=lngt[:, j, kc:kc + 1], reads=[hx.b, lngt.b, lnbt.b], writes=[hx.b])

        def hview(dr, t0, TB):
            return dr.ap().rearrange("(a p) t -> p a t", p=128)[:, :, t0:t0 + TB]

        def ln_tmps(sb):
            return dict(sq=[sb([128, 512], F32, "sq") for _ in range(2)], mean=sb([128, 512], F32, "mean"),
                        rstd=sb([128, 512], F32, "rstd"), nmr=sb([128, 512], F32, "nmr"))

        def mod_phase(l):
            with phase() as sb:
                cv = sb([128, KC, 2], F32, "cv")
                sc = sb([128, KC, 2], F32, "sc")
                bm = sb([128, 144], F32, "bm")
                P.dma(cv[:], cvec.ap(), writes=[cv.b])
                P.dma(bm[:], bmod.ap()[l], writes=[bm.b])
                P.dma(lngt[:], lng.ap()[l], writes=[lngt.b])
                P.dma(lnbt[:], lnb.ap()[l], writes=[lnbt.b])
                P.act(sc[:], cv[:], AF.Silu, reads=[cv.b], writes=[sc.b])
                wp = [sb([128, KC, 512], F32, "wmod") for _ in range(2)]
                wv = w_mod.ap()[l].rearrange("(a p) c -> p a c", p=128)
                for cbk in range(36):
                    w = wp[cbk % 2]
                    P.dma(w[:], wv[:, :, cbk * 512:(cbk + 1) * 512], writes=[w.b])
                    pm = psum[cbk % 2]
                    for m in range(4):
                        for kc in range(KC):
                            P.mm(pm[:, 2 * m:2 * m + 2], w[:, kc, m * 128:(m + 1) * 128], sc[:, kc, :],
                                 kc == 0, kc == KC - 1, reads=[w.b, sc.b], writes=[pm.b])
                    for m in range(4):
                        mt = cbk * 4 + m
                        P.ts("dve", modv[:, mt, :], pm[:, 2 * m:2 * m + 2], bm[:, mt:mt + 1], None, ALU.add,
                             reads=[pm.b, bm.b], writes=[modv.b])
                mv = modv[:].rearrange("p (j r k) w -> p j r k w", j=3, r=3)
                P.ts("dve", modc[:], mv[:, :, 1, :, :], 1.0, None, ALU.add, reads=[modv.b], writes=[modc.b])
                for j in range(3):
                    P.ts("dve", modg[:, j], mv[:, j, 2, :, :], 0.5 if j != 1 else 1.0, None, ALU.mult,
                         reads=[modv.b], writes=[modg.b])
            return

        def mshift(j, kc, which):
            return modv[:, (3 * j) * KC + kc, which:which + 1]

        def modulate(xm, hx, j, TB, which):
            for kc in range(KC):
                P.act(xm[:, kc, 0:TB], hx[:, kc, 0:TB], AF.Identity, bias=mshift(j, kc, which),
                      scale=modc[:, j, kc, which:which + 1], reads=[hx.b, modv.b, modc.b], writes=[xm.b])

        def ffn_phase(l, jj, src, dst, skip_ctx, to_out):
            j = 0 if jj == 0 else 2
            W1 = w1b.ap()[l, jj]
            W3 = w3b.ap()[l, jj]
            W2 = w2b.ap()[l, jj]
            with phase() as sb:
                hx = sb([128, KC, 512], F32, "hx")
                xm = sb([128, KC, 512], BF16, "xm")
                g = sb([128, FC, 512], BF16, "g")
                w1p = [sb([128, KC, 256], BF16, "w1p") for _ in range(2)]
                w3p = [sb([128, KC, 256], BF16, "w3p") for _ in range(2)]
                w2p = [sb([128, FC, 128], BF16, "w2p") for _ in range(2)]
                sa = [sb([128, 512], F32, "sa") for _ in range(2)]
                lt = ln_tmps(sb)
                for (t0, TB, which) in blocks:
                    if skip_ctx and which == 1:
                        continue
                    P.dma(hx[:, :, 0:TB], hview(src, t0, TB), reads=[DB(tname[id(src)])], writes=[hx.b])
                    modulate(xm, hx, j, TB, which)
                    for kc in range(KC):
                        P.ts(P.ve(), hx[:, kc, 0:TB], hx[:, kc, 0:TB], ALPHA, None, ALU.mult, reads=[hx.b], writes=[hx.b])
                    for fp in range(FC // 2):
                        a, b = w1p[fp % 2], w3p[fp % 2]
                        load_wpanel(a, W1, KC, fp * 256, 256)
                        load_wpanel(b, W3, KC, fp * 256, 256)
                        for f2 in range(2):
                            f = fp * 2 + f2
                            pa, pb = psum[(f % 2) * 2], psum[(f % 2) * 2 + 1]
                            for kc in range(KC):
                                P.mm(pa[:, 0:TB], a[:, kc, f2 * 128:(f2 + 1) * 128], xm[:, kc, 0:TB], kc == 0, kc == KC - 1,
                                     reads=[a.b, xm.b], writes=[pa.b])
                            for kc in range(KC):
                                P.mm(pb[:, 0:TB], b[:, kc, f2 * 128:(f2 + 1) * 128], xm[:, kc, 0:TB], kc == 0, kc == KC - 1,
                                     reads=[b.b, xm.b], writes=[pb.b])
                            s = sa[f % 2]
                            P.act(s[:, 0:TB], pa[:, 0:TB], AF.Silu, reads=[pa.b], writes=[s.b])
                            P.tt("dve", g[:, f, 0:TB], s[:, 0:TB], pb[:, 0:TB], ALU.mult, reads=[s.b, pb.b], writes=[g.b])
                    for m in range(KC):
                        w = w2p[m % 2]
                        load_wpanel(w, W2, FC, m * 128, 128)
                        py = psum[4 + m % 2]
                        for kc in range(FC):
                            P.mm(py[:, 0:TB], w[:, kc, :], g[:, kc, 0:TB], kc == 0, kc == FC - 1,
                                 reads=[w.b, g.b], writes=[py.b])
                        P.stt("dve", hx[:, m, 0:TB], py[:, 0:TB], modg[:, j, m, which:which + 1], hx[:, m, 0:TB],
                              ALU.mult, ALU.add, reads=[py.b, modg.b, hx.b], writes=[hx.b])
                    layer_norm(lt, hx, TB, l, j, psum[6], psum[7])
                    if to_out:
                        P.dma(yT.ap().rearrange("(a p) t -> p a t", p=128)[:, :, t0 - NCTX:t0 - NCTX + TB],
                              hx[:, :, 0:TB], reads=[hx.b], eng="pool", is_out=True)
                    else:
                        P.dma(hview(dst, t0, TB), hx[:, :, 0:TB], reads=[hx.b], writes=[DB(tname[id(dst)])], eng="pool")

        def inproj_phase(l):
            W = winb.ap()[l]
            fm_specs = [(0, qT, BF16, True), (1024, kT, BF16, True), (3072, zT, F32, False),
                        (4096, xbcT, F32, False), (4096 + 1024, xbcT, F32, False), (6176, uT, F32, False)] + \
                       [(7200 + 1024 * i, gT, F32, False) for i in range(6)]
            with phase() as sb:
                hx = sb([128, KC, 512], F32, "hx")
                xm = sb([128, KC, 512], BF16, "xm")
                wp = [sb([128, KC, 512], BF16, "wp") for _ in range(2)]
                st32 = [sb([128, 4, 512], F32, "st32") for _ in range(2)]
                st16 = [sb([128, 4, 512], BF16, "st16") for _ in range(2)]
                qf = [sb([128, 512], F32, "qf") for _ in range(2)]
                rc = sb([128, 512], F32, "rc")
                rs = sb([128, 512], F32, "rs")
                vst = [sb([128, 1024], BF16, "vst") for _ in range(2)]
                dst_ = [sb([128, 32], F32, "dst") for _ in range(2)]
                wdt = sb([128, KC, 32], BF16, "wdt")
                it = 0
                for (t0, TB, which) in blocks:
                    P.dma(hx[:, :, 0:TB], hview(hT, t0, TB), reads=[DB("hT")], writes=[hx.b])
                    modulate(xm, hx, 1, TB, which)
                    if which == 0:
                        P.dma(rc[:, 0:TB], ropec.ap()[:, t0 - NCTX:t0 - NCTX + TB], writes=[rc.b])
                        P.dma(rs[:, 0:TB], ropes.ap()[:, t0 - NCTX:t0 - NCTX + TB], writes=[rs.b])
                    for si, (c0, dstT, dt, rope) in enumerate(fm_specs):
                        for pn in range(2):
                            w = wp[it % 2]
                            cc = c0 + pn * 512
                            load_wpanel(w, W, KC, cc, 512)
                            stg = (st16 if dt == BF16 else st32)[it % 2]
                            for m in range(4):
                                pm = psum[(it * 4 + m) % 4]
                                for kc in range(KC):
                                    P.mm(pm[:, 0:TB], w[:, kc, m * 128:(m + 1) * 128], xm[:, kc, 0:TB], kc == 0, kc == KC - 1,
                                         reads=[w.b, xm.b], writes=[pm.b])
                                if rope and which == 0:
                                    q_ = qf[m % 2]
                                    P.copy("act", q_[:, 0:TB], pm[:, 0:TB], reads=[pm.b], writes=[q_.b])
                                    pr = psum[4 + m % 2]
                                    P.mm(pr[:, 0:TB], C_(PERM), q_[:, 0:TB], True, True, reads=[cst.b, q_.b], writes=[pr.b])
                                    P.tt("dve", q_[:, 0:TB], q_[:, 0:TB], rc[:, 0:TB], ALU.mult, reads=[q_.b, rc.b], writes=[q_.b])
                                    P.stt("dve", stg[:, m, 0:TB], pr[:, 0:TB], 1.0, rs[:, 0:TB], ALU.mult, ALU.mult,
                                          reads=[pr.b, rs.b], writes=[stg.b])
                                    P.tt("dve", stg[:, m, 0:TB], stg[:, m, 0:TB], q_[:, 0:TB], ALU.add,
                                         reads=[stg.b, q_.b], writes=[stg.b])
                                else:
                                    P.copy("act" if m % 2 else "dve", stg[:, m, 0:TB], pm[:, 0:TB], reads=[pm.b], writes=[stg.b])
                            rows = (cc - c0) + (0 if dstT not in (xbcT, gT) else (c0 - (4096 if dstT is xbcT else 7200)))
                            dv = dstT.ap().rearrange("(a p) t -> p a t", p=128)[:, rows // 128:rows // 128 + 4, t0:t0 + TB]
                            P.dma(dv, stg[:, :, 0:TB], reads=[stg.b], writes=[DB(tname[id(dstT)])], eng="pool")
                            it += 1
                    wv_ = [wp[0], wp[1]]
                    load_wpanel(wv_[0], W, KC, 2048, 512)
                    load_wpanel(wv_[1], W, KC, 2560, 512)
                    P.dma(wdt[:], W.rearrange("(a p) c -> p a c", p=128)[:, :, 6144:6176], writes=[wdt.b])
                    wdtv = wdt
                    for tt_ in range(TB // 128):
                        vs = vst[tt_ % 2]
                        for hh in range(2):
                            pm = psum[(tt_ * 2 + hh) % 4]
                            for kc in range(KC):
                                P.mm(pm[:, :], xm[:, kc, tt_ * 128:(tt_ + 1) * 128], wv_[hh][:, kc, :], kc == 0, kc == KC - 1,
                                     reads=[xm.b, wv_[hh].b], writes=[pm.b])
                            P.copy("act" if hh else "dve", vs[:, hh * 512:(hh + 1) * 512], pm[:, :], reads=[pm.b], writes=[vs.b])
                        P.dma(vtok.ap()[t0 + tt_ * 128:t0 + (tt_ + 1) * 128, :], vs[:], reads=[vs.b], writes=[DB("vtok")], eng="pool")
                        pd = psum[4 + tt_ % 2]
                        for kc in range(KC):
                            P.mm(pd[:, 0:32], xm[:, kc, tt_ * 128:(tt_ + 1) * 128], wdtv[:, kc, :], kc == 0, kc == KC - 1,
                                 reads=[xm.b, wdt.b], writes=[pd.b])
                        ds = dst_[tt_ % 2]
                        P.copy("dve", ds[:], pd[:, 0:32], reads=[pd.b], writes=[ds.b])
                        P.dma(dttok.ap()[t0 + tt_ * 128:t0 + (tt_ + 1) * 128, :], ds[:], reads=[ds.b], writes=[DB("dttok")], eng="pool")

        def att_phase(l, last):
            lam_init = 0.8 - 0.6 * math.exp(-0.3 * l)
            with phase() as sb:
                al = sb([128, 256], F32, "al")
                sw = sb([128, 1], F32, "sw")
                sm = sb([128, 8], F32, "sm")
                P.dma(al[:], attlam.ap()[l], writes=[al.b])
                P.dma(sw[:], subln.ap()[l], writes=[sw.b])
                pr_ = sb([128, 128], F32, "pr_")
                P.tt("dve", pr_[:, 0:64], al[:, 0:64], al[:, 64:128], ALU.mult, reads=[al.b], writes=[pr_.b])
                P.tt("dve", pr_[:, 64:128], al[:, 128:192], al[:, 192:256], ALU.mult, reads=[al.b], writes=[pr_.b])
                P.emit("dve", lambda e: e.tensor_reduce(sm[:, 0:2], pr_[:].rearrange("p (a b) -> p a b", a=2), AX.X, ALU.add),
                       reads=[pr_.b], writes=[sm.b])
                P.act(sm[:, 2:4], sm[:, 0:2], AF.Exp, reads=[sm.b], writes=[sm.b])
                P.tt("dve", sm[:, 4:5], sm[:, 3:4], sm[:, 2:3], ALU.subtract, reads=[sm.b], writes=[sm.b])
                P.ts("dve", sm[:, 5:6], sm[:, 4:5], -lam_init, None, ALU.add, reads=[sm.b], writes=[sm.b])
                P.ts("dve", sm[:, 6:7], sw[:, 0:1], 1.0 - lam_init, None, ALU.mult, reads=[sw.b], writes=[sm.b])
                neglam = sm[:, 5:6]
                swl = sm[:, 6:7]
                qh = [sb([128, T], BF16, "qh") for _ in range(2)]
                kh = [sb([128, T], BF16, "kh") for _ in range(2)]
                vh = [sb([128, NCH, 128], BF16, "vh") for _ in range(2)]
                sqt = [sb([128, 512], F32, "sqt") for _ in range(2)]
                mx = sb([128, 16], F32, "mx")
                negc = sb([128, 2], F32, "negc")
                pt = [sb([128, 512], BF16, "pt") for _ in range(4)]
                o0 = sb([128, 512], F32, "o0")
                o1 = sb([128, 512], F32, "o1")
                r0 = sb([128, 512], F32, "r0")
                ob = [sb([128, 512], BF16, "ob") for _ in range(2)]
                for h in range(8):
                    q_, k_, v_ = qh[h % 2], kh[h % 2], vh[h % 2]
                    P.dma(q_[:], qT.ap()[h * 128:(h + 1) * 128, :], reads=[DB("qT")], writes=[q_.b])
                    P.dma(k_[:], kT.ap()[h * 128:(h + 1) * 128, :], reads=[DB("kT")], writes=[k_.b])
                    P.dma(v_[:], vtok.ap().rearrange("(a p) c -> p a c", p=128)[:, :, h * 128:(h + 1) * 128],
                          reads=[DB("vtok")], writes=[v_.b])
                    P.memset("dve", mx[:], 0.0, writes=[mx.b])
                    it = 0
                    for qi, src_ in enumerate((q_, k_)):
                        for (t0, TB, which) in blocks:
                            s = sqt[it % 2]
                            P.act(s[:, 0:TB], src_[:, t0:t0 + TB], AF.Square, reads=[src_.b], writes=[s.b])
                            for jm in range(2):
                                pm = psum[(it * 2 + jm) % 4]
                                P.mm(pm[:, 0:TB], C_(BLK0 + jm), s[:, 0:TB], True, True, reads=[cst.b, s.b], writes=[pm.b])
                                col = 8 + qi * 2 + jm
                                P.emit("dve", lambda e, pm=pm, TB=TB, col=col: e.tensor_reduce(mx[:, col:col + 1], pm[:, 0:TB], AX.X, ALU.max),
                                       reads=[pm.b], writes=[mx.b])
                                c2 = qi * 2 + jm
                                P.tt("dve", mx[:, c2:c2 + 1], mx[:, c2:c2 + 1], mx[:, col:col + 1], ALU.max, reads=[mx.b], writes=[mx.b])
                            it += 1
                    P.tt("dve", negc[:], mx[:, 0:2], mx[:, 2:4], ALU.mult, reads=[mx.b], writes=[negc.b])
                    P.act(negc[:], negc[:], AF.Ln, reads=[negc.b], writes=[negc.b])
                    P.act(negc[:], negc[:], AF.Exp, scale=0.5, reads=[negc.b], writes=[negc.b])
                    P.ts("dve", negc[:], negc[:], -0.125, None, ALU.mult, reads=[negc.b], writes=[negc.b])
                    for bi, (t0, TB, which) in enumerate(blocks):
                        if which == 1 and last:
                            continue
                        nk = 2 if which == 1 else NCH
                        pO = (psum[4], psum[5])
                        pS = (psum[6], psum[7])
                        def qk(kt):
                            for jm in range(2):
                                pm = psum[(kt % 2) * 2 + jm]
                                P.mm(pm[:, 0:TB], k_[jm * 64:(jm + 1) * 64, kt * 128:(kt + 1) * 128],
                                     q_[jm * 64:(jm + 1) * 64, t0:t0 + TB], True, True, reads=[k_.b, q_.b], writes=[pm.b])

                        def rest(kt):
                            for jm in range(2):
                                pm = psum[(kt % 2) * 2 + jm]
                                p_ = pt[(kt % 2) * 2 + jm]
                                P.act(p_[:, 0:TB], pm[:, 0:TB], AF.Exp, bias=negc[:, jm:jm + 1], scale=0.125,
                                      reads=[pm.b, negc.b], writes=[p_.b])
                            for jm in range(2):
                                p_ = pt[(kt % 2) * 2 + jm]
                                P.mm(pO[jm][:, 0:TB], v_[:, kt, :], p_[:, 0:TB], kt == 0, kt == nk - 1,
                                     reads=[v_.b, p_.b], writes=[pO[jm].b])
                                P.mm(pS[jm][:, 0:TB], ones_bf[:], p_[:, 0:TB], kt == 0, kt == nk - 1,
                                     reads=[ones_bf.b, p_.b], writes=[pS[jm].b])

                        qk(0)
                        for kt in range(nk):
                            if kt + 1 < nk:
                                qk(kt + 1)
                            rest(kt)
                        P.emit("dve", lambda e, TB=TB: e.reciprocal(r0[:, 0:TB], pS[0][:, 0:TB]), reads=[pS[0].b], writes=[r0.b])
                        P.tt("dve", o0[:, 0:TB], pO[0][:, 0:TB], r0[:, 0:TB], ALU.mult, reads=[pO[0].b, r0.b], writes=[o0.b])
                        P.emit("dve", lambda e, TB=TB: e.reciprocal(r0[:, 0:TB], pS[1][:, 0:TB]), reads=[pS[1].b], writes=[r0.b])
                        P.tt("dve", o1[:, 0:TB], pO[1][:, 0:TB], r0[:, 0:TB], ALU.mult, reads=[pO[1].b, r0.b], writes=[o1.b])
                        P.stt("dve", o0[:, 0:TB], o1[:, 0:TB], neglam, o0[:, 0:TB], ALU.mult, ALU.add,
                              reads=[o1.b, o0.b, sm.b], writes=[o0.b])
                        P.act(o1[:, 0:TB], o0[:, 0:TB], AF.Square, reads=[o0.b], writes=[o1.b])
                        pm = psum[0]
                        P.mm(pm[:, 0:TB], C_(ONES), o1[:, 0:TB], True, True, reads=[cst.b, o1.b], writes=[pm.b])
                        P.act(r0[:, 0:TB], pm[:, 0:TB], AF.Ln, bias=kct[:, 1:2], scale=1.0 / 128, reads=[pm.b, kct.b], writes=[r0.b])
                        P.act(r0[:, 0:TB], r0[:, 0:TB], AF.Exp, scale=-0.5, reads=[r0.b], writes=[r0.b])
                        o_ = ob[bi % 2]
                        P.stt("dve", o_[:, 0:TB], o0[:, 0:TB], swl, r0[:, 0:TB], ALU.mult, ALU.mult,
                              reads=[o0.b, sm.b, r0.b], writes=[o_.b])
                        P.dma(oattT.ap()[h * 128:(h + 1) * 128, t0:t0 + TB], o_[:, 0:TB], reads=[o_.b],
                              writes=[DB("oattT")], eng="pool")

        def conv_phase(l):
            with phase() as sb:
                cw = sb([128, 16, 5], F32, "cw")
                cb = sb([128, 16], F32, "cb")
                P.dma(cw[:], convw.ap()[l], writes=[cw.b])
                P.dma(cb[:], convb.ap()[l], writes=[cb.b])
                xp = [sb([128, T + 8], F32, "xp") for _ in range(2)]
                acc = [sb([128, T], F32, "acc") for _ in range(2)]
                o16 = [sb([128, T], BF16, "o16") for _ in range(2)]
                tk32 = [sb([128, NCH, 128], F32, "tk32") for _ in range(1)]
                tk16 = [sb([128, NCH, 128], BF16, "tk16") for _ in range(1)]
                for x_ in xp:
                    P.memset("dve", x_[:], 0.0, writes=[x_.b])
                segs = [(0, NCTX, 2), (NCTX, NX, 6)]
                for ct in range(16):
                    x_, a_ = xp[ct % 2], acc[ct % 2]
                    for (s0, sl, off) in segs:
                        P.dma(x_[:, off + s0:off + s0 + sl], xbcT.ap()[ct * 128:(ct + 1) * 128, s0:s0 + sl],
                              reads=[DB("xbcT")], writes=[x_.b])
                    for (s0, sl, off) in segs:
                        e = "dve"
                        base = off + s0 - 2
                        P.ts(e, a_[:, s0:s0 + sl], x_[:, base:base + sl], cw[:, ct, 0:1], None, ALU.mult,
                             reads=[x_.b, cw.b], writes=[a_.b])
                        for k in range(1, 5):
                            P.stt(e, a_[:, s0:s0 + sl], x_[:, base + k:base + k + sl], cw[:, ct, k:k + 1], a_[:, s0:s0 + sl],
                                  ALU.mult, ALU.add, reads=[x_.b, cw.b, a_.b], writes=[a_.b])
                    P.act(a_[:], a_[:], AF.Silu, bias=cb[:, ct:ct + 1], reads=[a_.b, cb.b], writes=[a_.b])
                    if ct < 8:
                        P.dma(xsT.ap()[ct * 128:(ct + 1) * 128, :], a_[:], reads=[a_.b], writes=[DB("xsT")], eng="pool")
                        tk = tk32[0]
                        for c4 in range(0, NCH, 4):
                            n4 = min(4, NCH - c4)
                            pm = psum[(c4 // 4) % 4]
                            for i in range(n4):
                                P.tr(pm[:, i * 128:(i + 1) * 128], a_[:, (c4 + i) * 128:(c4 + i + 1) * 128], C_(IDENT),
                                     reads=[a_.b, cst.b], writes=[pm.b])
                            P.copy("act" if (c4 // 4) % 2 else "dve", tk[:, c4:c4 + n4, :],
                                   pm[:, 0:n4 * 128].rearrange("p (a c) -> p a c", a=n4), reads=[pm.b], writes=[tk.b])
                        P.dma(xstok.ap().rearrange("(a p) c -> p a c", p=128)[:, :, ct * 128:(ct + 1) * 128], tk[:],
                              reads=[tk.b], writes=[DB("xstok")], eng="pool")
                    else:
                        o_ = o16[ct % 2]
                        P.copy("act", o_[:], a_[:], reads=[a_.b], writes=[o_.b])
                        if ct < 12:
                            g_ = ct - 8
                            P.dma(BTs.ap()[g_ * 128:(g_ + 1) * 128, :], o_[:], reads=[o_.b], writes=[DB("BTs")], eng="pool")
                            tk = tk16[0]
                            for c4 in range(0, NCH, 4):
                                n4 = min(4, NCH - c4)
                                pm = psum[(c4 // 4) % 4]
                                for i in range(n4):
                                    P.tr(pm[:, i * 128:(i + 1) * 128], a_[:, (c4 + i) * 128:(c4 + i + 1) * 128], C_(IDENT),
                                         reads=[a_.b, cst.b], writes=[pm.b])
                                P.copy("act" if (c4 // 4) % 2 else "dve", tk[:, c4:c4 + n4, :],
                                       pm[:, 0:n4 * 128].rearrange("p (a c) -> p a c", a=n4), reads=[pm.b], writes=[tk.b])
                            P.dma(btok.ap().rearrange("(a p) c -> p a c", p=128)[:, :, g_ * 128:(g_ + 1) * 128], tk[:],
                                  reads=[tk.b], writes=[DB("btok")], eng="pool")
                        else:
                            g_ = ct - 12
                            P.dma(CTs.ap()[g_ * 128:(g_ + 1) * 128, :], o_[:], reads=[o_.b], writes=[DB("CTs")], eng="pool")

        def ssd_phase(l, last):
            nctx_ch = NCTX // 128
            with phase() as sb:
                al_ = sb([128, 32], F32, "al_")
                db_ = sb([128, 32], F32, "db_")
                A_ = sb([128, 32], F32, "A_")
                Dt = sb([128, 8], F32, "Dt")
                nw = sb([128, 8], F32, "nw")
                P.dma(al_[:], alog.ap()[l], writes=[al_.b])
                P.dma(db_[:], dtb.ap()[l], writes=[db_.b])
                P.dma(Dt[:], ssdD.ap()[l], writes=[Dt.b])
                P.dma(nw[:], ssdnw.ap()[l], writes=[nw.b])
                P.act(A_[:], al_[:], AF.Exp, reads=[al_.b], writes=[A_.b])
                P.ts("dve", A_[:], A_[:], -1.0, None, ALU.mult, reads=[A_.b], writes=[A_.b])
                H = sb([128, 1024], F32, "H")
                Hb = sb([128, 1024], BF16, "Hb")
                xs = [sb([128, 1024], F32, "xs") for _ in range(2)]
                bt = [sb([128, 512], BF16, "bt") for _ in range(2)]
                BT = [sb([128, 4, 128], BF16, "BT") for _ in range(2)]
                CT = [sb([128, 4, 128], BF16, "CT") for _ in range(2)]
                dr = [sb([128, 32], F32, "dr") for _ in range(2)]
                sm = sb([128, 8, 32], F32, "sm")
                xdt = sb([128, 1024], BF16, "xdt")
                xde = sb([128, 1024], BF16, "xde")
                Gm = sb([128, 512], F32, "Gm")
                rseg = sb([128, 16, 128], F32, "rseg")
                dcy = [sb([128, 512], F32, "dcy") for _ in range(2)]
                ecs = [sb([128, 512], F32, "ecs") for _ in range(2)]
                Mt = [sb([128, 4, 128], BF16, "Mt") for _ in range(2)]
                Cp = [sb([128, 4, 128], BF16, "Cp") for _ in range(2)]
                ysb = sb([128, 8, 128], F32, "ysb")
                y0 = sb([128, 8, 128], F32, "y0")
                xf = sb([128, 8, 128], F32, "xf")
                zf = sb([128, 8, 128], F32, "zf")
                sq = sb([128, 128], F32, "sq")
                rst = sb([128, 4, 128], F32, "rst")
                ob = sb([128, 8, 128], BF16, "ob")
                for d in range(2):
                    if d == 0:
                        order = list(range(NCH))
                    else:
                        order = list(range(nctx_ch - 1, -1, -1)) + list(range(NCH - 1, nctx_ch - 1, -1))
                    LM = C_(TRI_LE if d == 0 else TRI_GE)
                    U = C_(SGT if d == 0 else SLT)
                    MASK = LM
                    P.memset("dve", H[:], 0.0, writes=[H.b])
                    for oi, c in enumerate(order):
                        if last and d == 1 and c < nctx_ch and False:
                            pass
                        t0 = c * 128
                        x_, b_, B_, C2, d_ = xs[oi % 2], bt[oi % 2], BT[oi % 2], CT[oi % 2], dr[oi % 2]
                        P.dma(x_[:], xstok.ap()[t0:t0 + 128, :], reads=[DB("xstok")], writes=[x_.b])
                        P.dma(b_[:], btok.ap()[t0:t0 + 128, :], reads=[DB("btok")], writes=[b_.b])
                        P.dma(B_[:], BTs.ap().rearrange("(g n) t -> n g t", g=4)[:, :, t0:t0 + 128], reads=[DB("BTs")], writes=[B_.b])
                        P.dma(C2[:], CTs.ap().rearrange("(g n) t -> n g t", g=4)[:, :, t0:t0 + 128], reads=[DB("CTs")], writes=[C2.b])
                        P.dma(d_[:], dttok.ap()[t0:t0 + 128, :], reads=[DB("dttok")], writes=[d_.b])
                        xx, ax, ex, ln_, dtp, adt, toe, w2_ = (sm[:, i, :] for i in range(8))
                        P.tt("dve", xx, d_[:], db_[:], ALU.add, reads=[d_.b, db_.b], writes=[sm.b])
                        P.stt("dve", ax, xx, -1.0, xx, ALU.mult, ALU.max, reads=[sm.b], writes=[sm.b])
                        P.act(ex, ax, AF.Exp, scale=-1.0, reads=[sm.b], writes=[sm.b])
                        P.act(ln_, ex, AF.Ln, bias=kct[:, 2:3], reads=[sm.b, kct.b], writes=[sm.b])
                        P.stt("dve", dtp, xx, 0.0, ln_, ALU.max, ALU.add, reads=[sm.b], writes=[sm.b])
                        P.tt("dve", adt, dtp, A_[:], ALU.mult, reads=[sm.b, A_.b], writes=[sm.b])
                        dsl = slice(d * 16, (d + 1) * 16)
                        pe_ = psum[0]
                        P.mm(pe_[:, 0:16], U, adt[:, dsl], True, True, reads=[cst.b, sm.b], writes=[pe_.b])
                        P.mm(pe_[:, 16:32], C_(ONES), adt[:, dsl], True, True, reads=[cst.b, sm.b], writes=[pe_.b])
                        P.act(toe, pe_[:, 0:32], AF.Exp, reads=[pe_.b], writes=[sm.b])
                        P.tt("dve", w2_[:, 0:16], dtp[:, dsl], toe[:, 0:16], ALU.mult, reads=[sm.b], writes=[sm.b])
                        x3 = x_[:].rearrange("p (h q) -> p h q", h=16)
                        P.tt("dve", xdt[:].rearrange("p (h q) -> p h q", h=16), x3,
                             dtp[:, dsl].unsqueeze(2).to_broadcast([128, 16, 64]), ALU.mult, reads=[x_.b, sm.b], writes=[xdt.b])
                        P.tt("dve", xde[:].rearrange("p (h q) -> p h q", h=16), x3,
                             w2_[:, 0:16].unsqueeze(2).to_broadcast([128, 16, 64]), ALU.mult, reads=[x_.b, sm.b], writes=[xde.b])
                        for g_ in range(4):
                            pS_ = psum[1 + g_ // 2]
                            P.mm(pS_[:, (g_ % 2) * 256:(g_ % 2 + 1) * 256], b_[:, g_ * 128:(g_ + 1) * 128],
                                 xde[:, g_ * 256:(g_ + 1) * 256], True, True, reads=[b_.b, xde.b], writes=[pS_.b])
                        P.copy("act", Hb[:], H[:], reads=[H.b], writes=[Hb.b])
                        pG = psum[3]
                        for g_ in range(4):
                            P.mm(pG[:, g_ * 128:(g_ + 1) * 128], B_[:, g_, :], C2[:, g_, :], True, True,
                                 reads=[B_.b, C2.b], writes=[pG.b])
                        P.tt("dve", Gm[:].rearrange("p (g l) -> p g l", g=4), pG[:].rearrange("p (g l) -> p g l", g=4),
                             MASK.unsqueeze(1).to_broadcast([128, 4, 128]), ALU.mult, reads=[pG.b, cst.b], writes=[Gm.b])
                        P.tt("dve", rseg[:], LM.unsqueeze(1).to_broadcast([128, 16, 128]),
                             adt[:, dsl].unsqueeze(2).to_broadcast([128, 16, 128]), ALU.mult, reads=[cst.b, sm.b], writes=[rseg.b])
                        for g_ in range(4):
                            rr_ = rseg[:, 4 * g_:4 * g_ + 4, :].rearrange("p h l -> p (h l)")
                            pseg = psum[4 + g_ % 2]
                            pcs = psum[6 + g_ % 2]
                            P.mm(pseg[:], U, rr_, True, True, reads=[cst.b, rseg.b], writes=[pseg.b])
                            P.mm(pcs[:], C_(ONES), rr_, True, True, reads=[cst.b, rseg.b], writes=[pcs.b])
                            dc, ec, M_, Cq = dcy[g_ % 2], ecs[g_ % 2], Mt[g_ % 2], Cp[g_ % 2]
                            P.act(dc[:], pseg[:], AF.Exp, reads=[pseg.b], writes=[dc.b])
                            P.act(ec[:], pcs[:], AF.Exp, reads=[pcs.b], writes=[ec.b])
                            P.tt("dve", M_[:], dc[:].rearrange("p (h l) -> p h l", h=4),
                                 Gm[:, g_ * 128:(g_ + 1) * 128].unsqueeze(1).to_broadcast([128, 4, 128]), ALU.mult,
                                 reads=[dc.b, Gm.b], writes=[M_.b])
                            P.tt("dve", Cq[:], ec[:].rearrange("p (h l) -> p h l", h=4),
                                 C2[:, g_, :].unsqueeze(1).to_broadcast([128, 4, 128]), ALU.mult,
                                 reads=[ec.b, C2.b], writes=[Cq.b])
                            pY = psum[0] if g_ < 2 else psum[3]
                            for e_ in range(4):
                                hd = g_ * 4 + e_
                                hp = hd // 2
                                jj_ = hd % 2
                                oo = pY[jj_ * 64:(jj_ + 1) * 64, (hp % 4) * 128:(hp % 4 + 1) * 128]
                                P.mm(oo, xdt[:, hd * 64:(hd + 1) * 64], M_[:, e_, :], True, False,
                                     reads=[xdt.b, M_.b], writes=[pY.b])
                                P.mm(oo, Hb[:, hd * 64:(hd + 1) * 64], Cq[:, e_, :], False, True,
                                     reads=[Hb.b, Cq.b], writes=[pY.b])
                            if g_ % 2 == 1:
                                hp0 = (g_ - 1) * 2
                                P.copy("act", ysb[:, hp0:hp0 + 4, :], pY[:].rearrange("p (a l) -> p a l", a=4),
                                       reads=[pY.b], writes=[ysb.b])
                        P.tt("dve", H[:].rearrange("p (h q) -> p h q", h=16), H[:].rearrange("p (h q) -> p h q", h=16),
                             toe[:, 16:32].unsqueeze(2).to_broadcast([128, 16, 64]), ALU.mult, reads=[H.b, sm.b], writes=[H.b])
                        P.tt("dve", H[:, 0:512], H[:, 0:512], psum[1][:], ALU.add, reads=[H.b, psum[1].b], writes=[H.b])
                        P.tt("dve", H[:, 512:1024], H[:, 512:1024], psum[2][:], ALU.add, reads=[H.b, psum[2].b], writes=[H.b])
                        yv = y0T.ap().rearrange("(a p) t -> p a t", p=128)[:, :, t0:t0 + 128]
                        if d == 0:
                            P.dma(yv, ysb[:], reads=[ysb.b], writes=[DB("y0T")], eng="pool")
                        else:
                            if last and c < nctx_ch:
                                continue
                            P.dma(y0[:], yv, reads=[DB("y0T")], writes=[y0.b])
                            P.dma(xf[:], xsT.ap().rearrange("(a p) t -> p a t", p=128)[:, :, t0:t0 + 128], reads=[DB("xsT")], writes=[xf.b])
                            P.dma(zf[:], zT.ap().rearrange("(a p) t -> p a t", p=128)[:, :, t0:t0 + 128], reads=[DB("zT")], writes=[zf.b])
                            P.tt("dve", ysb[:], ysb[:], y0[:], ALU.add, reads=[ysb.b, y0.b], writes=[ysb.b])
                            P.tt("dve", xf[:], xf[:], Dt[:].unsqueeze(2).to_broadcast([128, 8, 128]), ALU.mult,
                                 reads=[xf.b, Dt.b], writes=[xf.b])
                            P.tt("dve", ysb[:], ysb[:], xf[:], ALU.add, reads=[ysb.b, xf.b], writes=[ysb.b])
                            P.act(zf[:], zf[:], AF.Silu, reads=[zf.b], writes=[zf.b])
                            P.tt("dve", ysb[:], ysb[:], zf[:], ALU.mult, reads=[ysb.b, zf.b], writes=[ysb.b])
                            pn = psum[3]
                            for hp in range(8):
                                P.act(sq[:], ysb[:, hp, :], AF.Square, reads=[ysb.b], writes=[sq.b])
                                P.mm(pn[:, (hp // 2) * 128:(hp // 2 + 1) * 128], C_(ONES), sq[:], hp % 2 == 0, hp % 2 == 1,
                                     reads=[cst.b, sq.b], writes=[pn.b])
                            P.act(rst[:], pn[:].rearrange("p (g l) -> p g l", g=4), AF.Ln, bias=kct[:, 1:2], scale=1.0 / 256,
                                  reads=[pn.b, kct.b], writes=[rst.b])
                            P.act(rst[:], rst[:], AF.Exp, scale=-0.5, reads=[rst.b], writes=[rst.b])
                            for g_ in range(4):
                                P.tt("dve", ysb[:, 2 * g_:2 * g_ + 2, :], ysb[:, 2 * g_:2 * g_ + 2, :],
                                     rst[:, g_, :].unsqueeze(1).to_broadcast([128, 2, 128]), ALU.mult,
                                     reads=[ysb.b, rst.b], writes=[ysb.b])
                            P.tt("dve", ob[:], ysb[:], nw[:].unsqueeze(2).to_broadcast([128, 8, 128]), ALU.mult,
                                 reads=[ysb.b, nw.b], writes=[ob.b])
                            P.dma(ossdT.ap().rearrange("(a p) t -> p a t", p=128)[:, :, t0:t0 + 128], ob[:], reads=[ob.b],
                                  writes=[DB("ossdT")], eng="pool")

        def s5_phase(l, last):
            with phase() as sb:
                prm = sb([128, 2, 24, 32], F32, "prm")
                prmi = sb([128, 128], I32, "prmi")

                def sincos(y, s_out, c_out, t1_, t2_, t3_, ti_, fb, ib, eng="dve"):
                    for off, out_ in ((0.0, s_out), (0.25, c_out)):
                        P.ts(eng, t1_, y, off, None, ALU.add, reads=[fb], writes=[fb])
                        P.copy(eng, ti_, t1_, reads=[fb], writes=[ib])
                        P.copy(eng, t2_, ti_, reads=[ib], writes=[fb])
                        P.tt(eng, t1_, t1_, t2_, ALU.subtract, reads=[fb], writes=[fb])
                        P.ts(eng, t3_, t1_, 0.5, None, ALU.is_gt, reads=[fb], writes=[fb])
                        P.tt(eng, t1_, t1_, t3_, ALU.subtract, reads=[fb], writes=[fb])
                        P.act(out_, t1_, AF.Sin, scale=2 * PI, reads=[fb], writes=[fb])
                for d in range(2):
                    pv = lambda i: prm[:, d, i, :]
                    LRE, LIM, STP, MAG, ANG, T1, ABR, ABI, DEN, KRE, KIM, T2, T3, AL, RL, CL, SL, ANGN, LNR, NLNR = range(20)
                    P.dma(pv(LRE), s5lre.ap()[l, d], writes=[prm.b])
                    P.dma(pv(LIM), s5lim.ap()[l, d], writes=[prm.b])
                    P.dma(pv(STP), s5ls.ap()[l, d], writes=[prm.b])
                    P.act(pv(STP), pv(STP), AF.Exp, reads=[prm.b], writes=[prm.b])
                    P.tt("dve", pv(MAG), pv(LRE), pv(STP), ALU.mult, reads=[prm.b], writes=[prm.b])
                    P.act(pv(MAG), pv(MAG), AF.Exp, reads=[prm.b], writes=[prm.b])
                    P.tt("dve", pv(ANG), pv(LIM), pv(STP), ALU.mult, reads=[prm.b], writes=[prm.b])
                    P.ts("dve", pv(ANGN), pv(ANG), 1.0 / (2 * PI), None, ALU.mult, reads=[prm.b], writes=[prm.b])
                    sincos(pv(ANGN), pv(ABI), pv(ABR), pv(T1), pv(T2), pv(T3), prmi[:, 0:32], prm.b, prmi.b)
                    P.tt("dve", pv(ABR), pv(ABR), pv(MAG), ALU.mult, reads=[prm.b], writes=[prm.b])
                    P.tt("dve", pv(ABI), pv(ABI), pv(MAG), ALU.mult, reads=[prm.b], writes=[prm.b])
                    P.tt("dve", pv(DEN), pv(LRE), pv(LRE), ALU.mult, reads=[prm.b], writes=[prm.b])
                    P.tt("dve", pv(T1), pv(LIM), pv(LIM), ALU.mult, reads=[prm.b], writes=[prm.b])
                    P.tt("dve", pv(DEN), pv(DEN), pv(T1), ALU.add, reads=[prm.b], writes=[prm.b])
                    P.emit("dve", lambda e, d=d: e.reciprocal(prm[:, d, DEN, :], prm[:, d, DEN, :]), reads=[prm.b], writes=[prm.b])
                    P.ts("dve", pv(T2), pv(ABR), -1.0, None, ALU.add, reads=[prm.b], writes=[prm.b])
                    P.tt("dve", pv(KRE), pv(T2), pv(LRE), ALU.mult, reads=[prm.b], writes=[prm.b])
                    P.tt("dve", pv(T1), pv(ABI), pv(LIM), ALU.mult, reads=[prm.b], writes=[prm.b])
                    P.tt("dve", pv(KRE), pv(KRE), pv(T1), ALU.add, reads=[prm.b], writes=[prm.b])
                    P.tt("dve", pv(KRE), pv(KRE), pv(DEN), ALU.mult, reads=[prm.b], writes=[prm.b])
                    P.tt("dve", pv(KIM), pv(ABI), pv(LRE), ALU.mult, reads=[prm.b], writes=[prm.b])
                    P.tt("dve", pv(T1), pv(T2), pv(LIM), ALU.mult, reads=[prm.b], writes=[prm.b])
                    P.tt("dve", pv(KIM), pv(KIM), pv(T1), ALU.subtract, reads=[prm.b], writes=[prm.b])
                    P.tt("dve", pv(KIM), pv(KIM), pv(DEN), ALU.mult, reads=[prm.b], writes=[prm.b])
                    P.ts("dve", pv(T1), pv(ANGN), 128.0, None, ALU.mult, reads=[prm.b], writes=[prm.b])
                    P.copy("dve", prmi[:, 0:32], pv(T1), reads=[prm.b], writes=[prmi.b])
                    P.copy("dve", pv(T3), prmi[:, 0:32], reads=[prmi.b], writes=[prm.b])
                    P.tt("dve", pv(AL), pv(T1), pv(T3), ALU.subtract, reads=[prm.b], writes=[prm.b])
                    P.tt("dve", pv(T1), pv(LRE), pv(STP), ALU.mult, reads=[prm.b], writes=[prm.b])
                    P.act(pv(RL), pv(T1), AF.Exp, scale=128.0, reads=[prm.b], writes=[prm.b])
                    P.copy("dve", pv(LNR), pv(T1), reads=[prm.b], writes=[prm.b])
                    P.ts("dve", pv(NLNR), pv(T1), -1.0, None, ALU.mult, reads=[prm.b], writes=[prm.b])
                rsts = sb([128, T], F32, "rsts")
                P.memset("dve", rsts[:], 1.0, writes=[rsts.b])
                P.memset("dve", rsts[:].rearrange("p (c l) -> p c l", l=128)[:, :, 0:1], 0.0, writes=[rsts.b])
                cidx = sb([128, NCH], F32, "cidx")
                P.copy("dve", cidx[:], cst[:, TAU, 0:NCH], reads=[cst.b], writes=[cidx.b])
                tbs = [sb([128, 12, 128], F32, "tb_") for _ in range(2)]
                prmi2 = sb([128, 128], I32, "prmi2")
                ct_ = sb([128, 12, NCH], F32, "ct_")
                ub = sb([128, T], BF16, "ub")
                wB = [sb([128, 2, 128], BF16, "wB") for _ in range(2)]
                xo = [sb([128, T], BF16, "xo") for _ in range(2)]

                class W2:
                    def __init__(self, nm):
                        self.tl = sb([128, T], F32, nm)
                        self.bp = Buf(nm + "p")
                        self.bd = Buf(nm + "d")
                        self.all = [self.bp, self.bd]

                    def __getitem__(self, k):
                        return self.tl[k]

                R_, I_, A_, B_ = W2("R_"), W2("I_"), W2("A_"), W2("B_")
                xob = [(Buf("xo0p"), Buf("xo0d")), (Buf("xo1p"), Buf("xo1d"))]
                nctx_ch = NCTX // 128
                nxp = max(0, int(round(0.5 * (NX // 128))) - 0)
                if NX // 128 <= 4:
                    nxp = 1
                pieces = [(0, NCTX, "dve", 0, NCTX), (NCTX, T, "dve", NCTX, NX)]
                pieces = [p for p in pieces if p[1] > p[0]]

                tbh = [None]

                def bsel(w, eng):
                    return w.bp if eng == "pool" else w.bd

                def big(out, in0, tbl_i, in1, op):
                    for (c0, c1, eng, _, _) in pieces:
                        o3 = out[:, c0:c1].rearrange("p (c l) -> p c l", l=128)
                        a3 = in0[:, c0:c1].rearrange("p (c l) -> p c l", l=128)
                        if tbl_i is not None:
                            tcur = tbh[0]
                            b3 = tcur[:, tbl_i, :].unsqueeze(1).to_broadcast([128, (c1 - c0) // 128, 128])
                            P.tt(eng, o3, a3, b3, op, reads=[bsel(in0, eng), tcur.b], writes=[bsel(out, eng)])
                        else:
                            P.tt(eng, o3, a3, in1[:, c0:c1].rearrange("p (c l) -> p c l", l=128), op,
                                 reads=[bsel(in0, eng), bsel(in1, eng)], writes=[bsel(out, eng)])

                for d in range(2):
                    pv = lambda i: prm[:, d, i, :]
                    for st in range(32):
                        ct = st // 4
                        if st % 4 == 0:
                            P.dma(ub[:], uT.ap()[ct * 128:(ct + 1) * 128, :], reads=[DB("uT")], writes=[ub.b], eng="pool")
                        w_ = wB[st % 2]
                        P.dma(w_[:, 0, :], sbreb.ap()[l, d, st], writes=[w_.b])
                        P.dma(w_[:, 1, :], sbimb.ap()[l, d, st], writes=[w_.b])
                        col = lambda i: prm[:, d, i, st:st + 1]
                        tb_ = tbs[(d * 32 + st) % 2]
                        tbh[0] = tb_
                        E_ = "dve"
                        P.ts(E_, tb_[:, 0, :], C_(TAU), col(ANGN), None, ALU.mult, reads=[cst.b, prm.b], writes=[tb_.b])
                        sincos(tb_[:, 0, :], tb_[:, 3, :], tb_[:, 2, :], tb_[:, 1, :], tb_[:, 6, :], tb_[:, 7, :], prmi2[:, 0:128], tb_.b, prmi2.b, eng=E_)
                        P.act(tb_[:, 10, :], C_(TAU), AF.Exp, scale=col(LNR), reads=[cst.b, prm.b], writes=[tb_.b])
                        P.act(tb_[:, 11, :], C_(TAU), AF.Exp, scale=col(NLNR), reads=[cst.b, prm.b], writes=[tb_.b])
                        P.ts(E_, tb_[:, 4, :], tb_[:, 2, :], col(KRE), None, ALU.mult, reads=[tb_.b, prm.b], writes=[tb_.b])
                        P.ts(E_, tb_[:, 6, :], tb_[:, 3, :], col(KIM), None, ALU.mult, reads=[tb_.b, prm.b], writes=[tb_.b])
                        P.tt(E_, tb_[:, 4, :], tb_[:, 4, :], tb_[:, 6, :], ALU.add, reads=[tb_.b], writes=[tb_.b])
                        P.ts(E_, tb_[:, 5, :], tb_[:, 2, :], col(KIM), None, ALU.mult, reads=[tb_.b, prm.b], writes=[tb_.b])
                        P.ts(E_, tb_[:, 6, :], tb_[:, 3, :], col(KRE), None, ALU.mult, reads=[tb_.b, prm.b], writes=[tb_.b])
                        P.tt(E_, tb_[:, 5, :], tb_[:, 5, :], tb_[:, 6, :], ALU.subtract, reads=[tb_.b], writes=[tb_.b])
                        P.tt(E_, tb_[:, 4, :], tb_[:, 4, :], tb_[:, 11, :], ALU.mult, reads=[tb_.b], writes=[tb_.b])
                        P.tt(E_, tb_[:, 5, :], tb_[:, 5, :], tb_[:, 11, :], ALU.mult, reads=[tb_.b], writes=[tb_.b])
                        P.tt(E_, tb_[:, 8, :], tb_[:, 2, :], tb_[:, 10, :], ALU.mult, reads=[tb_.b], writes=[tb_.b])
                        P.tt(E_, tb_[:, 9, :], tb_[:, 3, :], tb_[:, 10, :], ALU.mult, reads=[tb_.b], writes=[tb_.b])
                        for (t0, TB, which) in blocks:
                            for ri, dstb in ((0, R_), (1, I_)):
                                pm = psum[(2 * (t0 // 512) + ri) % 4]
                                P.mm(pm[:, 0:TB], w_[:, ri, :], ub[:, t0:t0 + TB], True, True, reads=[w_.b, ub.b], writes=[pm.b])
                                if d == 0:
                                    P.copy("act", dstb[:, t0:t0 + TB], pm[:, 0:TB], reads=[pm.b], writes=dstb.all)
                                else:
                                    s0, sl = (0, NCTX) if which == 1 else (NCTX, NX)
                                    p0 = s0 + (sl - 1) - (t0 + TB - 1 - s0)
                                    P.copy("act", dstb[:, p0:p0 + TB], rev_last(pm[:, 0:TB]), reads=[pm.b], writes=dstb.all)
                        big(A_, R_, 4, None, ALU.mult)
                        big(B_, I_, 5, None, ALU.mult)
                        big(A_, A_, None, B_, ALU.subtract)
                        big(B_, R_, 5, None, ALU.mult)
                        big(I_, I_, 4, None, ALU.mult)
                        big(I_, I_, None, B_, ALU.add)
                        bc3 = lambda a: a[:].rearrange("p (c l) -> p c l", l=128)
                        er, ei, vr, vi, cc_, ss_, tmp, tmp2 = (ct_[:, i, :] for i in range(8))
                        wl_r = ct_[:, 8, :]
                        wl_i = ct_[:, 9, :]
                        P.emit("dve", lambda e, A_=A_: e.tensor_reduce(ct_[:, 8, :], A_[:].rearrange("p (c l) -> p c l", l=128), AX.X, ALU.add),
                               reads=A_.all, writes=[ct_.b])
                        P.emit("dve", lambda e, I_=I_: e.tensor_reduce(ct_[:, 9, :], I_[:].rearrange("p (c l) -> p c l", l=128), AX.X, ALU.add),
                               reads=I_.all, writes=[ct_.b])
                        c127 = tb_[:, 8, 127:128]
                        s127 = tb_[:, 9, 127:128]
                        P.ts("dve", er, wl_r, c127, None, ALU.mult, reads=[ct_.b, tb_.b], writes=[ct_.b])
                        P.ts("dve", tmp, wl_i, s127, None, ALU.mult, reads=[ct_.b, tb_.b], writes=[ct_.b])
                        P.tt("dve", er, er, tmp, ALU.subtract, reads=[ct_.b], writes=[ct_.b])
                        P.ts("dve", ei, wl_i, c127, None, ALU.mult, reads=[ct_.b, tb_.b], writes=[ct_.b])
                        P.ts("dve", tmp, wl_r, s127, None, ALU.mult, reads=[ct_.b, tb_.b], writes=[ct_.b])
                        P.tt("dve", ei, ei, tmp, ALU.add, reads=[ct_.b], writes=[ct_.b])
                        P.ts("dve", tmp, cidx[:], col(AL), None, ALU.mult, reads=[cidx.b, prm.b], writes=[ct_.b])
                        sincos(tmp, ss_, cc_, tmp2, ct_[:, 10, :], ct_[:, 11, :], prmi[:, 0:NCH], ct_.b, prmi.b)
                        P.tt("dve", vr, er, cc_, ALU.mult, reads=[ct_.b], writes=[ct_.b])
                        P.tt("dve", tmp, ei, ss_, ALU.mult, reads=[ct_.b], writes=[ct_.b])
                        P.tt("dve", vr, vr, tmp, ALU.add, reads=[ct_.b], writes=[ct_.b])
                        P.tt("dve", vi, ei, cc_, ALU.mult, reads=[ct_.b], writes=[ct_.b])
                        P.tt("dve", tmp, er, ss_, ALU.mult, reads=[ct_.b], writes=[ct_.b])
                        P.tt("dve", vi, vi, tmp, ALU.subtract, reads=[ct_.b], writes=[ct_.b])
                        P.ts("dve", tmp2, cidx[:], 0.0, col(RL), ALU.mult, ALU.add, reads=[cidx.b, prm.b], writes=[ct_.b])
                        P.scan("dve", er, tmp2, vr, 0.0, reads=[ct_.b], writes=[ct_.b])
                        P.scan("dve", ei, tmp2, vi, 0.0, reads=[ct_.b], writes=[ct_.b])
                        P.tt("dve", vr, er, cc_, ALU.mult, reads=[ct_.b], writes=[ct_.b])
                        P.tt("dve", tmp, ei, ss_, ALU.mult, reads=[ct_.b], writes=[ct_.b])
                        P.tt("dve", vr, vr, tmp, ALU.subtract, reads=[ct_.b], writes=[ct_.b])
                        P.tt("dve", vi, ei, cc_, ALU.mult, reads=[ct_.b], writes=[ct_.b])
                        P.tt("dve", tmp, er, ss_, ALU.mult, reads=[ct_.b], writes=[ct_.b])
                        P.tt("dve", vi, vi, tmp, ALU.add, reads=[ct_.b], writes=[ct_.b])
                        P.ts("dve", er, vr, col(ABR), None, ALU.mult, reads=[ct_.b, prm.b], writes=[ct_.b])
                        P.ts("dve", tmp, vi, col(ABI), None, ALU.mult, reads=[ct_.b, prm.b], writes=[ct_.b])
                        P.tt("dve", er, er, tmp, ALU.subtract, reads=[ct_.b], writes=[ct_.b])
                        P.ts("dve", ei, vi, col(ABR), None, ALU.mult, reads=[ct_.b, prm.b], writes=[ct_.b])
                        P.ts("dve", tmp, vr, col(ABI), None, ALU.mult, reads=[ct_.b, prm.b], writes=[ct_.b])
                        P.tt("dve", ei, ei, tmp, ALU.add, reads=[ct_.b], writes=[ct_.b])
                        P.tt("dve", bc3(A_)[:, 1:NCH, 0], bc3(A_)[:, 1:NCH, 0], er[:, 0:NCH - 1], ALU.add, reads=A_.all + [ct_.b], writes=A_.all)
                        P.tt("dve", bc3(I_)[:, 1:NCH, 0], bc3(I_)[:, 1:NCH, 0], ei[:, 0:NCH - 1], ALU.add, reads=I_.all + [ct_.b], writes=I_.all)
                        P.scan("dve", R_[:], rsts[:], A_[:], 0.0, reads=[rsts.b] + A_.all, writes=R_.all)
                        P.scan("dve", B_[:], rsts[:], I_[:], 0.0, reads=[rsts.b] + I_.all, writes=B_.all)
                        for ri in range(2):
                            if ri == 0:
                                big(A_, R_, 8, None, ALU.mult)
                                big(I_, B_, 9, None, ALU.mult)
                                op = ALU.subtract
                            else:
                                big(A_, B_, 8, None, ALU.mult)
                                big(I_, R_, 9, None, ALU.mult)
                                op = ALU.add
                            xo_ = xo[ri]
                            for (c0, c1, eng, s0, sl) in pieces:
                                xb = xob[ri][0 if eng == "pool" else 1]
                                if d == 0:
                                    oo = xo_[:, c0:c1]
                                else:
                                    n0 = 2 * s0 + sl - c1
                                    oo = rev_last(xo_[:, n0:n0 + (c1 - c0)])
                                P.tt(eng, oo, A_[:, c0:c1], I_[:, c0:c1], op, reads=[bsel(A_, eng), bsel(I_, eng)], writes=[xb])
                            P.dma(XS.ap()[d, ri, st * 128:(st + 1) * 128, :], xo_[:], reads=list(xob[ri]), writes=[DB("XS")], eng="pool")
            with phase() as sb:
                sd = sb([128, 8], F32, "sd")
                gb = sb([128, 8], F32, "gb")
                P.dma(sd[:], s5d.ap()[l], writes=[sd.b])
                P.dma(gb[:], glub.ap()[l], writes=[gb.b])
                cw_ = sb([128, 2, 2, 32, 128], BF16, "cw_")
                for d in range(2):
                    P.dma(cw_[:, d, 0], screb.ap()[l, d].rearrange("s p c -> p s c"), writes=[cw_.b])
                    P.dma(cw_[:, d, 1], scimb.ap()[l, d].rearrange("s p c -> p s c"), writes=[cw_.b])
                gw = sb([128, 8, 1024], BF16, "gw")
                P.dma(gw[:], glub16.ap()[l].rearrange("(a p) c -> p a c", p=128), writes=[gw.b])
                xs_ = [sb([128, 16, 512], BF16, "xs_") for _ in range(2)]
                uu = [sb([128, 512], F32, "uu") for _ in range(2)]
                tt32 = sb([128, 8, 512], F32, "tt32")
                tt16 = sb([128, 8, 512], BF16, "tt16")
                sg = [sb([128, 512], F32, "sg") for _ in range(2)]
                og = [sb([128, 8, 512], BF16, "og") for _ in range(1)]
                for (t0, TB, which) in blocks:
                    if last and which == 1:
                        continue
                    for ct in range(8):
                        x_ = xs_[ct % 2]
                        u_ = uu[ct % 2]
                        for d in range(2):
                            for ri in range(2):
                                P.dma(x_[:, (d * 2 + ri) * 4:(d * 2 + ri) * 4 + 4, 0:TB],
                                      XS.ap()[d, ri, ct * 512:(ct + 1) * 512, t0:t0 + TB].rearrange("(a p) t -> p a t", p=128),
                                      reads=[DB("XS")], writes=[x_.b])
                        P.dma(u_[:, 0:TB], uT.ap()[ct * 128:(ct + 1) * 128, t0:t0 + TB], reads=[DB("uT")], writes=[u_.b])
                        pm = psum[ct % 4]
                        n = 0
                        for d in range(2):
                            for ri in range(2):
                                for s4 in range(4):
                                    P.mm(pm[:, 0:TB], cw_[:, d, ri, ct * 4 + s4, :], x_[:, (d * 2 + ri) * 4 + s4, 0:TB], n == 0, n == 15,
                                         reads=[cw_.b, x_.b], writes=[pm.b])
                                    n += 1
                        P.stt("dve", tt32[:, ct, 0:TB], u_[:, 0:TB], sd[:, ct:ct + 1], pm[:, 0:TB], ALU.mult, ALU.add,
                              reads=[u_.b, sd.b, pm.b], writes=[tt32.b])
                        P.act(tt32[:, ct, 0:TB], tt32[:, ct, 0:TB], AF.Gelu, reads=[tt32.b], writes=[tt32.b])
                        P.copy("dve", tt16[:, ct, 0:TB], tt32[:, ct, 0:TB], reads=[tt32.b], writes=[tt16.b])
                    o_ = og[0]
                    for m in range(8):
                        pm = psum[4 + m % 4]
                        for kc in range(8):
                            P.mm(pm[:, 0:TB], gw[:, kc, m * 128:(m + 1) * 128], tt16[:, kc, 0:TB], kc == 0, kc == 7,
                                 reads=[gw.b, tt16.b], writes=[pm.b])
                        s_ = sg[m % 2]
                        P.act(s_[:, 0:TB], pm[:, 0:TB], AF.Sigmoid, bias=gb[:, m:m + 1], reads=[pm.b, gb.b], writes=[s_.b])
                        P.tt("dve", o_[:, m, 0:TB], tt32[:, m, 0:TB], s_[:, 0:TB], ALU.mult, reads=[tt32.b, s_.b], writes=[o_.b])
                    P.dma(os5T.ap().rearrange("(a p) t -> p a t", p=128)[:, :, t0:t0 + TB], o_[:, :, 0:TB], reads=[o_.b],
                          writes=[DB("os5T")], eng="pool")

        def merge_phase(l, last):

            brs = (oattT, ossdT, os5T)
            with phase() as sb:
                hx = sb([128, KC, 512], F32, "hx")
                brt = [sb([128, 8, 512], BF16, "brt") for _ in range(3)]
                wbp = [sb([128, 3, 8, 128], BF16, "wbp") for _ in range(2)]
                wop = [sb([128, KC, 128], BF16, "wop") for _ in range(2)]
                gl = [sb([128, 3, 512], F32, "gl") for _ in range(2)]
                acc = sb([128, 512], F32, "acc")
                tm = sb([128, 512], F32, "tm")
                mixed = sb([128, KC, 512], BF16, "mixed")
                lt = ln_tmps(sb)
                for (t0, TB, which) in blocks:
                    if last and which == 1:
                        continue
                    P.dma(hx[:, :, 0:TB], hview(hT, t0, TB), reads=[DB("hT")], writes=[hx.b])
                    for j in range(3):
                        P.dma(brt[j][:, :, 0:TB], brs[j].ap().rearrange("(a p) t -> p a t", p=128)[:, :, t0:t0 + TB],
                              reads=[DB(tname[id(brs[j])])], writes=[brt[j].b])
                    for kc in range(KC):
                        P.ts(P.ve(), hx[:, kc, 0:TB], hx[:, kc, 0:TB], ALPHA, None, ALU.mult, reads=[hx.b], writes=[hx.b])
                    for m in range(KC):
                        w = wbp[m % 2]
                        g_ = gl[m % 2]
                        for j in range(3):
                            P.dma(w[:, j], wbrb.ap()[l, j].rearrange("(a p) c -> p a c", p=128)[:, :, m * 128:(m + 1) * 128], writes=[w.b])
                        P.dma(g_[:, :, 0:TB], gT.ap().rearrange("(j a p) t -> p j a t", j=3, p=128)[:, :, m, t0:t0 + TB],
                              reads=[DB("gT")], writes=[g_.b])
                        P.act(g_[:, :, 0:TB], g_[:, :, 0:TB], AF.Sigmoid, reads=[g_.b], writes=[g_.b])
                        for j in range(3):
                            pm = psum[(m * 3 + j) % 4]
                            for kc in range(8):
                                P.mm(pm[:, 0:TB], w[:, j, kc, :], brt[j][:, kc, 0:TB], kc == 0, kc == 7,
                                     reads=[w.b, brt[j].b], writes=[pm.b])
                            if j == 0:
                                P.tt("dve", acc[:, 0:TB], pm[:, 0:TB], g_[:, 0, 0:TB], ALU.mult, reads=[pm.b, g_.b], writes=[acc.b])
                            else:
                                P.tt("dve", tm[:, 0:TB], pm[:, 0:TB], g_[:, j, 0:TB], ALU.mult, reads=[pm.b, g_.b], writes=[tm.b])
                                if j == 1:
                                    P.tt("dve", acc[:, 0:TB], acc[:, 0:TB], tm[:, 0:TB], ALU.add, reads=[acc.b, tm.b], writes=[acc.b])
                                else:
                                    P.tt("dve", mixed[:, m, 0:TB], acc[:, 0:TB], tm[:, 0:TB], ALU.add, reads=[acc.b, tm.b], writes=[mixed.b])
                    for m in range(KC):
                        w = wop[m % 2]
                        load_wpanel(w, woutb.ap()[l], KC, m * 128, 128)
                        py = psum[4 + m % 2]
                        for kc in range(KC):
                            P.mm(py[:, 0:TB], w[:, kc, :], mixed[:, kc, 0:TB], kc == 0, kc == KC - 1, reads=[w.b, mixed.b], writes=[py.b])
                        P.stt("dve", hx[:, m, 0:TB], py[:, 0:TB], modg[:, 1, m, which:which + 1], hx[:, m, 0:TB],
                              ALU.mult, ALU.add, reads=[py.b, modg.b, hx.b], writes=[hx.b])
                    layer_norm(lt, hx, TB, l, 1, psum[6], psum[7])
                    P.dma(hview(hT, t0, TB), hx[:, :, 0:TB], reads=[hx.b], writes=[DB("hT")], eng="pool")

        negpi = mk_sb(es, [128, 1], F32, "negpi")
        P.memset("dve", negpi[:], -PI, writes=[negpi.b])
        kct = mk_sb(es, [128, 4], F32, "kct")
        P.memset("dve", kct[:, 0:1], 1e-5, writes=[kct.b])
        P.memset("dve", kct[:, 1:2], 1e-6, writes=[kct.b])
        P.memset("dve", kct[:, 2:3], 1.0, writes=[kct.b])
        stages = dbg.get("stages") if dbg else None

        def on(name, l):
            return stages is None or (name, l) in stages or name in stages

        for l in range(DEPTH):
            last = (l == DEPTH - 1)
            if on("mod", l):
                P.phase_name = 'mod_phase' + str(l)
                mod_phase(l)
            if on("ffn1", l):
                P.phase_name = 'ffn1_' + str(l)
                ffn_phase(l, 0, xT if l == 0 else hT, hT, False, False)
            if on("inproj", l):
                P.phase_name = 'inproj_phase' + str(l)
                inproj_phase(l)
            if on("att", l):
                P.phase_name = 'att_phase' + str(l)
                att_phase(l, last)
            if on("conv", l):
                P.phase_name = 'conv_phase' + str(l)
                conv_phase(l)
            if on("ssd", l):
                P.phase_name = 'ssd_phase' + str(l)
                ssd_phase(l, last)
            if on("s5", l):
                P.phase_name = 's5_phase' + str(l)
                s5_phase(l, last)
            if on("merge", l):
                P.phase_name = 'merge_phase' + str(l)
                merge_phase(l, last)
            if on("ffn2", l):
                P.phase_name = 'ffn2_' + str(l)
                ffn_phase(l, 1, hT, hT, last, last)
        if dbg and dbg.get("dump"):
            for nm in dbg["dump"]:
                s_ = scr_map[nm]
                shp = list(s_.ap().shape)
                o = nc.dram_tensor("dbg_" + nm, shp, s_.ap().dtype, kind="ExternalOutput")
                rows = shp[0]
                for r0 in range(0, rows, 128):
                    with ExitStack() as e2:
                        tl = mk_sb(e2, [128, shp[1]], s_.ap().dtype, "dbgt")
                        P.dma(tl[:], s_.ap()[r0:r0 + 128, :], reads=[DB(nm)], writes=[tl.b])
                        P.dma(o.ap()[r0:r0 + 128, :], tl[:], reads=[tl.b], eng="pool", is_out=True)
                        P.barrier()
        P.finish()
    return nc


def _consts(NX):
    c = np.zeros((128, 10, 128), np.float32)
    i = np.arange(128)
    c[:, 0] = np.eye(128)
    c[:, 1] = 1.0
    c[:, 2] = (i[:, None] <= i[None, :])
    c[:, 3] = (i[:, None] >= i[None, :])
    c[:, 4] = (i[:, None] > i[None, :])
    c[:, 5] = (i[:, None] < i[None, :])
    pm = np.zeros((128, 128), np.float32)
    for d in range(128):
        dd = d % 64
        half = (dd % 32) // 16
        partner = d + 16 if half == 0 else d - 16
        pm[partner, d] = 1.0
    c[:, 6] = pm
    c[:, 7] = (i[:, None] < 64) * 1.0
    c[:, 8] = (i[:, None] >= 64) * 1.0
    c[:, 9] = i[None, :].astype(np.float32)
    t = np.arange(NX)
    row = (t // 64).astype(np.float32)
    colp = (t % 64).astype(np.float32)
    inv = (10000.0 ** (-np.arange(16, dtype=np.float32) / 16)).astype(np.float32)
    rc = np.zeros((128, NX), np.float32)
    rs = np.zeros((128, NX), np.float32)
    for d in range(128):
        dd = d % 64
        axis = dd // 32
        half = (dd % 32) // 16
        f = dd % 16
        pos = row if axis == 0 else colp
        ang = (pos * inv[f]).astype(np.float32)
        rc[d] = np.cos(ang)
        rs[d] = np.sin(ang) * (-1.0 if half == 0 else 1.0)
    return c, rc, rs


def prep_shared(inp, NX):
    f = lambda a: np.ascontiguousarray(np.asarray(a, dtype=np.float32))
    out = {}
    c, rc, rs = _consts(NX)
    out["consts"], out["ropec"], out["ropes"] = c, rc, rs
    out["w_mod"] = f(inp["w_mod"])
    out["bmod"] = f(np.asarray(inp["b_mod"]).reshape(2, 144, 128).transpose(0, 2, 1))
    out["lng"] = f(np.asarray(inp["ln_g"]).reshape(2, 3, 16, 128).transpose(0, 3, 1, 2))
    out["lnb"] = f(np.asarray(inp["ln_b"]).reshape(2, 3, 16, 128).transpose(0, 3, 1, 2))
    for k in ("ffn_w1", "ffn_w3", "ffn_w2", "w_in", "glu_w", "w_branch", "w_out"):
        out[k] = f(inp["s5_glu_w"] if k == "glu_w" else inp[k])
    out["attlam"] = f(np.broadcast_to(np.asarray(inp["att_lam"]).reshape(2, 1, 256), (2, 128, 256)))
    out["subln"] = f(np.asarray(inp["att_subln"]).reshape(2, 128, 1))
    out["convw"] = f(np.asarray(inp["ssd_conv_w"]).reshape(2, 5, 16, 128).transpose(0, 3, 2, 1))
    out["convb"] = f(np.asarray(inp["ssd_conv_b"]).reshape(2, 16, 128).transpose(0, 2, 1))
    out["alog"] = f(np.broadcast_to(np.asarray(inp["ssd_a_log"]).reshape(2, 1, 32), (2, 128, 32)))
    out["dtb"] = f(np.broadcast_to(np.asarray(inp["ssd_dt_bias"]).reshape(2, 1, 32), (2, 128, 32)))
    sd = np.asarray(inp["ssd_d"]).reshape(2, 8, 2)
    out["ssdD"] = f(np.repeat(sd, 64, axis=2).transpose(0, 2, 1))
    out["ssdnw"] = f(np.asarray(inp["ssd_norm"]).reshape(2, 8, 128).transpose(0, 2, 1))
    out["s5lre"] = f(np.asarray(inp["s5_lam_re"]).reshape(2, 2, 32, 128).transpose(0, 1, 3, 2))
    out["s5lim"] = f(np.asarray(inp["s5_lam_im"]).reshape(2, 2, 32, 128).transpose(0, 1, 3, 2))
    ls = np.repeat(np.asarray(inp["s5_log_step"]).reshape(2, 2, 64, 1), 64, axis=3)
    out["s5ls"] = f(ls.reshape(2, 2, 32, 128).transpose(0, 1, 3, 2))
    bre = np.asarray(inp["s5_b_re"]); bim = np.asarray(inp["s5_b_im"])
    cre = np.asarray(inp["s5_c_re"]); cim = np.asarray(inp["s5_c_im"])
    Bre = np.zeros((2, 2, 32, 128, 128), np.float32); Bim = np.zeros_like(Bre)
    Cre = np.zeros_like(Bre); Cim = np.zeros_like(Bre)
    for st in range(32):
        for g2 in range(2):
            g = 2 * st + g2
            gl = g % 8
            Bre[:, :, st, gl * 16:(gl + 1) * 16, g2 * 64:(g2 + 1) * 64] = bre[:, :, g].transpose(0, 1, 3, 2)
            Bim[:, :, st, gl * 16:(gl + 1) * 16, g2 * 64:(g2 + 1) * 64] = bim[:, :, g].transpose(0, 1, 3, 2)
            Cre[:, :, st, g2 * 64:(g2 + 1) * 64, gl * 16:(gl + 1) * 16] = cre[:, :, g].transpose(0, 1, 3, 2)
            Cim[:, :, st, g2 * 64:(g2 + 1) * 64, gl * 16:(gl + 1) * 16] = cim[:, :, g].transpose(0, 1, 3, 2)
    out["s5bre"], out["s5bim"], out["s5cre"], out["s5cim"] = Bre, Bim, Cre, Cim
    out["s5d"] = f(np.asarray(inp["s5_d"]).reshape(2, 8, 128).transpose(0, 2, 1))
    out["glub"] = f(np.asarray(inp["s5_glu_b"]).reshape(2, 8, 128).transpose(0, 2, 1))
    return out


def prep_core(inp, b):
    x = np.asarray(inp["x"][b], dtype=np.float32)
    ctx = np.asarray(inp["ctx"][b], dtype=np.float32)
    xT = np.ascontiguousarray(np.concatenate([ctx, x], axis=0).T)
    cv = np.stack([np.asarray(inp["c"][b]).reshape(16, 128).T, np.asarray(inp["c_ctx"]).reshape(16, 128).T], axis=2)
    return {"xT": xT, "cvec": np.ascontiguousarray(cv.astype(np.float32))}


def kernel(**inputs):
    NX = int(np.asarray(inputs["x"]).shape[1])
    B = int(np.asarray(inputs["x"]).shape[0])
    shared = prep_shared(inputs, NX)
    nc = build(NX)
    in_maps = []
    for core in range(8):
        m = dict(shared)
        m.update(prep_core(inputs, core % B))
        in_maps.append(m)
    res = run_bass_kernel_spmd(nc, in_maps, core_ids=list(range(8)))
    out = np.stack([np.ascontiguousarray(res.results[b]["yT"].T) for b in range(B)], axis=0)
    return out.astype(np.float32)
```

```python
import math
from contextlib import ExitStack, contextmanager
import numpy as np
import concourse.bass as bass
import concourse.mybir as mybir
from concourse.bass_utils import run_bass_kernel_spmd
from concourse.ap import AP

F32 = mybir.dt.float32
BF16 = mybir.dt.bfloat16
I32 = mybir.dt.int32
AF = mybir.ActivationFunctionType
ALU = mybir.AluOpType
AX = mybir.AxisListType

EPOCH = 16000
NDS = 48
D = 2048
KC = 16
DFF = 5632
FC = 44
NCTX = 256
INC = 13344
DEPTH = 2
ALPHA = (2 * DEPTH) ** 0.25
PI = math.pi


class Buf:
    __slots__ = ("name", "w", "r")

    def __init__(self, name=""):
        self.name = name
        self.w = None
        self.r = {}


class TL:
    def __init__(self, t, b):
        self.t = t
        self.b = b

    def __getitem__(self, k):
        return self.t[k]


class Prog:
    ENGS = ("pe", "act", "dve", "pool", "sp")

    def __init__(self, nc, es):
        self.nc = nc
        self.es = es
        self.q = {e: [] for e in self.ENGS}
        self.cnt = {e: 0 for e in self.ENGS}
        self.csem = {e: [] for e in self.ENGS}
        self.waited = {e: {} for e in self.ENGS}
        self.dsems = [es.enter_context(nc.semaphore(f"dq{i}")) for i in range(NDS)]
        self.dcnt = [0] * NDS
        self.dlast = [None] * NDS
        self.dnext = 0
        self.bufs = {}
        self.out_events = []
        self.rr = 0
        self.phase_name = "init"
        self.scopes = False

    def buf(self, key):
        b = self.bufs.get(key)
        if b is None:
            b = self.bufs[key] = Buf(str(key))
        return b

    def _csem(self, e, ep):
        while len(self.csem[e]) <= ep:
            self.csem[e].append(
                self.es.enter_context(self.nc.semaphore(f"c{e}{len(self.csem[e])}")))
        return self.csem[e][ep]

    def emit(self, eng, fn, reads=(), writes=(), dma=False, is_out=False):
        deps = {}

        def add(ev):
            if ev is None:
                return
            k = ev[0]
            if k not in deps or deps[k][2] < ev[2]:
                deps[k] = ev

        for b in reads:
            add(b.w)
        for b in writes:
            add(b.w)
            for ev in b.r.values():
                add(ev)
        if dma:
            i = self.dnext
            self.dnext = (self.dnext + 1) % NDS
            add(self.dlast[i])
            self.dcnt[i] += 1
            ev = (("d", i), self.dsems[i], 16 * self.dcnt[i])
            self.dlast[i] = ev
            inc = 16
        else:
            self.cnt[eng] += 1
            ep = (self.cnt[eng] - 1) // EPOCH
            ev = ((eng, ep), self._csem(eng, ep), self.cnt[eng] - ep * EPOCH)
            inc = 1
        waits = []
        wd = self.waited[eng]
        for k, (_, s, v) in deps.items():
            if eng == "pe" and k[0] == "pe":
                continue
            if wd.get(k, 0) >= v:
                continue
            wd[k] = v
            waits.append((s, v))
        self.q[eng].append((waits, fn, ev[1], inc, self.phase_name))
        for b in writes:
            b.w = ev
            b.r = {}
        for b in reads:
            k = ev[0]
            if k not in b.r or b.r[k][2] < ev[2]:
                b.r[k] = ev
        if is_out:
            self.out_events.append(ev)
        return ev

    def barrier(self):
        evs = []
        for e in self.ENGS:
            c = self.cnt[e]
            if c > 0:
                ep = (c - 1) // EPOCH
                evs.append(((e, ep), self._csem(e, ep), c - ep * EPOCH))
        for i in range(NDS):
            if self.dlast[i] is not None:
                evs.append(self.dlast[i])
        for e in self.ENGS:
            waits = []
            wd = self.waited[e]
            for (k, s, v) in evs:
                if k[0] == e:
                    continue
                if wd.get(k, 0) >= v:
                    continue
                wd[k] = v
                waits.append((s, v))
            if waits:
                self.q[e].append((waits, None, None, 0, self.phase_name))

    def finish(self):
        self.barrier()
        with self.nc.Block() as block:
            def mk(e):
                def f(eo):
                    cur = None
                    cm = None
                    for waits, fn, sem, inc, ph in self.q[e]:
                        if self.scopes and ph != cur:
                            if cm is not None:
                                cm.__exit__(None, None, None)
                            cm = self.nc.named_scope(ph)
                            cm.__enter__()
                            cur = ph
                        for (s, v) in waits:
                            eo.wait_ge(s, v)
                        if fn is not None:
                            fn(eo).then_inc(sem, inc)
                    if cm is not None:
                        cm.__exit__(None, None, None)
                return f
            block.tensor(mk("pe"))
            block.scalar(mk("act"))
            block.vector(mk("dve"))
            block.gpsimd(mk("pool"))
            block.sync(mk("sp"))

    def dma(self, out, in_, reads=(), writes=(), eng="sp", is_out=False):
        return self.emit(eng, lambda e: e.dma_start(out=out, in_=in_), reads, writes,
                         dma=True, is_out=is_out)

    def mm(self, out, lhsT, rhs, start, stop, reads=(), writes=()):
        return self.emit("pe", lambda e: e.matmul(out, lhsT, rhs, start=start, stop=stop),
                         reads, writes)

    def tr(self, out, in_, ident, reads=(), writes=()):
        return self.emit("pe", lambda e: e.transpose(out, in_, ident), reads, writes)

    def act(self, out, in_, func, bias=0.0, scale=1.0, reads=(), writes=()):
        return self.emit("act", lambda e: e.activation(out, in_, func, bias=bias, scale=scale),
                         reads, writes)

    def tt(self, eng, out, in0, in1, op, reads=(), writes=()):
        return self.emit(eng, lambda e: e.tensor_tensor(out, in0, in1, op), reads, writes)

    def ts(self, eng, out, in0, s1, s2, op0, op1=None, reads=(), writes=()):
        if op1 is None:
            return self.emit(eng, lambda e: e.tensor_scalar(out, in0, s1, None, op0), reads, writes)
        return self.emit(eng, lambda e: e.tensor_scalar(out, in0, s1, s2, op0, op1), reads, writes)

    def stt(self, eng, out, in0, scalar, in1, op0, op1, reads=(), writes=()):
        return self.emit(eng, lambda e: e.scalar_tensor_tensor(out, in0, scalar, in1, op0, op1),
                         reads, writes)

    def copy(self, eng, out, in_, reads=(), writes=()):
        if eng == "act":
            return self.emit(eng, lambda e: e.copy(out, in_), reads, writes)
        return self.emit(eng, lambda e: e.tensor_copy(out, in_), reads, writes)

    def memset(self, eng, ap, val, writes=()):
        return self.emit(eng, lambda e: e.memset(ap, val), (), writes)

    def scan(self, eng, out, d0, d1, init, reads=(), writes=()):
        return self.emit(eng, lambda e: e.tensor_tensor_scan(out, d0, d1, init, ALU.mult, ALU.add),
                         reads, writes)

    def ve(self):
        return "dve"


def rev_last(a):
    ap = [list(x) for x in a.ap]
    st, n = ap[-1]
    ap[-1] = [-st, n]
    return AP(a.tensor, a.offset + (n - 1) * st, ap)


def build(NX=4096, dbg=None):
    T = NCTX + NX
    NCH = T // 128
    blocks = [(0, 256, 1)] + [(NCTX + 512 * i, 512, 0) for i in range(NX // 512)]
    nc = bass.Bass("TRN2", target_bir_lowering=False)
    din = {}

    tname = {}

    def inp(name, shape, dt=F32):
        din[name] = nc.dram_tensor(name, list(shape), dt, kind="ExternalInput")
        tname[id(din[name])] = name
        return din[name]

    xT = inp("xT", [D, T])
    cvec = inp("cvec", [128, KC, 2])
    w_mod = inp("w_mod", [2, D, 9 * D])
    bmod = inp("bmod", [2, 128, 144])
    lng = inp("lng", [2, 128, 3, KC])
    lnb = inp("lnb", [2, 128, 3, KC])
    ffn_w1 = inp("ffn_w1", [2, 2, D, DFF])
    ffn_w3 = inp("ffn_w3", [2, 2, D, DFF])
    ffn_w2 = inp("ffn_w2", [2, 2, DFF, D])
    w_in = inp("w_in", [2, D, INC])
    attlam = inp("attlam", [2, 128, 256])
    subln = inp("subln", [2, 128, 1])
    convw = inp("convw", [2, 128, 16, 5])
    convb = inp("convb", [2, 128, 16])
    alog = inp("alog", [2, 128, 32])
    dtb = inp("dtb", [2, 128, 32])
    ssdD = inp("ssdD", [2, 128, 8])
    ssdnw = inp("ssdnw", [2, 128, 8])
    s5lre = inp("s5lre", [2, 2, 128, 32])
    s5lim = inp("s5lim", [2, 2, 128, 32])
    s5ls = inp("s5ls", [2, 2, 128, 32])
    s5bre = inp("s5bre", [2, 2, 32, 128, 128])
    s5bim = inp("s5bim", [2, 2, 32, 128, 128])
    s5cre = inp("s5cre", [2, 2, 32, 128, 128])
    s5cim = inp("s5cim", [2, 2, 32, 128, 128])
    s5d = inp("s5d", [2, 128, 8])
    glu_w = inp("glu_w", [2, 1024, 1024])
    glub = inp("glub", [2, 128, 8])
    w_branch = inp("w_branch", [2, 3, 1024, D])
    w_out = inp("w_out", [2, D, D])
    consts = inp("consts", [128, 10, 128])
    ropec = inp("ropec", [128, NX])
    ropes = inp("ropes", [128, NX])
    yT = nc.dram_tensor("yT", [D, NX], F32, kind="ExternalOutput")

    def scr(name, shape, dt):
        t_ = nc.dram_tensor(name, list(shape), dt, kind="Internal")
        tname[id(t_)] = name
        return t_

    w1b = scr("w1b", [2, 2, D, DFF], BF16)
    w3b = scr("w3b", [2, 2, D, DFF], BF16)
    w2b = scr("w2b", [2, 2, DFF, D], BF16)
    winb = scr("winb", [2, D, INC], BF16)
    wbrb = scr("wbrb", [2, 3, 1024, D], BF16)
    woutb = scr("woutb", [2, D, D], BF16)
    glub16 = scr("glub16", [2, 1024, 1024], BF16)
    sbreb = scr("sbreb", [2, 2, 32, 128, 128], BF16)
    sbimb = scr("sbimb", [2, 2, 32, 128, 128], BF16)
    screb = scr("screb", [2, 2, 32, 128, 128], BF16)
    scimb = scr("scimb", [2, 2, 32, 128, 128], BF16)
    hT = scr("hT", [D, T], F32)
    qT = scr("qT", [1024, T], BF16)
    kT = scr("kT", [1024, T], BF16)
    vtok = scr("vtok", [T, 1024], BF16)
    zT = scr("zT", [1024, T], F32)
    xbcT = scr("xbcT", [2048, T], F32)
    dttok = scr("dttok", [T, 32], F32)
    uT = scr("uT", [1024, T], F32)
    gT = scr("gT", [3 * D, T], F32)
    oattT = scr("oattT", [1024, T], BF16)
    xsT = scr("xsT", [1024, T], F32)
    xstok = scr("xstok", [T, 1024], F32)
    btok = scr("btok", [T, 512], BF16)
    BTs = scr("BTs", [512, T], BF16)
    CTs = scr("CTs", [512, T], BF16)
    y0T = scr("y0T", [1024, T], F32)
    ossdT = scr("ossdT", [1024, T], BF16)
    XS = scr("XS", [2, 2, 4096, T], BF16)
    os5T = scr("os5T", [1024, T], BF16)
    dbg_out = None
    scr_map = dict(hT=hT, qT=qT, kT=kT, vtok=vtok, zT=zT, xbcT=xbcT, dttok=dttok, uT=uT, gT=gT,
                   oattT=oattT, xsT=xsT, xstok=xstok, btok=btok, BTs=BTs, CTs=CTs, y0T=y0T,
                   ossdT=ossdT, os5T=os5T)

    with ExitStack() as es:
        P = Prog(nc, es)
        P.scopes = bool(dbg and dbg.get("scopes"))
        uid = [0]

        def mk_sb(stack, shape, dt=F32, name=None):
            uid[0] += 1
            n = f"{name or 't'}{uid[0]}"
            t = stack.enter_context(nc.sbuf_tensor(n, list(shape), dt))
            return TL(t, Buf(n))

        psum = []
        for i in range(8):
            t = es.enter_context(nc.psum_tensor(f"ps{i}", [128, 512], F32))
            psum.append(TL(t, Buf(f"ps{i}")))

        DB = P.buf

        @contextmanager
        def phase():
            with ExitStack() as ph:
                yield lambda shape, dt=F32, name=None: mk_sb(ph, shape, dt, name)
                P.barrier()

        cst = mk_sb(es, [128, 10, 128], F32, "cst")
        P.dma(cst[:], consts.ap(), writes=[cst.b])
        IDENT, ONES, TRI_LE, TRI_GE, SGT, SLT, PERM, BLK0, BLK1, TAU = range(10)
        C_ = lambda i: cst[:, i, :]
        ones_bf = mk_sb(es, [128, 128], BF16, "onesbf")
        P.copy("dve", ones_bf[:], C_(ONES), reads=[cst.b], writes=[ones_bf.b])
        modv = mk_sb(es, [128, 144, 2], F32, "modv")
        modc = mk_sb(es, [128, 3, KC, 2], F32, "modc")
        modg = mk_sb(es, [128, 3, KC, 2], F32, "modg")
        lngt = mk_sb(es, [128, 3, KC], F32, "lngt")
        lnbt = mk_sb(es, [128, 3, KC], F32, "lnbt")

        def pm_dst(t, lead, KCt, PW):
            def f(a0, an, c0, cn):
                assert c0 % PW == 0 and cn % PW == 0
                n0, nn = c0 // PW, cn // PW
                off = lead + n0 * 128 * KCt * PW + a0 * PW
                return [AP(t, off + ai * PW, [[KCt * PW, 128], [128 * KCt * PW, nn], [1, PW]]) for ai in range(an)]
            f.PW = PW
            return f

        def pm_panel(t, lead, KCt, PW, n):
            return AP(t, lead + n * 128 * KCt * PW, [[KCt * PW, 128], [1, KCt * PW]])

        def cast3(src, dst, A, C, scale=None):
            with phase() as sb:
                st = [sb([128, 4096], F32, "cst32") for _ in range(3)]
                sbf = [sb([128, 4096], BF16, "cst16") for _ in range(3)]
                it = 0
                cb = min(C, 4096)
                ab = max(1, 4096 // cb)
                for a0 in range(0, A, ab):
                    an = min(ab, A - a0)
                    for c0 in range(0, C, cb):
                        cn = min(cb, C - c0)
                        s32 = st[it % 3]
                        s16 = sbf[it % 3]
                        v32 = s32[:, 0:an * cn].rearrange("p (a c) -> p a c", a=an)
                        v16 = s16[:, 0:an * cn].rearrange("p (a c) -> p a c", a=an)
                        P.dma(v32, src[:, a0:a0 + an, c0:c0 + cn], writes=[s32.b])
                        eng = ("dve", "act")[it % 2]
                        if scale is not None:
                            P.ts("dve", v16, v32, scale, None, ALU.mult, reads=[s32.b], writes=[s16.b])
                        else:
                            P.copy(eng, v16, v32, reads=[s32.b], writes=[s16.b])
                        if callable(dst):
                            for ai, dap in enumerate(dst(a0, an, c0, cn)):
                                P.dma(dap, v16[:, ai, :].rearrange("p (n c) -> p n c", c=dst.PW), reads=[s16.b], eng="act")
                        else:
                            P.dma(dst[:, a0:a0 + an, c0:c0 + cn], v16, reads=[s16.b], eng="pool")
                        it += 1

        def cast_w(src_ap, dst_ap, R, C, scale=None):
            cast3(src_ap.rearrange("(a p) c -> p a c", p=128),
                  dst_ap if callable(dst_ap) else dst_ap.rearrange("(a p) c -> p a c", p=128), R // 128, C, scale)

        for l in range(0 if not (dbg and dbg.get('nocast')) else 2, 2):
            for j in range(2):
                cast_w(ffn_w1.ap()[l, j], w1b.ap()[l, j], D, DFF)
                cast_w(ffn_w3.ap()[l, j], w3b.ap()[l, j], D, DFF)
                cast_w(ffn_w2.ap()[l, j], w2b.ap()[l, j], DFF, D)
            cast_w(w_in.ap()[l], winb.ap()[l], D, INC)
            for j in range(3):
                cast_w(w_branch.ap()[l, j], wbrb.ap()[l, j], 1024, D)
            cast_w(w_out.ap()[l], woutb.ap()[l], D, D)
            cast_w(glu_w.ap()[l], glub16.ap()[l], 1024, 1024)
            for d in range(2):
                for (s_, d_, sc_) in ((s5bre, sbreb, None), (s5bim, sbimb, None), (s5cre, screb, None),
                                      (s5cim, scimb, -1.0)):
                    cast3(s_.ap()[l, d].rearrange("s p c -> p s c"), d_.ap()[l, d].rearrange("s p c -> p s c"),
                          32, 128, sc_)

        def load_wpanel(tile, wsrc, kc_n, c0, cn):
            P.dma(tile[:, 0:kc_n, 0:cn], wsrc.rearrange("(a p) c -> p a c", p=128)[:, :, c0:c0 + cn],
                  writes=[tile.b])

        def layer_norm(sb_tmp, hx, TB, l, j, pss1, pss2):
            sq = sb_tmp["sq"]
            for kc in range(KC):
                P.mm(pss1[:, 0:TB], C_(ONES), hx[:, kc, 0:TB], kc == 0, kc == KC - 1,
                     reads=[cst.b, hx.b], writes=[pss1.b])
            for kc in range(KC):
                s = sq[kc % 2]
                P.act(s[:, 0:TB], hx[:, kc, 0:TB], AF.Square, reads=[hx.b], writes=[s.b])
                P.mm(pss2[:, 0:TB], C_(ONES), s[:, 0:TB], kc == 0, kc == KC - 1,
                     reads=[cst.b, s.b], writes=[pss2.b])
            mean, rstd, nmr = sb_tmp["mean"], sb_tmp["rstd"], sb_tmp["nmr"]
            P.ts("dve", mean[:, 0:TB], pss1[:, 0:TB], 1.0 / D, None, ALU.mult, reads=[pss1.b], writes=[mean.b])
            P.tt("dve", nmr[:, 0:TB], mean[:, 0:TB], mean[:, 0:TB], ALU.mult, reads=[mean.b], writes=[nmr.b])
            P.stt("dve", rstd[:, 0:TB], pss2[:, 0:TB], 1.0 / D, nmr[:, 0:TB], ALU.mult, ALU.subtract,
                  reads=[pss2.b, nmr.b], writes=[rstd.b])
            P.act(rstd[:, 0:TB], rstd[:, 0:TB], AF.Ln, bias=kct[:, 0:1], reads=[rstd.b, kct.b], writes=[rstd.b])
            P.act(rstd[:, 0:TB], rstd[:, 0:TB], AF.Exp, scale=-0.5, reads=[rstd.b], writes=[rstd.b])
            P.stt("dve", nmr[:, 0:TB], mean[:, 0:TB], -1.0, rstd[:, 0:TB], ALU.mult, ALU.mult,
                  reads=[mean.b, rstd.b], writes=[nmr.b])
            for kc in range(KC):
                e = P.ve()
                P.tt(e, hx[:, kc, 0:TB], hx[:, kc, 0:TB], rstd[:, 0:TB], ALU.mult, reads=[hx.b, rstd.b], writes=[hx.b])
                P.tt(e, hx[:, kc, 0:TB], hx[:, kc, 0:TB], nmr[:, 0:TB], ALU.add, reads=[hx.b, nmr.b], writes=[hx.b])
                P.act(hx[:, kc, 0:TB], hx[:, kc, 0:TB], AF.Identity, bias=lnbt[:, j, kc:kc + 1],
                      scale=lngt[:, j, kc:kc + 1], reads=[hx.b, lngt.b, lnbt.b], writes=[hx.b])

        def hview(dr, t0, TB):
            return dr.ap().rearrange("(a p) t -> p a t", p=128)[:, :, t0:t0 + TB]

        def ln_tmps(sb):
            return dict(sq=[sb([128, 512], F32, "sq") for _ in range(2)], mean=sb([128, 512], F32, "mean"),
                        rstd=sb([128, 512], F32, "rstd"), nmr=sb([128, 512], F32, "nmr"))

        def mod_phase(l):
            with phase() as sb:
                cv = sb([128, KC, 2], F32, "cv")
                sc = sb([128, KC, 2], F32, "sc")
                bm = sb([128, 144], F32, "bm")
                P.dma(cv[:], cvec.ap(), writes=[cv.b])
                P.dma(bm[:], bmod.ap()[l], writes=[bm.b])
                P.dma(lngt[:], lng.ap()[l], writes=[lngt.b])
                P.dma(lnbt[:], lnb.ap()[l], writes=[lnbt.b])
                P.act(sc[:], cv[:], AF.Silu, reads=[cv.b], writes=[sc.b])
                wp = [sb([128, KC, 512], F32, "wmod") for _ in range(2)]
                wv = w_mod.ap()[l].rearrange("(a p) c -> p a c", p=128)
                for cbk in range(36):
                    w = wp[cbk % 2]
                    P.dma(w[:], wv[:, :, cbk * 512:(cbk + 1) * 512], writes=[w.b])
                    pm = psum[cbk % 2]
                    for m in range(4):
                        for kc in range(KC):
                            P.mm(pm[:, 2 * m:2 * m + 2], w[:, kc, m * 128:(m + 1) * 128], sc[:, kc, :],
                                 kc == 0, kc == KC - 1, reads=[w.b, sc.b], writes=[pm.b])
                    for m in range(4):
                        mt = cbk * 4 + m
                        P.ts("dve", modv[:, mt, :], pm[:, 2 * m:2 * m + 2], bm[:, mt:mt + 1], None, ALU.add,
                             reads=[pm.b, bm.b], writes=[modv.b])
                mv = modv[:].rearrange("p (j r k) w -> p j r k w", j=3, r=3)
                P.ts("dve", modc[:], mv[:, :, 1, :, :], 1.0, None, ALU.add, reads=[modv.b], writes=[modc.b])
                for j in range(3):
                    P.ts("dve", modg[:, j], mv[:, j, 2, :, :], 0.5 if j != 1 else 1.0, None, ALU.mult,
                         reads=[modv.b], writes=[modg.b])
            return

        def mshift(j, kc, which):
            return modv[:, (3 * j) * KC + kc, which:which + 1]

        def modulate(xm, hx, j, TB, which):
            for kc in range(KC):
                P.act(xm[:, kc, 0:TB], hx[:, kc, 0:TB], AF.Identity, bias=mshift(j, kc, which),
                      scale=modc[:, j, kc, which:which + 1], reads=[hx.b, modv.b, modc.b], writes=[xm.b])

        def ffn_phase(l, jj, src, dst, skip_ctx, to_out):
            j = 0 if jj == 0 else 2
            W1 = w1b.ap()[l, jj]
            W3 = w3b.ap()[l, jj]
            W2 = w2b.ap()[l, jj]
            with phase() as sb:
                hx = sb([128, KC, 512], F32, "hx")
                xm = sb([128, KC, 512], BF16, "xm")
                g = sb([128, FC, 512], BF16, "g")
                w1p = [sb([128, KC, 256], BF16, "w1p") for _ in range(2)]
                w3p = [sb([128, KC, 256], BF16, "w3p") for _ in range(2)]
                w2p = [sb([128, FC, 128], BF16, "w2p") for _ in range(2)]
                sa = [sb([128, 512], F32, "sa") for _ in range(2)]
                lt = ln_tmps(sb)
                for (t0, TB, which) in blocks:
                    if skip_ctx and which == 1:
                        continue
                    P.dma(hx[:, :, 0:TB], hview(src, t0, TB), reads=[DB(tname[id(src)])], writes=[hx.b])
                    modulate(xm, hx, j, TB, which)
                    for kc in range(KC):
                        P.ts(P.ve(), hx[:, kc, 0:TB], hx[:, kc, 0:TB], ALPHA, None, ALU.mult, reads=[hx.b], writes=[hx.b])
                    for fp in range(FC // 2):
                        a, b = w1p[fp % 2], w3p[fp % 2]
                        load_wpanel(a, W1, KC, fp * 256, 256)
                        load_wpanel(b, W3, KC, fp * 256, 256)
                        for f2 in range(2):
                            f = fp * 2 + f2
                            pa, pb = psum[(f % 2) * 2], psum[(f % 2) * 2 + 1]
                            for kc in range(KC):
                                P.mm(pa[:, 0:TB], a[:, kc, f2 * 128:(f2 + 1) * 128], xm[:, kc, 0:TB], kc == 0, kc == KC - 1,
                                     reads=[a.b, xm.b], writes=[pa.b])
                            for kc in range(KC):
                                P.mm(pb[:, 0:TB], b[:, kc, f2 * 128:(f2 + 1) * 128], xm[:, kc, 0:TB], kc == 0, kc == KC - 1,
                                     reads=[b.b, xm.b], writes=[pb.b])
                            s = sa[f % 2]
                            P.act(s[:, 0:TB], pa[:, 0:TB], AF.Silu, reads=[pa.b], writes=[s.b])
                            P.tt("dve", g[:, f, 0:TB], s[:, 0:TB], pb[:, 0:TB], ALU.mult, reads=[s.b, pb.b], writes=[g.b])
                    for m in range(KC):
                        w = w2p[m % 2]
                        load_wpanel(w, W2, FC, m * 128, 128)
                        py = psum[4 + m % 2]
                        for kc in range(FC):
                            P.mm(py[:, 0:TB], w[:, kc, :], g[:, kc, 0:TB], kc == 0, kc == FC - 1,
                                 reads=[w.b, g.b], writes=[py.b])
                        P.stt("dve", hx[:, m, 0:TB], py[:, 0:TB], modg[:, j, m, which:which + 1], hx[:, m, 0:TB],
                              ALU.mult, ALU.add, reads=[py.b, modg.b, hx.b], writes=[hx.b])
                    layer_norm(lt, hx, TB, l, j, psum[6], psum[7])
                    if to_out:
                        P.dma(yT.ap().rearrange("(a p) t -> p a t", p=128)[:, :, t0 - NCTX:t0 - NCTX + TB],
                              hx[:, :, 0:TB], reads=[hx.b], eng="pool", is_out=True)
                    else:
                        P.dma(hview(dst, t0, TB), hx[:, :, 0:TB], reads=[hx.b], writes=[DB(tname[id(dst)])], eng="pool")

        def inproj_phase(l):
            W = winb.ap()[l]
            fm_specs = [(0, qT, BF16, True), (1024, kT, BF16, True), (3072, zT, F32, False),
                        (4096, xbcT, F32, False), (4096 + 1024, xbcT, F32, False), (6176, uT, F32, False)] + \
                       [(7200 + 1024 * i, gT, F32, False) for i in range(6)]
            with phase() as sb:
                hx = sb([128, KC, 512], F32, "hx")
                xm = sb([128, KC, 512], BF16, "xm")
                wp = [sb([128, KC, 512], BF16, "wp") for _ in range(2)]
                st32 = [sb([128, 4, 512], F32, "st32") for _ in range(2)]
                st16 = [sb([128, 4, 512], BF16, "st16") for _ in range(2)]
                qf = [sb([128, 512], F32, "qf") for _ in range(2)]
                rc = sb([128, 512], F32, "rc")
                rs = sb([128, 512], F32, "rs")
                vst = [sb([128, 1024], BF16, "vst") for _ in range(2)]
                dst_ = [sb([128, 32], F32, "dst") for _ in range(2)]
                wdt = sb([128, KC, 32], BF16, "wdt")
                it = 0
                for (t0, TB, which) in blocks:
                    P.dma(hx[:, :, 0:TB], hview(hT, t0, TB), reads=[DB("hT")], writes=[hx.b])
                    modulate(xm, hx, 1, TB, which)
                    if which == 0:
                        P.dma(rc[:, 0:TB], ropec.ap()[:, t0 - NCTX:t0 - NCTX + TB], writes=[rc.b])
                        P.dma(rs[:, 0:TB], ropes.ap()[:, t0 - NCTX:t0 - NCTX + TB], writes=[rs.b])
                    for si, (c0, dstT, dt, rope) in enumerate(fm_specs):
                        for pn in range(2):
                            w = wp[it % 2]
                            cc = c0 + pn * 512
                            load_wpanel(w, W, KC, cc, 512)
                            stg = (st16 if dt == BF16 else st32)[it % 2]
                            for m in range(4):
                                pm = psum[(it * 4 + m) % 4]
                                for kc in range(KC):
                                    P.mm(pm[:, 0:TB], w[:, kc, m * 128:(m + 1) * 128], xm[:, kc, 0:TB], kc == 0, kc == KC - 1,
                                         reads=[w.b, xm.b], writes=[pm.b])
                                if rope and which == 0:
                                    q_ = qf[m % 2]
                                    P.copy("act", q_[:, 0:TB], pm[:, 0:TB], reads=[pm.b], writes=[q_.b])
                                    pr = psum[4 + m % 2]
                                    P.mm(pr[:, 0:TB], C_(PERM), q_[:, 0:TB], True, True, reads=[cst.b, q_.b], writes=[pr.b])
                                    P.tt("dve", q_[:, 0:TB], q_[:, 0:TB], rc[:, 0:TB], ALU.mult, reads=[q_.b, rc.b], writes=[q_.b])
                                    P.stt("dve", stg[:, m, 0:TB], pr[:, 0:TB], 1.0, rs[:, 0:TB], ALU.mult, ALU.mult,
                                          reads=[pr.b, rs.b], writes=[stg.b])
                                    P.tt("dve", stg[:, m, 0:TB], stg[:, m, 0:TB], q_[:, 0:TB], ALU.add,
                                         reads=[stg.b, q_.b], writes=[stg.b])
                                else:
                                    P.copy("act" if m % 2 else "dve", stg[:, m, 0:TB], pm[:, 0:TB], reads=[pm.b], writes=[stg.b])
                            rows = (cc - c0) + (0 if dstT not in (xbcT, gT) else (c0 - (4096 if dstT is xbcT else 7200)))
                            dv = dstT.ap().rearrange("(a p) t -> p a t", p=128)[:, rows // 128:rows // 128 + 4, t0:t0 + TB]
                            P.dma(dv, stg[:, :, 0:TB], reads=[stg.b], writes=[DB(tname[id(dstT)])], eng="pool")
                            it += 1
                    wv_ = [wp[0], wp[1]]
                    load_wpanel(wv_[0], W, KC, 2048, 512)
                    load_wpanel(wv_[1], W, KC, 2560, 512)
                    P.dma(wdt[:], W.rearrange("(a p) c -> p a c", p=128)[:, :, 6144:6176], writes=[wdt.b])
                    wdtv = wdt
                    for tt_ in range(TB // 128):
                        vs = vst[tt_ % 2]
                        for hh in range(2):
                            pm = psum[(tt_ * 2 + hh) % 4]
                            for kc in range(KC):
                                P.mm(pm[:, :], xm[:, kc, tt_ * 128:(tt_ + 1) * 128], wv_[hh][:, kc, :], kc == 0, kc == KC - 1,
                                     reads=[xm.b, wv_[hh].b], writes=[pm.b])
                            P.copy("act" if hh else "dve", vs[:, hh * 512:(hh + 1) * 512], pm[:, :], reads=[pm.b], writes=[vs.b])
                        P.dma(vtok.ap()[t0 + tt_ * 128:t0 + (tt_ + 1) * 128, :], vs[:], reads=[vs.b], writes=[DB("vtok")], eng="pool")
                        pd = psum[4 + tt_ % 2]
                        for kc in range(KC):
                            P.mm(pd[:, 0:32], xm[:, kc, tt_ * 128:(tt_ + 1) * 128], wdtv[:, kc, :], kc == 0, kc == KC - 1,
                                 reads=[xm.b, wdt.b], writes=[pd.b])
                        ds = dst_[tt_ % 2]
                        P.copy("dve", ds[:], pd[:, 0:32], reads=[pd.b], writes=[ds.b])
                        P.dma(dttok.ap()[t0 + tt_ * 128:t0 + (tt_ + 1) * 128, :], ds[:], reads=[ds.b], writes=[DB("dttok")], eng="pool")

        def att_phase(l, last):
            lam_init = 0.8 - 0.6 * math.exp(-0.3 * l)
            with phase() as sb:
                al = sb([128, 256], F32, "al")
                sw = sb([128, 1], F32, "sw")
                sm = sb([128, 8], F32, "sm")
                P.dma(al[:], attlam.ap()[l], writes=[al.b])
                P.dma(sw[:], subln.ap()[l], writes=[sw.b])
                pr_ = sb([128, 128], F32, "pr_")
                P.tt("dve", pr_[:, 0:64], al[:, 0:64], al[:, 64:128], ALU.mult, reads=[al.b], writes=[pr_.b])
                P.tt("dve", pr_[:, 64:128], al[:, 128:192], al[:, 192:256], ALU.mult, reads=[al.b], writes=[pr_.b])
                P.emit("dve", lambda e: e.tensor_reduce(sm[:, 0:2], pr_[:].rearrange("p (a b) -> p a b", a=2), AX.X, ALU.add),
                       reads=[pr_.b], writes=[sm.b])
                P.act(sm[:, 2:4], sm[:, 0:2], AF.Exp, reads=[sm.b], writes=[sm.b])
                P.tt("dve", sm[:, 4:5], sm[:, 3:4], sm[:, 2:3], ALU.subtract, reads=[sm.b], writes=[sm.b])
                P.ts("dve", sm[:, 5:6], sm[:, 4:5], -lam_init, None, ALU.add, reads=[sm.b], writes=[sm.b])
                P.ts("dve", sm[:, 6:7], sw[:, 0:1], 1.0 - lam_init, None, ALU.mult, reads=[sw.b], writes=[sm.b])
                neglam = sm[:, 5:6]
                swl = sm[:, 6:7]
                qh = [sb([128, T], BF16, "qh") for _ in range(2)]
                kh = [sb([128, T], BF16, "kh") for _ in range(2)]
                vh = [sb([128, NCH, 128], BF16, "vh") for _ in range(2)]
                sqt = [sb([128, 512], F32, "sqt") for _ in range(2)]
                mx = sb([128, 16], F32, "mx")
                negc = sb([128, 2], F32, "negc")
                pt = [sb([128, 512], BF16, "pt") for _ in range(4)]
                o0 = sb([128, 512], F32, "o0")
                o1 = sb([128, 512], F32, "o1")
                r0 = sb([128, 512], F32, "r0")
                ob = [sb([128, 512], BF16, "ob") for _ in range(2)]
                for h in range(8):
                    q_, k_, v_ = qh[h % 2], kh[h % 2], vh[h % 2]
                    P.dma(q_[:], qT.ap()[h * 128:(h + 1) * 128, :], reads=[DB("qT")], writes=[q_.b])
                    P.dma(k_[:], kT.ap()[h * 128:(h + 1) * 128, :], reads=[DB("kT")], writes=[k_.b])
                    P.dma(v_[:], vtok.ap().rearrange("(a p) c -> p a c", p=128)[:, :, h * 128:(h + 1) * 128],
                          reads=[DB("vtok")], writes=[v_.b])
                    P.memset("dve", mx[:], 0.0, writes=[mx.b])
                    it = 0
                    for qi, src_ in enumerate((q_, k_)):
                        for (t0, TB, which) in blocks:
                            s = sqt[it % 2]
                            P.act(s[:, 0:TB], src_[:, t0:t0 + TB], AF.Square, reads=[src_.b], writes=[s.b])
                            for jm in range(2):
                                pm = psum[(it * 2 + jm) % 4]
                                P.mm(pm[:, 0:TB], C_(BLK0 + jm), s[:, 0:TB], True, True, reads=[cst.b, s.b], writes=[pm.b])
                                col = 8 + qi * 2 + jm
                                P.emit("dve", lambda e, pm=pm, TB=TB, col=col: e.tensor_reduce(mx[:, col:col + 1], pm[:, 0:TB], AX.X, ALU.max),
                                       reads=[pm.b], writes=[mx.b])
                                c2 = qi * 2 + jm
                                P.tt("dve", mx[:, c2:c2 + 1], mx[:, c2:c2 + 1], mx[:, col:col + 1], ALU.max, reads=[mx.b], writes=[mx.b])
                            it += 1
                    P.tt("dve", negc[:], mx[:, 0:2], mx[:, 2:4], ALU.mult, reads=[mx.b], writes=[negc.b])
                    P.act(negc[:], negc[:], AF.Ln, reads=[negc.b], writes=[negc.b])
                    P.act(negc[:], negc[:], AF.Exp, scale=0.5, reads=[negc.b], writes=[negc.b])
                    P.ts("dve", negc[:], negc[:], -0.125, None, ALU.mult, reads=[negc.b], writes=[negc.b])
                    for bi, (t0, TB, which) in enumerate(blocks):
                        if which == 1 and last:
                            continue
                        nk = 2 if which == 1 else NCH
                        pO = (psum[4], psum[5])
                        pS = (psum[6], psum[7])
                        def qk(kt):
                            for jm in range(2):
                                pm = psum[(kt % 2) * 2 + jm]
                                P.mm(pm[:, 0:TB], k_[jm * 64:(jm + 1) * 64, kt * 128:(kt + 1) * 128],
                                     q_[jm * 64:(jm + 1) * 64, t0:t0 + TB], True, True, reads=[k_.b, q_.b], writes=[pm.b])

                        def rest(kt):
                            for jm in range(2):
                                pm = psum[(kt % 2) * 2 + jm]
                                p_ = pt[(kt % 2) * 2 + jm]
                                P.act(p_[:, 0:TB], pm[:, 0:TB], AF.Exp, bias=negc[:, jm:jm + 1], scale=0.125,
                                      reads=[pm.b, negc.b], writes=[p_.b])
                            for jm in range(2):
                                p_ = pt[(kt % 2) * 2 + jm]
                                P.mm(pO[jm][:, 0:TB], v_[:, kt, :], p_[:, 0:TB], kt == 0, kt == nk - 1,
                                     reads=[v_.b, p_.b], writes=[pO[jm].b])
                                P.mm(pS[jm][:, 0:TB], ones_bf[:], p_[:, 0:TB], kt == 0, kt == nk - 1,
                                     reads=[ones_bf.b, p_.b], writes=[pS[jm].b])

                        qk(0)
                        for kt in range(nk):
                            if kt + 1 < nk:
                                qk(kt + 1)
                            rest(kt)
                        P.emit("dve", lambda e, TB=TB: e.reciprocal(r0[:, 0:TB], pS[0][:, 0:TB]), reads=[pS[0].b], writes=[r0.b])
                        P.tt("dve", o0[:, 0:TB], pO[0][:, 0:TB], r0[:, 0:TB], ALU.mult, reads=[pO[0].b, r0.b], writes=[o0.b])
                        P.emit("dve", lambda e, TB=TB: e.reciprocal(r0[:, 0:TB], pS[1][:, 0:TB]), reads=[pS[1].b], writes=[r0.b])
                        P.tt("dve", o1[:, 0:TB], pO[1][:, 0:TB], r0[:, 0:TB], ALU.mult, reads=[pO[1].b, r0.b], writes=[o1.b])
                        P.stt("dve", o0[:, 0:TB], o1[:, 0:TB], neglam, o0[:, 0:TB], ALU.mult, ALU.add,
                              reads=[o1.b, o0.b, sm.b], writes=[o0.b])
                        P.act(o1[:, 0:TB], o0[:, 0:TB], AF.Square, reads=[o0.b], writes=[o1.b])
                        pm = psum[0]
                        P.mm(pm[:, 0:TB], C_(ONES), o1[:, 0:TB], True, True, reads=[cst.b, o1.b], writes=[pm.b])
                        P.act(r0[:, 0:TB], pm[:, 0:TB], AF.Ln, bias=kct[:, 1:2], scale=1.0 / 128, reads=[pm.b, kct.b], writes=[r0.b])
                        P.act(r0[:, 0:TB], r0[:, 0:TB], AF.Exp, scale=-0.5, reads=[r0.b], writes=[r0.b])
                        o_ = ob[bi % 2]
                        P.stt("dve", o_[:, 0:TB], o0[:, 0:TB], swl, r0[:, 0:TB], ALU.mult, ALU.mult,
                              reads=[o0.b, sm.b, r0.b], writes=[o_.b])
                        P.dma(oattT.ap()[h * 128:(h + 1) * 128, t0:t0 + TB], o_[:, 0:TB], reads=[o_.b],
                              writes=[DB("oattT")], eng="pool")

        def conv_phase(l):
            with phase() as sb:
                cw = sb([128, 16, 5], F32, "cw")
                cb = sb([128, 16], F32, "cb")
                P.dma(cw[:], convw.ap()[l], writes=[cw.b])
                P.dma(cb[:], convb.ap()[l], writes=[cb.b])
                xp = [sb([128, T + 8], F32, "xp") for _ in range(2)]
                acc = [sb([128, T], F32, "acc") for _ in range(2)]
                o16 = [sb([128, T], BF16, "o16") for _ in range(2)]
                tk32 = [sb([128, NCH, 128], F32, "tk32") for _ in range(1)]
                tk16 = [sb([128, NCH, 128], BF16, "tk16") for _ in range(1)]
                for x_ in xp:
                    P.memset("dve", x_[:], 0.0, writes=[x_.b])
                segs = [(0, NCTX, 2), (NCTX, NX, 6)]
                for ct in range(16):
                    x_, a_ = xp[ct % 2], acc[ct % 2]
                    for (s0, sl, off) in segs:
                        P.dma(x_[:, off + s0:off + s0 + sl], xbcT.ap()[ct * 128:(ct + 1) * 128, s0:s0 + sl],
                              reads=[DB("xbcT")], writes=[x_.b])
                    for (s0, sl, off) in segs:
                        e = "dve"
                        base = off + s0 - 2
                        P.ts(e, a_[:, s0:s0 + sl], x_[:, base:base + sl], cw[:, ct, 0:1], None, ALU.mult,
                             reads=[x_.b, cw.b], writes=[a_.b])
                        for k in range(1, 5):
                            P.stt(e, a_[:, s0:s0 + sl], x_[:, base + k:base + k + sl], cw[:, ct, k:k + 1], a_[:, s0:s0 + sl],
                                  ALU.mult, ALU.add, reads=[x_.b, cw.b, a_.b], writes=[a_.b])
                    P.act(a_[:], a_[:], AF.Silu, bias=cb[:, ct:ct + 1], reads=[a_.b, cb.b], writes=[a_.b])
                    if ct < 8:
                        P.dma(xsT.ap()[ct * 128:(ct + 1) * 128, :], a_[:], reads=[a_.b], writes=[DB("xsT")], eng="pool")
                        tk = tk32[0]
                        for c4 in range(0, NCH, 4):
                            n4 = min(4, NCH - c4)
                            pm = psum[(c4 // 4) % 4]
                            for i in range(n4):
                                P.tr(pm[:, i * 128:(i + 1) * 128], a_[:, (c4 + i) * 128:(c4 + i + 1) * 128], C_(IDENT),
                                     reads=[a_.b, cst.b], writes=[pm.b])
                            P.copy("act" if (c4 // 4) % 2 else "dve", tk[:, c4:c4 + n4, :],
                                   pm[:, 0:n4 * 128].rearrange("p (a c) -> p a c", a=n4), reads=[pm.b], writes=[tk.b])
                        P.dma(xstok.ap().rearrange("(a p) c -> p a c", p=128)[:, :, ct * 128:(ct + 1) * 128], tk[:],
                              reads=[tk.b], writes=[DB("xstok")], eng="pool")
                    else:
                        o_ = o16[ct % 2]
                        P.copy("act", o_[:], a_[:], reads=[a_.b], writes=[o_.b])
                        if ct < 12:
                            g_ = ct - 8
                            P.dma(BTs.ap()[g_ * 128:(g_ + 1) * 128, :], o_[:], reads=[o_.b], writes=[DB("BTs")], eng="pool")
                            tk = tk16[0]
                            for c4 in range(0, NCH, 4):
                                n4 = min(4, NCH - c4)
                                pm = psum[(c4 // 4) % 4]
                                for i in range(n4):
                                    P.tr(pm[:, i * 128:(i + 1) * 128], a_[:, (c4 + i) * 128:(c4 + i + 1) * 128], C_(IDENT),
                                         reads=[a_.b, cst.b], writes=[pm.b])
                                P.copy("act" if (c4 // 4) % 2 else "dve", tk[:, c4:c4 + n4, :],
                                       pm[:, 0:n4 * 128].rearrange("p (a c) -> p a c", a=n4), reads=[pm.b], writes=[tk.b])
                            P.dma(btok.ap().rearrange("(a p) c -> p a c", p=128)[:, :, g_ * 128:(g_ + 1) * 128], tk[:],
                                  reads=[tk.b], writes=[DB("btok")], eng="pool")
                        else:
                            g_ = ct - 12
                            P.dma(CTs.ap()[g_ * 128:(g_ + 1) * 128, :], o_[:], reads=[o_.b], writes=[DB("CTs")], eng="pool")

        def ssd_phase(l, last):
            nctx_ch = NCTX // 128
            with phase() as sb:
                al_ = sb([128, 32], F32, "al_")
                db_ = sb([128, 32], F32, "db_")
                A_ = sb([128, 32], F32, "A_")
                Dt = sb([128, 8], F32, "Dt")
                nw = sb([128, 8], F32, "nw")
                P.dma(al_[:], alog.ap()[l], writes=[al_.b])
                P.dma(db_[:], dtb.ap()[l], writes=[db_.b])
                P.dma(Dt[:], ssdD.ap()[l], writes=[Dt.b])
                P.dma(nw[:], ssdnw.ap()[l], writes=[nw.b])
                P.act(A_[:], al_[:], AF.Exp, reads=[al_.b], writes=[A_.b])
                P.ts("dve", A_[:], A_[:], -1.0, None, ALU.mult, reads=[A_.b], writes=[A_.b])
                H = sb([128, 1024], F32, "H")
                Hb = sb([128, 1024], BF16, "Hb")
                xs = [sb([128, 1024], F32, "xs") for _ in range(2)]
                bt = [sb([128, 512], BF16, "bt") for _ in range(2)]
                BT = [sb([128, 4, 128], BF16, "BT") for _ in range(2)]
                CT = [sb([128, 4, 128], BF16, "CT") for _ in range(2)]
                dr = [sb([128, 32], F32, "dr") for _ in range(2)]
                sm = sb([128, 8, 32], F32, "sm")
                xdt = sb([128, 1024], BF16, "xdt")
                xde = sb([128, 1024], BF16, "xde")
                Gm = sb([128, 512], F32, "Gm")
                rseg = sb([128, 16, 128], F32, "rseg")
                dcy = [sb([128, 512], F32, "dcy") for _ in range(2)]
                ecs = [sb([128, 512], F32, "ecs") for _ in range(2)]
                Mt = [sb([128, 4, 128], BF16, "Mt") for _ in range(2)]
                Cp = [sb([128, 4, 128], BF16, "Cp") for _ in range(2)]
                ysb = sb([128, 8, 128], F32, "ysb")
                y0 = sb([128, 8, 128], F32, "y0")
                xf = sb([128, 8, 128], F32, "xf")
                zf = sb([128, 8, 128], F32, "zf")
                sq = sb([128, 128], F32, "sq")
                rst = sb([128, 4, 128], F32, "rst")
                ob = sb([128, 8, 128], BF16, "ob")
                for d in range(2):
                    if d == 0:
                        order = list(range(NCH))
                    else:
                        order = list(range(nctx_ch - 1, -1, -1)) + list(range(NCH - 1, nctx_ch - 1, -1))
                    LM = C_(TRI_LE if d == 0 else TRI_GE)
                    U = C_(SGT if d == 0 else SLT)
                    MASK = LM
                    P.memset("dve", H[:], 0.0, writes=[H.b])
                    for oi, c in enumerate(order):
                        if last and d == 1 and c < nctx_ch and False:
                            pass
                        t0 = c * 128
                        x_, b_, B_, C2, d_ = xs[oi % 2], bt[oi % 2], BT[oi % 2], CT[oi % 2], dr[oi % 2]
                        P.dma(x_[:], xstok.ap()[t0:t0 + 128, :], reads=[DB("xstok")], writes=[x_.b])
                        P.dma(b_[:], btok.ap()[t0:t0 + 128, :], reads=[DB("btok")], writes=[b_.b])
                        P.dma(B_[:], BTs.ap().rearrange("(g n) t -> n g t", g=4)[:, :, t0:t0 + 128], reads=[DB("BTs")], writes=[B_.b])
                        P.dma(C2[:], CTs.ap().rearrange("(g n) t -> n g t", g=4)[:, :, t0:t0 + 128], reads=[DB("CTs")], writes=[C2.b])
                        P.dma(d_[:], dttok.ap()[t0:t0 + 128, :], reads=[DB("dttok")], writes=[d_.b])
                        xx, ax, ex, ln_, dtp, adt, toe, w2_ = (sm[:, i, :] for i in range(8))
                        P.tt("dve", xx, d_[:], db_[:], ALU.add, reads=[d_.b, db_.b], writes=[sm.b])
                        P.stt("dve", ax, xx, -1.0, xx, ALU.mult, ALU.max, reads=[sm.b], writes=[sm.b])
                        P.act(ex, ax, AF.Exp, scale=-1.0, reads=[sm.b], writes=[sm.b])
                        P.act(ln_, ex, AF.Ln, bias=kct[:, 2:3], reads=[sm.b, kct.b], writes=[sm.b])
                        P.stt("dve", dtp, xx, 0.0, ln_, ALU.max, ALU.add, reads=[sm.b], writes=[sm.b])
                        P.tt("dve", adt, dtp, A_[:], ALU.mult, reads=[sm.b, A_.b], writes=[sm.b])
                        dsl = slice(d * 16, (d + 1) * 16)
                        pe_ = psum[0]
                        P.mm(pe_[:, 0:16], U, adt[:, dsl], True, True, reads=[cst.b, sm.b], writes=[pe_.b])
                        P.mm(pe_[:, 16:32], C_(ONES), adt[:, dsl], True, True, reads=[cst.b, sm.b], writes=[pe_.b])
                        P.act(toe, pe_[:, 0:32], AF.Exp, reads=[pe_.b], writes=[sm.b])
                        P.tt("dve", w2_[:, 0:16], dtp[:, dsl], toe[:, 0:16], ALU.mult, reads=[sm.b], writes=[sm.b])
                        x3 = x_[:].rearrange("p (h q) -> p h q", h=16)
                        P.tt("dve", xdt[:].rearrange("p (h q) -> p h q", h=16), x3,
                             dtp[:, dsl].unsqueeze(2).to_broadcast([128, 16, 64]), ALU.mult, reads=[x_.b, sm.b], writes=[xdt.b])
                        P.tt("dve", xde[:].rearrange("p (h q) -> p h q", h=16), x3,
                             w2_[:, 0:16].unsqueeze(2).to_broadcast([128, 16, 64]), ALU.mult, reads=[x_.b, sm.b], writes=[xde.b])
                        for g_ in range(4):
                            pS_ = psum[1 + g_ // 2]
                            P.mm(pS_[:, (g_ % 2) * 256:(g_ % 2 + 1) * 256], b_[:, g_ * 128:(g_ + 1) * 128],
                                 xde[:, g_ * 256:(g_ + 1) * 256], True, True, reads=[b_.b, xde.b], writes=[pS_.b])
                        P.copy("act", Hb[:], H[:], reads=[H.b], writes=[Hb.b])
                        pG = psum[3]
                        for g_ in range(4):
                            P.mm(pG[:, g_ * 128:(g_ + 1) * 128], B_[:, g_, :], C2[:, g_, :], True, True,
                                 reads=[B_.b, C2.b], writes=[pG.b])
                        P.tt("dve", Gm[:].rearrange("p (g l) -> p g l", g=4), pG[:].rearrange("p (g l) -> p g l", g=4),
                             MASK.unsqueeze(1).to_broadcast([128, 4, 128]), ALU.mult, reads=[pG.b, cst.b], writes=[Gm.b])
                        P.tt("dve", rseg[:], LM.unsqueeze(1).to_broadcast([128, 16, 128]),
                             adt[:, dsl].unsqueeze(2).to_broadcast([128, 16, 128]), ALU.mult, reads=[cst.b, sm.b], writes=[rseg.b])
                        for g_ in range(4):
                            rr_ = rseg[:, 4 * g_:4 * g_ + 4, :].rearrange("p h l -> p (h l)")
                            pseg = psum[4 + g_ % 2]
                            pcs = psum[6 + g_ % 2]
                            P.mm(pseg[:], U, rr_, True, True, reads=[cst.b, rseg.b], writes=[pseg.b])
                            P.mm(pcs[:], C_(ONES), rr_, True, True, reads=[cst.b, rseg.b], writes=[pcs.b])
                            dc, ec, M_, Cq = dcy[g_ % 2], ecs[g_ % 2], Mt[g_ % 2], Cp[g_ % 2]
                            P.act(dc[:], pseg[:], AF.Exp, reads=[pseg.b], writes=[dc.b])
                            P.act(ec[:], pcs[:], AF.Exp, reads=[pcs.b], writes=[ec.b])
                            P.tt("dve", M_[:], dc[:].rearrange("p (h l) -> p h l", h=4),
                                 Gm[:, g_ * 128:(g_ + 1) * 128].unsqueeze(1).to_broadcast([128, 4, 128]), ALU.mult,
                                 reads=[dc.b, Gm.b], writes=[M_.b])
                            P.tt("dve", Cq[:], ec[:].rearrange("p (h l) -> p h l", h=4),
                                 C2[:, g_, :].unsqueeze(1).to_broadcast([128, 4, 128]), ALU.mult,
                                 reads=[ec.b, C2.b], writes=[Cq.b])
                            pY = psum[0] if g_ < 2 else psum[3]
                            for e_ in range(4):
                                hd = g_ * 4 + e_
                                hp = hd // 2
                                jj_ = hd % 2
                                oo = pY[jj_ * 64:(jj_ + 1) * 64, (hp % 4) * 128:(hp % 4 + 1) * 128]
                                P.mm(oo, xdt[:, hd * 64:(hd + 1) * 64], M_[:, e_, :], True, False,
                                     reads=[xdt.b, M_.b], writes=[pY.b])
                                P.mm(oo, Hb[:, hd * 64:(hd + 1) * 64], Cq[:, e_, :], False, True,
                                     reads=[Hb.b, Cq.b], writes=[pY.b])
                            if g_ % 2 == 1:
                                hp0 = (g_ - 1) * 2
                                P.copy("act", ysb[:, hp0:hp0 + 4, :], pY[:].rearrange("p (a l) -> p a l", a=4),
                                       reads=[pY.b], writes=[ysb.b])
                        P.tt("dve", H[:].rearrange("p (h q) -> p h q", h=16), H[:].rearrange("p (h q) -> p h q", h=16),
                             toe[:, 16:32].unsqueeze(2).to_broadcast([128, 16, 64]), ALU.mult, reads=[H.b, sm.b], writes=[H.b])
                        P.tt("dve", H[:, 0:512], H[:, 0:512], psum[1][:], ALU.add, reads=[H.b, psum[1].b], writes=[H.b])
                        P.tt("dve", H[:, 512:1024], H[:, 512:1024], psum[2][:], ALU.add, reads=[H.b, psum[2].b], writes=[H.b])
                        yv = y0T.ap().rearrange("(a p) t -> p a t", p=128)[:, :, t0:t0 + 128]
                        if d == 0:
                            P.dma(yv, ysb[:], reads=[ysb.b], writes=[DB("y0T")], eng="pool")
                        else:
                            if last and c < nctx_ch:
                                continue
                            P.dma(y0[:], yv, reads=[DB("y0T")], writes=[y0.b])
                            P.dma(xf[:], xsT.ap().rearrange("(a p) t -> p a t", p=128)[:, :, t0:t0 + 128], reads=[DB("xsT")], writes=[xf.b])
                            P.dma(zf[:], zT.ap().rearrange("(a p) t -> p a t", p=128)[:, :, t0:t0 + 128], reads=[DB("zT")], writes=[zf.b])
                            P.tt("dve", ysb[:], ysb[:], y0[:], ALU.add, reads=[ysb.b, y0.b], writes=[ysb.b])
                            P.tt("dve", xf[:], xf[:], Dt[:].unsqueeze(2).to_broadcast([128, 8, 128]), ALU.mult,
                                 reads=[xf.b, Dt.b], writes=[xf.b])
                            P.tt("dve", ysb[:], ysb[:], xf[:], ALU.add, reads=[ysb.b, xf.b], writes=[ysb.b])
                            P.act(zf[:], zf[:], AF.Silu, reads=[zf.b], writes=[zf.b])
                            P.tt("dve", ysb[:], ysb[:], zf[:], ALU.mult, reads=[ysb.b, zf.b], writes=[ysb.b])
                            pn = psum[3]
                            for hp in range(8):
                                P.act(sq[:], ysb[:, hp, :], AF.Square, reads=[ysb.b], writes=[sq.b])
                                P.mm(pn[:, (hp // 2) * 128:(hp // 2 + 1) * 128], C_(ONES), sq[:], hp % 2 == 0, hp % 2 == 1,
                                     reads=[cst.b, sq.b], writes=[pn.b])
                            P.act(rst[:], pn[:].rearrange("p (g l) -> p g l", g=4), AF.Ln, bias=kct[:, 1:2], scale=1.0 / 256,
                                  reads=[pn.b, kct.b], writes=[rst.b])
                            P.act(rst[:], rst[:], AF.Exp, scale=-0.5, reads=[rst.b], writes=[rst.b])
                            for g_ in range(4):
                                P.tt("dve", ysb[:, 2 * g_:2 * g_ + 2, :], ysb[:, 2 * g_:2 * g_ + 2, :],
                                     rst[:, g_, :].unsqueeze(1).to_broadcast([128, 2, 128]), ALU.mult,
                                     reads=[ysb.b, rst.b], writes=[ysb.b])
                            P.tt("dve", ob[:], ysb[:], nw[:].unsqueeze(2).to_broadcast([128, 8, 128]), ALU.mult,
                                 reads=[ysb.b, nw.b], writes=[ob.b])
                            P.dma(ossdT.ap().rearrange("(a p) t -> p a t", p=128)[:, :, t0:t0 + 128], ob[:], reads=[ob.b],
                                  writes=[DB("ossdT")], eng="pool")

        def s5_phase(l, last):
            with phase() as sb:
                prm = sb([128, 2, 24, 32], F32, "prm")
                prmi = sb([128, 128], I32, "prmi")

                def sincos(y, s_out, c_out, t1_, t2_, t3_, ti_, fb, ib, eng="dve"):
                    for off, out_ in ((0.0, s_out), (0.25, c_out)):
                        P.ts(eng, t1_, y, off, None, ALU.add, reads=[fb], writes=[fb])
                        P.copy(eng, ti_, t1_, reads=[fb], writes=[ib])
                        P.copy(eng, t2_, ti_, reads=[ib], writes=[fb])
                        P.tt(eng, t1_, t1_, t2_, ALU.subtract, reads=[fb], writes=[fb])
                        P.ts(eng, t3_, t1_, 0.5, None, ALU.is_gt, reads=[fb], writes=[fb])
                        P.tt(eng, t1_, t1_, t3_, ALU.subtract, reads=[fb], writes=[fb])
                        P.act(out_, t1_, AF.Sin, scale=2 * PI, reads=[fb], writes=[fb])
                for d in range(2):
                    pv = lambda i: prm[:, d, i, :]
                    LRE, LIM, STP, MAG, ANG, T1, ABR, ABI, DEN, KRE, KIM, T2, T3, AL, RL, CL, SL, ANGN, LNR, NLNR = range(20)
                    P.dma(pv(LRE), s5lre.ap()[l, d], writes=[prm.b])
                    P.dma(pv(LIM), s5lim.ap()[l, d], writes=[prm.b])
                    P.dma(pv(STP), s5ls.ap()[l, d], writes=[prm.b])
                    P.act(pv(STP), pv(STP), AF.Exp, reads=[prm.b], writes=[prm.b])
                    P.tt("dve", pv(MAG), pv(LRE), pv(STP), ALU.mult, reads=[prm.b], writes=[prm.b])
                    P.act(pv(MAG), pv(MAG), AF.Exp, reads=[prm.b], writes=[prm.b])
                    P.tt("dve", pv(ANG), pv(LIM), pv(STP), ALU.mult, reads=[prm.b], writes=[prm.b])
                    P.ts("dve", pv(ANGN), pv(ANG), 1.0 / (2 * PI), None, ALU.mult, reads=[prm.b], writes=[prm.b])
                    sincos(pv(ANGN), pv(ABI), pv(ABR), pv(T1), pv(T2), pv(T3), prmi[:, 0:32], prm.b, prmi.b)
                    P.tt("dve", pv(ABR), pv(ABR), pv(MAG), ALU.mult, reads=[prm.b], writes=[prm.b])
                    P.tt("dve", pv(ABI), pv(ABI), pv(MAG), ALU.mult, reads=[prm.b], writes=[prm.b])
                    P.tt("dve", pv(DEN), pv(LRE), pv(LRE), ALU.mult, reads=[prm.b], writes=[prm.b])
                    P.tt("dve", pv(T1), pv(LIM), pv(LIM), ALU.mult, reads=[prm.b], writes=[prm.b])
                    P.tt("dve", pv(DEN), pv(DEN), pv(T1), ALU.add, reads=[prm.b], writes=[prm.b])
                    P.emit("dve", lambda e, d=d: e.reciprocal(prm[:, d, DEN, :], prm[:, d, DEN, :]), reads=[prm.b], writes=[prm.b])
                    P.ts("dve", pv(T2), pv(ABR), -1.0, None, ALU.add, reads=[prm.b], writes=[prm.b])
                    P.tt("dve", pv(KRE), pv(T2), pv(LRE), ALU.mult, reads=[prm.b], writes=[prm.b])
                    P.tt("dve", pv(T1), pv(ABI), pv(LIM), ALU.mult, reads=[prm.b], writes=[prm.b])
                    P.tt("dve", pv(KRE), pv(KRE), pv(T1), ALU.add, reads=[prm.b], writes=[prm.b])
                    P.tt("dve", pv(KRE), pv(KRE), pv(DEN), ALU.mult, reads=[prm.b], writes=[prm.b])
                    P.tt("dve", pv(KIM), pv(ABI), pv(LRE), ALU.mult, reads=[prm.b], writes=[prm.b])
                    P.tt("dve", pv(T1), pv(T2), pv(LIM), ALU.mult, reads=[prm.b], writes=[prm.b])
                    P.tt("dve", pv(KIM), pv(KIM), pv(T1), ALU.subtract, reads=[prm.b], writes=[prm.b])
                    P.tt("dve", pv(KIM), pv(KIM), pv(DEN), ALU.mult, reads=[prm.b], writes=[prm.b])
                    P.ts("dve", pv(T1), pv(ANGN), 128.0, None, ALU.mult, reads=[prm.b], writes=[prm.b])
                    P.copy("dve", prmi[:, 0:32], pv(T1), reads=[prm.b], writes=[prmi.b])
                    P.copy("dve", pv(T3), prmi[:, 0:32], reads=[prmi.b], writes=[prm.b])
                    P.tt("dve", pv(AL), pv(T1), pv(T3), ALU.subtract, reads=[prm.b], writes=[prm.b])
                    P.tt("dve", pv(T1), pv(LRE), pv(STP), ALU.mult, reads=[prm.b], writes=[prm.b])
                    P.act(pv(RL), pv(T1), AF.Exp, scale=128.0, reads=[prm.b], writes=[prm.b])
                    P.copy("dve", pv(LNR), pv(T1), reads=[prm.b], writes=[prm.b])
                    P.ts("dve", pv(NLNR), pv(T1), -1.0, None, ALU.mult, reads=[prm.b], writes=[prm.b])
                rsts = sb([128, T], F32, "rsts")
                P.memset("dve", rsts[:], 1.0, writes=[rsts.b])
                P.memset("dve", rsts[:].rearrange("p (c l) -> p c l", l=128)[:, :, 0:1], 0.0, writes=[rsts.b])
                cidx = sb([128, NCH], F32, "cidx")
                P.copy("dve", cidx[:], cst[:, TAU, 0:NCH], reads=[cst.b], writes=[cidx.b])
                tbs = [sb([128, 12, 128], F32, "tb_") for _ in range(2)]
                prmi2 = sb([128, 128], I32, "prmi2")
                ct_ = sb([128, 12, NCH], F32, "ct_")
                ub = sb([128, T], BF16, "ub")
                wB = [sb([128, 2, 128], BF16, "wB") for _ in range(2)]
                xo = [sb([128, T], BF16, "xo") for _ in range(2)]

                class W2:
                    def __init__(self, nm):
                        self.tl = sb([128, T], F32, nm)
                        self.bp = Buf(nm + "p")
                        self.bd = Buf(nm + "d")
                        self.all = [self.bp, self.bd]

                    def __getitem__(self, k):
                        return self.tl[k]

                R_, I_, A_, B_ = W2("R_"), W2("I_"), W2("A_"), W2("B_")
                xob = [(Buf("xo0p"), Buf("xo0d")), (Buf("xo1p"), Buf("xo1d"))]
                nctx_ch = NCTX // 128
                nxp = max(0, int(round(0.5 * (NX // 128))) - 0)
                if NX // 128 <= 4:
                    nxp = 1
                pieces = [(0, NCTX, "dve", 0, NCTX), (NCTX, T, "dve", NCTX, NX)]
                pieces = [p for p in pieces if p[1] > p[0]]

                tbh = [None]

                def bsel(w, eng):
                    return w.bp if eng == "pool" else w.bd

                def big(out, in0, tbl_i, in1, op):
                    for (c0, c1, eng, _, _) in pieces:
                        o3 = out[:, c0:c1].rearrange("p (c l) -> p c l", l=128)
                        a3 = in0[:, c0:c1].rearrange("p (c l) -> p c l", l=128)
                        if tbl_i is not None:
                            tcur = tbh[0]
                            b3 = tcur[:, tbl_i, :].unsqueeze(1).to_broadcast([128, (c1 - c0) // 128, 128])
                            P.tt(eng, o3, a3, b3, op, reads=[bsel(in0, eng), tcur.b], writes=[bsel(out, eng)])
                        else:
                            P.tt(eng, o3, a3, in1[:, c0:c1].rearrange("p (c l) -> p c l", l=128), op,
                                 reads=[bsel(in0, eng), bsel(in1, eng)], writes=[bsel(out, eng)])

                for d in range(2):
                    pv = lambda i: prm[:, d, i, :]
                    for st in range(32):
                        ct = st // 4
                        if st % 4 == 0:
                            P.dma(ub[:], uT.ap()[ct * 128:(ct + 1) * 128, :], reads=[DB("uT")], writes=[ub.b], eng="pool")
                        w_ = wB[st % 2]
                        P.dma(w_[:, 0, :], sbreb.ap()[l, d, st], writes=[w_.b])
                        P.dma(w_[:, 1, :], sbimb.ap()[l, d, st], writes=[w_.b])
                        col = lambda i: prm[:, d, i, st:st + 1]
                        tb_ = tbs[(d * 32 + st) % 2]
                        tbh[0] = tb_
                        E_ = "dve"
                        P.ts(E_, tb_[:, 0, :], C_(TAU), col(ANGN), None, ALU.mult, reads=[cst.b, prm.b], writes=[tb_.b])
                        sincos(tb_[:, 0, :], tb_[:, 3, :], tb_[:, 2, :], tb_[:, 1, :], tb_[:, 6, :], tb_[:, 7, :], prmi2[:, 0:128], tb_.b, prmi2.b, eng=E_)
                        P.act(tb_[:, 10, :], C_(TAU), AF.Exp, scale=col(LNR), reads=[cst.b, prm.b], writes=[tb_.b])
                        P.act(tb_[:, 11, :], C_(TAU), AF.Exp, scale=col(NLNR), reads=[cst.b, prm.b], writes=[tb_.b])
                        P.ts(E_, tb_[:, 4, :], tb_[:, 2, :], col(KRE), None, ALU.mult, reads=[tb_.b, prm.b], writes=[tb_.b])
                        P.ts(E_, tb_[:, 6, :], tb_[:, 3, :], col(KIM), None, ALU.mult, reads=[tb_.b, prm.b], writes=[tb_.b])
                        P.tt(E_, tb_[:, 4, :], tb_[:, 4, :], tb_[:, 6, :], ALU.add, reads=[tb_.b], writes=[tb_.b])
                        P.ts(E_, tb_[:, 5, :], tb_[:, 2, :], col(KIM), None, ALU.mult, reads=[tb_.b, prm.b], writes=[tb_.b])
                        P.ts(E_, tb_[:, 6, :], tb_[:, 3, :], col(KRE), None, ALU.mult, reads=[tb_.b, prm.b], writes=[tb_.b])
                        P.tt(E_, tb_[:, 5, :], tb_[:, 5, :], tb_[:, 6, :], ALU.subtract, reads=[tb_.b], writes=[tb_.b])
                        P.tt(E_, tb_[:, 4, :], tb_[:, 4, :], tb_[:, 11, :], ALU.mult, reads=[tb_.b], writes=[tb_.b])
                        P.tt(E_, tb_[:, 5, :], tb_[:, 5, :], tb_[:, 11, :], ALU.mult, reads=[tb_.b], writes=[tb_.b])
                        P.tt(E_, tb_[:, 8, :], tb_[:, 2, :], tb_[:, 10, :], ALU.mult, reads=[tb_.b], writes=[tb_.b])
                        P.tt(E_, tb_[:, 9, :], tb_[:, 3, :], tb_[:, 10, :], ALU.mult, reads=[tb_.b], writes=[tb_.b])
                        for (t0, TB, which) in blocks:
                            for ri, dstb in ((0, R_), (1, I_)):
                                pm = psum[(2 * (t0 // 512) + ri) % 4]
                                P.mm(pm[:, 0:TB], w_[:, ri, :], ub[:, t0:t0 + TB], True, True, reads=[w_.b, ub.b], writes=[pm.b])
                                if d == 0:
                                    P.copy("act", dstb[:, t0:t0 + TB], pm[:, 0:TB], reads=[pm.b], writes=dstb.all)
                                else:
                                    s0, sl = (0, NCTX) if which == 1 else (NCTX, NX)
                                    p0 = s0 + (sl - 1) - (t0 + TB - 1 - s0)
                                    P.copy("act", dstb[:, p0:p0 + TB], rev_last(pm[:, 0:TB]), reads=[pm.b], writes=dstb.all)
                        big(A_, R_, 4, None, ALU.mult)
                        big(B_, I_, 5, None, ALU.mult)
                        big(A_, A_, None, B_, ALU.subtract)
                        big(B_, R_, 5, None, ALU.mult)
                        big(I_, I_, 4, None, ALU.mult)
                        big(I_, I_, None, B_, ALU.add)
                        bc3 = lambda a: a[:].rearrange("p (c l) -> p c l", l=128)
                        er, ei, vr, vi, cc_, ss_, tmp, tmp2 = (ct_[:, i, :] for i in range(8))
                        wl_r = ct_[:, 8, :]
                        wl_i = ct_[:, 9, :]
                        P.emit("dve", lambda e, A_=A_: e.tensor_reduce(ct_[:, 8, :], A_[:].rearrange("p (c l) -> p c l", l=128), AX.X, ALU.add),
                               reads=A_.all, writes=[ct_.b])
                        P.emit("dve", lambda e, I_=I_: e.tensor_reduce(ct_[:, 9, :], I_[:].rearrange("p (c l) -> p c l", l=128), AX.X, ALU.add),
                               reads=I_.all, writes=[ct_.b])
                        c127 = tb_[:, 8, 127:128]
                        s127 = tb_[:, 9, 127:128]
                        P.ts("dve", er, wl_r, c127, None, ALU.mult, reads=[ct_.b, tb_.b], writes=[ct_.b])
                        P.ts("dve", tmp, wl_i, s127, None, ALU.mult, reads=[ct_.b, tb_.b], writes=[ct_.b])
                        P.tt("dve", er, er, tmp, ALU.subtract, reads=[ct_.b], writes=[ct_.b])
                        P.ts("dve", ei, wl_i, c127, None, ALU.mult, reads=[ct_.b, tb_.b], writes=[ct_.b])
                        P.ts("dve", tmp, wl_r, s127, None, ALU.mult, reads=[ct_.b, tb_.b], writes=[ct_.b])
                        P.tt("dve", ei, ei, tmp, ALU.add, reads=[ct_.b], writes=[ct_.b])
                        P.ts("dve", tmp, cidx[:], col(AL), None, ALU.mult, reads=[cidx.b, prm.b], writes=[ct_.b])
                        sincos(tmp, ss_, cc_, tmp2, ct_[:, 10, :], ct_[:, 11, :], prmi[:, 0:NCH], ct_.b, prmi.b)
                        P.tt("dve", vr, er, cc_, ALU.mult, reads=[ct_.b], writes=[ct_.b])
                        P.tt("dve", tmp, ei, ss_, ALU.mult, reads=[ct_.b], writes=[ct_.b])
                        P.tt("dve", vr, vr, tmp, ALU.add, reads=[ct_.b], writes=[ct_.b])
                        P.tt("dve", vi, ei, cc_, ALU.mult, reads=[ct_.b], writes=[ct_.b])
                        P.tt("dve", tmp, er, ss_, ALU.mult, reads=[ct_.b], writes=[ct_.b])
                        P.tt("dve", vi, vi, tmp, ALU.subtract, reads=[ct_.b], writes=[ct_.b])
                        P.ts("dve", tmp2, cidx[:], 0.0, col(RL), ALU.mult, ALU.add, reads=[cidx.b, prm.b], writes=[ct_.b])
                        P.scan("dve", er, tmp2, vr, 0.0, reads=[ct_.b], writes=[ct_.b])
                        P.scan("dve", ei, tmp2, vi, 0.0, reads=[ct_.b], writes=[ct_.b])
                        P.tt("dve", vr, er, cc_, ALU.mult, reads=[ct_.b], writes=[ct_.b])
                        P.tt("dve", tmp, ei, ss_, ALU.mult, reads=[ct_.b], writes=[ct_.b])
                        P.tt("dve", vr, vr, tmp, ALU.subtract, reads=[ct_.b], writes=[ct_.b])
                        P.tt("dve", vi, ei, cc_, ALU.mult, reads=[ct_.b], writes=[ct_.b])
                        P.tt("dve", tmp, er, ss_, ALU.mult, reads=[ct_.b], writes=[ct_.b])
                        P.tt("dve", vi, vi, tmp, ALU.add, reads=[ct_.b], writes=[ct_.b])
                        P.ts("dve", er, vr, col(ABR), None, ALU.mult, reads=[ct_.b, prm.b], writes=[ct_.b])
                        P.ts("dve", tmp, vi, col(ABI), None, ALU.mult, reads=[ct_.b, prm.b], writes=[ct_.b])
                        P.tt("dve", er, er, tmp, ALU.subtract, reads=[ct_.b], writes=[ct_.b])
                        P.ts("dve", ei, vi, col(ABR), None, ALU.mult, reads=[ct_.b, prm.b], writes=[ct_.b])
                        P.ts("dve", tmp, vr, col(ABI), None, ALU.mult, reads=[ct_.b, prm.b], writes=[ct_.b])
                        P.tt("dve", ei, ei, tmp, ALU.add, reads=[ct_.b], writes=[ct_.b])
                        P.tt("dve", bc3(A_)[:, 1:NCH, 0], bc3(A_)[:, 1:NCH, 0], er[:, 0:NCH - 1], ALU.add, reads=A_.all + [ct_.b], writes=A_.all)
                        P.tt("dve", bc3(I_)[:, 1:NCH, 0], bc3(I_)[:, 1:NCH, 0], ei[:, 0:NCH - 1], ALU.add, reads=I_.all + [ct_.b], writes=I_.all)
                        P.scan("dve", R_[:], rsts[:], A_[:], 0.0, reads=[rsts.b] + A_.all, writes=R_.all)
                        P.scan("dve", B_[:], rsts[:], I_[:], 0.0, reads=[rsts.b] + I_.all, writes=B_.all)
                        for ri in range(2):
                            if ri == 0:
                                big(A_, R_, 8, None, ALU.mult)
                                big(I_, B_, 9, None, ALU.mult)
                                op = ALU.subtract
                            else:
                                big(A_, B_, 8, None, ALU.mult)
                                big(I_, R_, 9, None, ALU.mult)
                                op = ALU.add
                            xo_ = xo[ri]
                            for (c0, c1, eng, s0, sl) in pieces:
                                xb = xob[ri][0 if eng == "pool" else 1]
                                if d == 0:
                                    oo = xo_[:, c0:c1]
                                else:
                                    n0 = 2 * s0 + sl - c1
                                    oo = rev_last(xo_[:, n0:n0 + (c1 - c0)])
                                P.tt(eng, oo, A_[:, c0:c1], I_[:, c0:c1], op, reads=[bsel(A_, eng), bsel(I_, eng)], writes=[xb])
                            P.dma(XS.ap()[d, ri, st * 128:(st + 1) * 128, :], xo_[:], reads=list(xob[ri]), writes=[DB("XS")], eng="pool")
            with phase() as sb:
                sd = sb([128, 8], F32, "sd")
                gb = sb([128, 8], F32, "gb")
                P.dma(sd[:], s5d.ap()[l], writes=[sd.b])
                P.dma(gb[:], glub.ap()[l], writes=[gb.b])
                cw_ = sb([128, 2, 2, 32, 128], BF16, "cw_")
                for d in range(2):
                    P.dma(cw_[:, d, 0], screb.ap()[l, d].rearrange("s p c -> p s c"), writes=[cw_.b])
                    P.dma(cw_[:, d, 1], scimb.ap()[l, d].rearrange("s p c -> p s c"), writes=[cw_.b])
                gw = sb([128, 8, 1024], BF16, "gw")
                P.dma(gw[:], glub16.ap()[l].rearrange("(a p) c -> p a c", p=128), writes=[gw.b])
                xs_ = [sb([128, 16, 512], BF16, "xs_") for _ in range(2)]
                uu = [sb([128, 512], F32, "uu") for _ in range(2)]
                tt32 = sb([128, 8, 512], F32, "tt32")
                tt16 = sb([128, 8, 512], BF16, "tt16")
                sg = [sb([128, 512], F32, "sg") for _ in range(2)]
                og = [sb([128, 8, 512], BF16, "og") for _ in range(1)]
                for (t0, TB, which) in blocks:
                    if last and which == 1:
                        continue
                    for ct in range(8):
                        x_ = xs_[ct % 2]
                        u_ = uu[ct % 2]
                        for d in range(2):
                            for ri in range(2):
                                P.dma(x_[:, (d * 2 + ri) * 4:(d * 2 + ri) * 4 + 4, 0:TB],
                                      XS.ap()[d, ri, ct * 512:(ct + 1) * 512, t0:t0 + TB].rearrange("(a p) t -> p a t", p=128),
                                      reads=[DB("XS")], writes=[x_.b])
                        P.dma(u_[:, 0:TB], uT.ap()[ct * 128:(ct + 1) * 128, t0:t0 + TB], reads=[DB("uT")], writes=[u_.b])
                        pm = psum[ct % 4]
                        n = 0
                        for d in range(2):
                            for ri in range(2):
                                for s4 in range(4):
                                    P.mm(pm[:, 0:TB], cw_[:, d, ri, ct * 4 + s4, :], x_[:, (d * 2 + ri) * 4 + s4, 0:TB], n == 0, n == 15,
                                         reads=[cw_.b, x_.b], writes=[pm.b])
                                    n += 1
                        P.stt("dve", tt32[:, ct, 0:TB], u_[:, 0:TB], sd[:, ct:ct + 1], pm[:, 0:TB], ALU.mult, ALU.add,
                              reads=[u_.b, sd.b, pm.b], writes=[tt32.b])
                        P.act(tt32[:, ct, 0:TB], tt32[:, ct, 0:TB], AF.Gelu, reads=[tt32.b], writes=[tt32.b])
                        P.copy("dve", tt16[:, ct, 0:TB], tt32[:, ct, 0:TB], reads=[tt32.b], writes=[tt16.b])
                    o_ = og[0]
                    for m in range(8):
                        pm = psum[4 + m % 4]
                        for kc in range(8):
                            P.mm(pm[:, 0:TB], gw[:, kc, m * 128:(m + 1) * 128], tt16[:, kc, 0:TB], kc == 0, kc == 7,
                                 reads=[gw.b, tt16.b], writes=[pm.b])
                        s_ = sg[m % 2]
                        P.act(s_[:, 0:TB], pm[:, 0:TB], AF.Sigmoid, bias=gb[:, m:m + 1], reads=[pm.b, gb.b], writes=[s_.b])
                        P.tt("dve", o_[:, m, 0:TB], tt32[:, m, 0:TB], s_[:, 0:TB], ALU.mult, reads=[tt32.b, s_.b], writes=[o_.b])
                    P.dma(os5T.ap().rearrange("(a p) t -> p a t", p=128)[:, :, t0:t0 + TB], o_[:, :, 0:TB], reads=[o_.b],
                          writes=[DB("os5T")], eng="pool")

        def merge_phase(l, last):

            brs = (oattT, ossdT, os5T)
            with phase() as sb:
                hx = sb([128, KC, 512], F32, "hx")
                brt = [sb([128, 8, 512], BF16, "brt") for _ in range(3)]
                wbp = [sb([128, 3, 8, 128], BF16, "wbp") for _ in range(2)]
                wop = [sb([128, KC, 128], BF16, "wop") for _ in range(2)]
                gl = [sb([128, 3, 512], F32, "gl") for _ in range(2)]
                acc = sb([128, 512], F32, "acc")
                tm = sb([128, 512], F32, "tm")
                mixed = sb([128, KC, 512], BF16, "mixed")
                lt = ln_tmps(sb)
                for (t0, TB, which) in blocks:
                    if last and which == 1:
                        continue
                    P.dma(hx[:, :, 0:TB], hview(hT, t0, TB), reads=[DB("hT")], writes=[hx.b])
                    for j in range(3):
                        P.dma(brt[j][:, :, 0:TB], brs[j].ap().rearrange("(a p) t -> p a t", p=128)[:, :, t0:t0 + TB],
                              reads=[DB(tname[id(brs[j])])], writes=[brt[j].b])
                    for kc in range(KC):
                        P.ts(P.ve(), hx[:, kc, 0:TB], hx[:, kc, 0:TB], ALPHA, None, ALU.mult, reads=[hx.b], writes=[hx.b])
                    for m in range(KC):
                        w = wbp[m % 2]
                        g_ = gl[m % 2]
                        for j in range(3):
                            P.dma(w[:, j], wbrb.ap()[l, j].rearrange("(a p) c -> p a c", p=128)[:, :, m * 128:(m + 1) * 128], writes=[w.b])
                        P.dma(g_[:, :, 0:TB], gT.ap().rearrange("(j a p) t -> p j a t", j=3, p=128)[:, :, m, t0:t0 + TB],
                              reads=[DB("gT")], writes=[g_.b])
                        P.act(g_[:, :, 0:TB], g_[:, :, 0:TB], AF.Sigmoid, reads=[g_.b], writes=[g_.b])
                        for j in range(3):
                            pm = psum[(m * 3 + j) % 4]
                            for kc in range(8):
                                P.mm(pm[:, 0:TB], w[:, j, kc, :], brt[j][:, kc, 0:TB], kc == 0, kc == 7,
                                     reads=[w.b, brt[j].b], writes=[pm.b])
                            if j == 0:
                                P.tt("dve", acc[:, 0:TB], pm[:, 0:TB], g_[:, 0, 0:TB], ALU.mult, reads=[pm.b, g_.b], writes=[acc.b])
                            else:
                                P.tt("dve", tm[:, 0:TB], pm[:, 0:TB], g_[:, j, 0:TB], ALU.mult, reads=[pm.b, g_.b], writes=[tm.b])
                                if j == 1:
                                    P.tt("dve", acc[:, 0:TB], acc[:, 0:TB], tm[:, 0:TB], ALU.add, reads=[acc.b, tm.b], writes=[acc.b])
                                else:
                                    P.tt("dve", mixed[:, m, 0:TB], acc[:, 0:TB], tm[:, 0:TB], ALU.add, reads=[acc.b, tm.b], writes=[mixed.b])
                    for m in range(KC):
                        w = wop[m % 2]
                        load_wpanel(w, woutb.ap()[l], KC, m * 128, 128)
                        py = psum[4 + m % 2]
                        for kc in range(KC):
                            P.mm(py[:, 0:TB], w[:, kc, :], mixed[:, kc, 0:TB], kc == 0, kc == KC - 1, reads=[w.b, mixed.b], writes=[py.b])
                        P.stt("dve", hx[:, m, 0:TB], py[:, 0:TB], modg[:, 1, m, which:which + 1], hx[:, m, 0:TB],
                              ALU.mult, ALU.add, reads=[py.b, modg.b, hx.b], writes=[hx.b])
                    layer_norm(lt, hx, TB, l, 1, psum[6], psum[7])
                    P.dma(hview(hT, t0, TB), hx[:, :, 0:TB], reads=[hx.b], writes=[DB("hT")], eng="pool")

        negpi = mk_sb(es, [128, 1], F32, "negpi")
        P.memset("dve", negpi[:], -PI, writes=[negpi.b])
        kct = mk_sb(es, [128, 4], F32, "kct")
        P.memset("dve", kct[:, 0:1], 1e-5, writes=[kct.b])
        P.memset("dve", kct[:, 1:2], 1e-6, writes=[kct.b])
        P.memset("dve", kct[:, 2:3], 1.0, writes=[kct.b])
        stages = dbg.get("stages") if dbg else None

        def on(name, l):
            return stages is None or (name, l) in stages or name in stages

        for l in range(DEPTH):
            last = (l == DEPTH - 1)
            if on("mod", l):
                P.phase_name = 'mod_phase' + str(l)
                mod_phase(l)
            if on("ffn1", l):
                P.phase_name = 'ffn1_' + str(l)
                ffn_phase(l, 0, xT if l == 0 else hT, hT, False, False)
            if on("inproj", l):
                P.phase_name = 'inproj_phase' + str(l)
                inproj_phase(l)
            if on("att", l):
                P.phase_name = 'att_phase' + str(l)
                att_phase(l, last)
            if on("conv", l):
                P.phase_name = 'conv_phase' + str(l)
                conv_phase(l)
            if on("ssd", l):
                P.phase_name = 'ssd_phase' + str(l)
                ssd_phase(l, last)
            if on("s5", l):
                P.phase_name = 's5_phase' + str(l)
                s5_phase(l, last)
            if on("merge", l):
                P.phase_name = 'merge_phase' + str(l)
                merge_phase(l, last)
            if on("ffn2", l):
                P.phase_name = 'ffn2_' + str(l)
                ffn_phase(l, 1, hT, hT, last, last)
        if dbg and dbg.get("dump"):
            for nm in dbg["dump"]:
                s_ = scr_map[nm]
                shp = list(s_.ap().shape)
                o = nc.dram_tensor("dbg_" + nm, shp, s_.ap().dtype, kind="ExternalOutput")
                rows = shp[0]
                for r0 in range(0, rows, 128):
                    with ExitStack() as e2:
                        tl = mk_sb(e2, [128, shp[1]], s_.ap().dtype, "dbgt")
                        P.dma(tl[:], s_.ap()[r0:r0 + 128, :], reads=[DB(nm)], writes=[tl.b])
                        P.dma(o.ap()[r0:r0 + 128, :], tl[:], reads=[tl.b], eng="pool", is_out=True)
                        P.barrier()
        P.finish()
    return nc


def _consts(NX):
    c = np.zeros((128, 10, 128), np.float32)
    i = np.arange(128)
    c[:, 0] = np.eye(128)
    c[:, 1] = 1.0
    c[:, 2] = (i[:, None] <= i[None, :])
    c[:, 3] = (i[:, None] >= i[None, :])
    c[:, 4] = (i[:, None] > i[None, :])
    c[:, 5] = (i[:, None] < i[None, :])
    pm = np.zeros((128, 128), np.float32)
    for d in range(128):
        dd = d % 64
        half = (dd % 32) // 16
        partner = d + 16 if half == 0 else d - 16
        pm[partner, d] = 1.0
    c[:, 6] = pm
    c[:, 7] = (i[:, None] < 64) * 1.0
    c[:, 8] = (i[:, None] >= 64) * 1.0
    c[:, 9] = i[None, :].astype(np.float32)
    t = np.arange(NX)
    row = (t // 64).astype(np.float32)
    colp = (t % 64).astype(np.float32)
    inv = (10000.0 ** (-np.arange(16, dtype=np.float32) / 16)).astype(np.float32)
    rc = np.zeros((128, NX), np.float32)
    rs = np.zeros((128, NX), np.float32)
    for d in range(128):
        dd = d % 64
        axis = dd // 32
        half = (dd % 32) // 16
        f = dd % 16
        pos = row if axis == 0 else colp
        ang = (pos * inv[f]).astype(np.float32)
        rc[d] = np.cos(ang)
        rs[d] = np.sin(ang) * (-1.0 if half == 0 else 1.0)
    return c, rc, rs


def prep_shared(inp, NX):
    f = lambda a: np.ascontiguousarray(np.asarray(a, dtype=np.float32))
    out = {}
    c, rc, rs = _consts(NX)
    out["consts"], out["ropec"], out["ropes"] = c, rc, rs
    out["w_mod"] = f(inp["w_mod"])
    out["bmod"] = f(np.asarray(inp["b_mod"]).reshape(2, 144, 128).transpose(0, 2, 1))
    out["lng"] = f(np.asarray(inp["ln_g"]).reshape(2, 3, 16, 128).transpose(0, 3, 1, 2))
    out["lnb"] = f(np.asarray(inp["ln_b"]).reshape(2, 3, 16, 128).transpose(0, 3, 1, 2))
    for k in ("ffn_w1", "ffn_w3", "ffn_w2", "w_in", "glu_w", "w_branch", "w_out"):
        out[k] = f(inp["s5_glu_w"] if k == "glu_w" else inp[k])
    out["attlam"] = f(np.broadcast_to(np.asarray(inp["att_lam"]).reshape(2, 1, 256), (2, 128, 256)))
    out["subln"] = f(np.asarray(inp["att_subln"]).reshape(2, 128, 1))
    out["convw"] = f(np.asarray(inp["ssd_conv_w"]).reshape(2, 5, 16, 128).transpose(0, 3, 2, 1))
    out["convb"] = f(np.asarray(inp["ssd_conv_b"]).reshape(2, 16, 128).transpose(0, 2, 1))
    out["alog"] = f(np.broadcast_to(np.asarray(inp["ssd_a_log"]).reshape(2, 1, 32), (2, 128, 32)))
    out["dtb"] = f(np.broadcast_to(np.asarray(inp["ssd_dt_bias"]).reshape(2, 1, 32), (2, 128, 32)))
    sd = np.asarray(inp["ssd_d"]).reshape(2, 8, 2)
    out["ssdD"] = f(np.repeat(sd, 64, axis=2).transpose(0, 2, 1))
    out["ssdnw"] = f(np.asarray(inp["ssd_norm"]).reshape(2, 8, 128).transpose(0, 2, 1))
    out["s5lre"] = f(np.asarray(inp["s5_lam_re"]).reshape(2, 2, 32, 128).transpose(0, 1, 3, 2))
    out["s5lim"] = f(np.asarray(inp["s5_lam_im"]).reshape(2, 2, 32, 128).transpose(0, 1, 3, 2))
    ls = np.repeat(np.asarray(inp["s5_log_step"]).reshape(2, 2, 64, 1), 64, axis=3)
    out["s5ls"] = f(ls.reshape(2, 2, 32, 128).transpose(0, 1, 3, 2))
    bre = np.asarray(inp["s5_b_re"]); bim = np.asarray(inp["s5_b_im"])
    cre = np.asarray(inp["s5_c_re"]); cim = np.asarray(inp["s5_c_im"])
    Bre = np.zeros((2, 2, 32, 128, 128), np.float32); Bim = np.zeros_like(Bre)
    Cre = np.zeros_like(Bre); Cim = np.zeros_like(Bre)
    for st in range(32):
        for g2 in range(2):
            g = 2 * st + g2
            gl = g % 8
            Bre[:, :, st, gl * 16:(gl + 1) * 16, g2 * 64:(g2 + 1) * 64] = bre[:, :, g].transpose(0, 1, 3, 2)
            Bim[:, :, st, gl * 16:(gl + 1) * 16, g2 * 64:(g2 + 1) * 64] = bim[:, :, g].transpose(0, 1, 3, 2)
            Cre[:, :, st, g2 * 64:(g2 + 1) * 64, gl * 16:(gl + 1) * 16] = cre[:, :, g].transpose(0, 1, 3, 2)
            Cim[:, :, st, g2 * 64:(g2 + 1) * 64, gl * 16:(gl + 1) * 16] = cim[:, :, g].transpose(0, 1, 3, 2)
    out["s5bre"], out["s5bim"], out["s5cre"], out["s5cim"] = Bre, Bim, Cre, Cim
    out["s5d"] = f(np.asarray(inp["s5_d"]).reshape(2, 8, 128).transpose(0, 2, 1))
    out["glub"] = f(np.asarray(inp["s5_glu_b"]).reshape(2, 8, 128).transpose(0, 2, 1))
    return out


def prep_core(inp, b):
    x = np.asarray(inp["x"][b], dtype=np.float32)
    ctx = np.asarray(inp["ctx"][b], dtype=np.float32)
    xT = np.ascontiguousarray(np.concatenate([ctx, x], axis=0).T)
    cv = np.stack([np.asarray(inp["c"][b]).reshape(16, 128).T, np.asarray(inp["c_ctx"]).reshape(16, 128).T], axis=2)
    return {"xT": xT, "cvec": np.ascontiguousarray(cv.astype(np.float32))}


def kernel(**inputs):
    NX = int(np.asarray(inputs["x"]).shape[1])
    B = int(np.asarray(inputs["x"]).shape[0])
    shared = prep_shared(inputs, NX)
    nc = build(NX)
    active = {0: 0, 1: 1, 4: 2, 5: 3}
    keep = ("consts", "ropec", "ropes", "s5lre", "s5lim", "s5ls", "alog", "dtb")
    idle = {k: (v if k in keep else np.zeros_like(v)) for k, v in shared.items()}
    core0 = prep_core(inputs, 0)
    idle.update({k: np.zeros_like(v) for k, v in core0.items()})
    in_maps = []
    for core in range(8):
        if core in active and active[core] < B:
            m = dict(shared)
            m.update(prep_core(inputs, active[core]))
        else:
            m = idle
        in_maps.append(m)
    res = run_bass_kernel_spmd(nc, in_maps, core_ids=list(range(8)))
    inv = {b: c for c, b in active.items()}
    out = np.stack([np.ascontiguousarray(res.results[inv[b]]["yT"].T) for b in range(B)], axis=0)
    return out.astype(np.float32)
```
